# Optimizing a Trainium2 kernel written in Bass

```python
import jax
import jax.numpy as jnp
from jax import lax

D_MODEL = 2048
BATCH = 4
SEQ = 4096
DEPTH = 4

CTX_LEN = 256
GRID_W = 64
HEAD_DIM = 128
N_HEADS = 8
N_KV_HEADS = 2
Q_PER_KV = N_HEADS // N_KV_HEADS
ATTN_WIDTH = N_HEADS * HEAD_DIM
KV_WIDTH = N_KV_HEADS * HEAD_DIM
CONV_WIDTH = D_MODEL - ATTN_WIDTH
CONV_K = 3
IN_WIDTH = ATTN_WIDTH + 2 * KV_WIDTH + 3 * CONV_WIDTH
IN_SPLITS = (
    ATTN_WIDTH,
    ATTN_WIDTH + KV_WIDTH,
    ATTN_WIDTH + 2 * KV_WIDTH,
    ATTN_WIDTH + 2 * KV_WIDTH + CONV_WIDTH,
    ATTN_WIDTH + 2 * KV_WIDTH + 2 * CONV_WIDTH,
)
MLP_HIDDEN = 4 * D_MODEL
N_MOD = 6
ROPE_THETA = 10000.0
AXIS_DIM = HEAD_DIM // 2
BLOCK_Q = 128
EPS = 1e-6

kernel_name = 'hybrid_dit_parallel_shortconv_gqa'


def rms_norm(x, g):
    x32 = x.astype(jnp.float32)
    y = x32 * lax.rsqrt(jnp.mean(x32 * x32, axis=-1, keepdims=True) + EPS)
    return (y * g.astype(jnp.float32)).astype(x.dtype)


def modulate(h, shift, scale):
    return h * (1 + scale) + shift


def adaln(cond, w, b):
    return [m[..., None, :] for m in jnp.split(jax.nn.silu(cond) @ w + b, N_MOD, axis=-1)]


def axial_rope_tables(rows):
    row = jnp.broadcast_to(jnp.arange(rows, dtype=jnp.float32)[:, None], (rows, GRID_W)).reshape(-1)
    col = jnp.broadcast_to(jnp.arange(GRID_W, dtype=jnp.float32)[None, :], (rows, GRID_W)).reshape(-1)
    inv_freq = ROPE_THETA ** (-jnp.arange(0, AXIS_DIM, 2, dtype=jnp.float32) / AXIS_DIM)
    ang_r = row[:, None] * inv_freq[None, :]
    ang_c = col[:, None] * inv_freq[None, :]
    ang_r = jnp.concatenate([ang_r, ang_r], axis=-1)
    ang_c = jnp.concatenate([ang_c, ang_c], axis=-1)
    return jnp.cos(ang_r), jnp.sin(ang_r), jnp.cos(ang_c), jnp.sin(ang_c)


def _rotate_axis(x, cos, sin):
    x1, x2 = jnp.split(x, 2, axis=-1)
    return x * cos + jnp.concatenate([-x2, x1], axis=-1) * sin


def apply_axial_rope(x, tables):
    cos_r, sin_r, cos_c, sin_c = [t[None, :, None, :] for t in tables]
    xr, xc = jnp.split(x.astype(jnp.float32), 2, axis=-1)
    out = jnp.concatenate([_rotate_axis(xr, cos_r, sin_r), _rotate_axis(xc, cos_c, sin_c)], axis=-1)
    return out.astype(x.dtype)


def gqa_attend(q, k, v):
    s = jnp.einsum('bqhrd,bthd->bhrqt', q, k, preferred_element_type=jnp.float32) * (HEAD_DIM ** -0.5)
    p = jax.nn.softmax(s, axis=-1).astype(v.dtype)
    return jnp.einsum('bhrqt,bthd->bqhrd', p, v)


def latent_attention(q, k, v):
    b, s = q.shape[0], q.shape[1]
    n_blocks = s // BLOCK_Q
    qb = q.reshape(b, n_blocks, BLOCK_Q, N_KV_HEADS, Q_PER_KV, HEAD_DIM).swapaxes(0, 1)
    out = lax.map(lambda blk: gqa_attend(blk, k, v), qb)
    return out.swapaxes(0, 1).reshape(b, s, ATTN_WIDTH)


def context_attention(q, k, v):
    b, t = q.shape[0], q.shape[1]
    qg = q.reshape(b, t, N_KV_HEADS, Q_PER_KV, HEAD_DIM)
    return gqa_attend(qg, k, v).reshape(b, t, ATTN_WIDTH)


def mixer_inputs(h, w_in, q_g, k_g):
    b, t = h.shape[0], h.shape[1]
    q, k, v, gate_b, gate_c, u = jnp.split(h @ w_in, list(IN_SPLITS), axis=-1)
    q = rms_norm(q.reshape(b, t, N_HEADS, HEAD_DIM), q_g)
    k = rms_norm(k.reshape(b, t, N_KV_HEADS, HEAD_DIM), k_g)
    v = v.reshape(b, t, N_KV_HEADS, HEAD_DIM)
    return q, k, v, gate_b, gate_c, u


def short_conv_mixer(gate_b, gate_c, u, w, bias):
    z = jnp.pad(gate_c * u, ((0, 0), (1, 1), (0, 0)))
    conv = z[:, :-2] * w[0] + z[:, 1:-1] * w[1] + z[:, 2:] * w[2] + bias
    return gate_b * conv


def merge_heads(attn, conv, attn_g, conv_g, w_out):
    return jnp.concatenate([rms_norm(attn, attn_g), rms_norm(conv, conv_g)], axis=-1) @ w_out


def sq_relu_mlp(h, w1, w2):
    return jnp.square(jax.nn.relu(h @ w1)) @ w2


def setup_inputs(seed: int = 0) -> dict:
    key = jax.random.key(seed)
    ks = jax.random.split(key, 18)
    f32 = jnp.float32

    def nrm(k, shape, scale):
        return jax.random.normal(k, shape, f32) * scale

    def gain(k, shape):
        return 1.0 + nrm(k, shape, 0.05)

    return {
        'x': nrm(ks[0], (BATCH, SEQ, D_MODEL), 1.0),
        'c': nrm(ks[1], (BATCH, D_MODEL), 1.0),
        'ctx': nrm(ks[2], (BATCH, CTX_LEN, D_MODEL), 1.0),
        'c_ctx': nrm(ks[3], (D_MODEL,), 1.0),
        'w_ada': nrm(ks[4], (DEPTH, D_MODEL, N_MOD * D_MODEL), 0.5 * D_MODEL ** -0.5),
        'b_ada': nrm(ks[5], (DEPTH, N_MOD * D_MODEL), 0.02),
        'norm1_g': gain(ks[6], (DEPTH, D_MODEL)),
        'w_in': nrm(ks[7], (DEPTH, D_MODEL, IN_WIDTH), D_MODEL ** -0.5),
        'q_norm_g': gain(ks[8], (DEPTH, HEAD_DIM)),
        'k_norm_g': gain(ks[9], (DEPTH, HEAD_DIM)),
        'conv_w': nrm(ks[10], (DEPTH, CONV_K, CONV_WIDTH), CONV_K ** -0.5),
        'conv_b': nrm(ks[11], (DEPTH, CONV_WIDTH), 0.02),
        'attn_out_g': gain(ks[12], (DEPTH, ATTN_WIDTH)),
        'conv_out_g': gain(ks[13], (DEPTH, CONV_WIDTH)),
        'w_out': nrm(ks[14], (DEPTH, D_MODEL, D_MODEL), D_MODEL ** -0.5),
        'norm2_g': gain(ks[15], (DEPTH, D_MODEL)),
        'w_mlp_in': nrm(ks[16], (DEPTH, D_MODEL, MLP_HIDDEN), D_MODEL ** -0.5),
        'w_mlp_out': nrm(ks[17], (DEPTH, MLP_HIDDEN, D_MODEL), MLP_HIDDEN ** -0.5),
    }


def reference(x, c, ctx, c_ctx, w_ada, b_ada, norm1_g, w_in, q_norm_g, k_norm_g,
              conv_w, conv_b, attn_out_g, conv_out_g, w_out, norm2_g, w_mlp_in, w_mlp_out):
    rows = x.shape[1] // GRID_W
    tables = axial_rope_tables(rows)
    for l in range(DEPTH):
        sh1, sc1, g1, sh2, sc2, g2 = adaln(c, w_ada[l], b_ada[l])
        csh1, csc1, cg1, csh2, csc2, cg2 = adaln(c_ctx, w_ada[l], b_ada[l])

        hx = modulate(rms_norm(x, norm1_g[l]), sh1, sc1)
        hc = modulate(rms_norm(ctx, norm1_g[l]), csh1, csc1)
        qx, kx, vx, bx, cx, ux = mixer_inputs(hx, w_in[l], q_norm_g[l], k_norm_g[l])
        qc, kc, vc, bc, cc, uc = mixer_inputs(hc, w_in[l], q_norm_g[l], k_norm_g[l])

        qx = apply_axial_rope(qx, tables)
        kx = apply_axial_rope(kx, tables)
        k_all = jnp.concatenate([kx, kc], axis=1)
        v_all = jnp.concatenate([vx, vc], axis=1)
        attn_x = latent_attention(qx, k_all, v_all)
        conv_x = short_conv_mixer(bx, cx, ux, conv_w[l], conv_b[l])
        x_new = x + g1 * merge_heads(attn_x, conv_x, attn_out_g[l], conv_out_g[l], w_out[l])
        x_new = x_new + g2 * sq_relu_mlp(modulate(rms_norm(x_new, norm2_g[l]), sh2, sc2),
                                         w_mlp_in[l], w_mlp_out[l])

        if l < DEPTH - 1:
            attn_c = context_attention(qc, kc, vc)
            conv_c = short_conv_mixer(bc, cc, uc, conv_w[l], conv_b[l])
            ctx = ctx + cg1 * merge_heads(attn_c, conv_c, attn_out_g[l], conv_out_g[l], w_out[l])
            ctx = ctx + cg2 * sq_relu_mlp(modulate(rms_norm(ctx, norm2_g[l]), csh2, csc2),
                                          w_mlp_in[l], w_mlp_out[l])
        x = x_new
    return x
```

```python
import numpy as np
from contextlib import ExitStack
import concourse.bass as bass
import concourse.mybir as mybir
from concourse.bass_utils import run_bass_kernel_spmd

F32 = mybir.dt.float32
BF16 = mybir.dt.bfloat16
ALU = mybir.AluOpType
AF = mybir.ActivationFunctionType

D = 2048
KC = 16
NLAT = 2048
NCTX = 256
NTOK = NLAT + NCTX
DEPTH = 4
EPS = 1e-6
SCALE = 128.0 ** -0.5
NVEC = 178
V_BADA, V_N1G, V_N2G, V_QG, V_KG, V_CW, V_CB, V_AG, V_CG = 0, 96, 112, 128, 129, 130, 154, 162, 170
GROUPS = [(0, 512, False), (512, 512, False), (1024, 512, False), (1536, 512, False), (2048, 256, True)]
PAIRS = [[0, 1], [2, 3], [4, 5], [6, 7]]
NW2 = 4


class Tile:
    __slots__ = ("name", "w", "r")

    def __init__(self, name):
        self.name = name
        self.w = None
        self.r = {}


class Sem:
    __slots__ = ("h", "count")

    def __init__(self, h):
        self.h = h
        self.count = 0


class Queue:
    def __init__(self, name, kind):
        self.name = name
        self.kind = kind
        self.ops = []
        self.seen = {}
        self.sem = None
        self.dsems = []
        self.di = 0


class Tracker:
    def wait(self, q, tok):
        s, v = tok
        if q.seen.get(s, 0) >= v:
            return
        q.seen[s] = v
        q.ops.append(lambda e, s=s, v=v: e.wait_ge(s.h, v))

    def deps(self, q, reads, writes):
        toks = []
        for t in reads:
            if t.w is not None:
                toks.append(t.w)
        for t in writes:
            if t.w is not None:
                toks.append(t.w)
            toks.extend(t.r.items())
        for tok in toks:
            if q.kind == "pe" and tok[0] is q.sem:
                continue
            self.wait(q, tok)

    def finish(self, tok, reads, writes):
        s, v = tok
        for t in reads:
            if t.r.get(s, 0) < v:
                t.r[s] = v
        for t in writes:
            t.w = tok
            t.r = {}

    def op(self, q, f, reads=(), writes=(), mark=True):
        self.deps(q, reads, writes)
        s = q.sem
        if mark:
            s.count += 1
            v = s.count
            q.ops.append(lambda e, f=f, s=s: f(e).then_inc(s.h, 1))
        else:
            v = s.count + 1
            q.ops.append(lambda e, f=f: f(e))
        self.finish((s, v), reads, writes)

    def dma(self, q, f, reads=(), writes=()):
        self.deps(q, reads, writes)
        s = q.dsems[q.di]
        q.di = (q.di + 1) % len(q.dsems)
        if s.count > 0:
            self.wait(q, (s, s.count))
        s.count += 16
        v = s.count
        q.ops.append(lambda e, f=f, s=s: f(e).then_inc(s.h, 16))
        self.finish((s, v), reads, writes)


def build(depth=DEPTH, stage=99, fake_cc=False):
    nc = bass.Bass("TRN2", target_bir_lowering=False)
    K = Tracker()

    def din(name, shape, dt=F32):
        return nc.dram_tensor(name, shape, dt, kind="ExternalInput")

    xT = din("xT", [D, NLAT])
    cxT = din("cxT", [D, NCTX])
    cc_d = din("cc", [128, 32])
    ropec_d = din("ropec", [128, NLAT])
    ropes_d = din("ropes", [128, NLAT])
    hmask_d = din("hmask", [128, 2])
    vecs_d = din("vecs", [128, depth * NVEC])
    w_ada = din("w_ada", [depth, D, 6 * D])
    w_in = din("w_in", [depth, D, 4608])
    w_out = din("w_out", [depth, D, D])
    w1 = din("w1", [depth, D, 4 * D])
    w2 = din("w2", [depth, 4 * D, D])
    outT = nc.dram_tensor("outT", [D, NLAT], F32, kind="ExternalOutput")

    xres = nc.dram_tensor("xres", [D, NTOK], F32)
    q_d = nc.dram_tensor("q_d", [1024, NTOK], BF16)
    yg_d = nc.dram_tensor("yg_d", [1024, NTOK], BF16)
    kctx_d = nc.dram_tensor("kctx_d", [256, NCTX], BF16)
    vctx_d = nc.dram_tensor("vctx_d", [NCTX, 256], BF16)
    kT_in = [nc.dram_tensor(f"kT_in{l}", [256, NLAT], BF16) for l in range(depth)]
    v_in = [nc.dram_tensor(f"v_in{l}", [NLAT, 256], BF16) for l in range(depth)]
    halo_in = [nc.dram_tensor(f"halo_in{l}", [128, 16], F32) for l in range(depth)]
    kT_all = [nc.dram_tensor(f"kT_all{l}", [512, NLAT], BF16) for l in range(depth)]
    v_all = [nc.dram_tensor(f"v_all{l}", [2 * NLAT, 256], BF16) for l in range(depth)]
    halo_all = [nc.dram_tensor(f"halo_all{l}", [256, 16], F32) for l in range(depth)]

    xres_v = xres.ap().rearrange("(k p) t -> p k t", p=128)
    outT_v = outT.ap().rearrange("(k p) t -> p k t", p=128)
    xT_v = xT.ap().rearrange("(k p) t -> p k t", p=128)
    cxT_v = cxT.ap().rearrange("(k p) t -> p k t", p=128)
    q_d_v = q_d.ap().rearrange("(h p) t -> p h t", p=128)
    yg_d_v = yg_d.ap().rearrange("(h p) t -> p h t", p=128)
    vctx_v = vctx_d.ap().rearrange("(t p) c -> p t c", p=128)
    w_in_v = [w_in[l].rearrange("(k p) c -> p k c", p=128) for l in range(depth)]
    w_out_v = [w_out[l].rearrange("(k p) c -> p k c", p=128) for l in range(depth)]
    w1_v = [w1[l].rearrange("(k p) c -> p k c", p=128) for l in range(depth)]
    w2_v = [w2[l].rearrange("(k p) c -> p k c", p=128) for l in range(depth)]
    w_ada_v = [w_ada[l].rearrange("(k p) c -> p k c", p=128) for l in range(depth)]
    v_all_v = [v_all[l].ap().rearrange("(t p) c -> p t c", p=128) for l in range(depth)]

    with ExitStack() as es:
        def sb(name, shape, dt):
            return es.enter_context(nc.sbuf_tensor(name, shape, dt))

        def newsem(name):
            return Sem(es.enter_context(nc.semaphore(name)))

        qP = Queue("pe", "pe")
        qA = Queue("act", "act")
        qV = Queue("dve", "dve")
        qG = Queue("pool", "pool")
        qS = Queue("sp", "sp")
        for q in (qP, qA, qV, qG, qS):
            q.sem = newsem("s_" + q.name)
        qS.dsems = [newsem(f"ds{i}") for i in range(16)]
        qG.dsems = [newsem(f"dg{i}") for i in range(8)]

        class Buf:
            def __init__(self, ap, t):
                self.ap = ap
                self.t = t

        def mk(name, shape, dt):
            t = sb(name, shape, dt)
            return Buf(t[:], Tile(name))

        ones_d = mk("ones_d", [128, 128], BF16)
        ones_h = mk("ones_h", [128, 128], BF16)
        ones_g = mk("ones_g", [128, 128], BF16)
        ones_1 = mk("ones_1", [128, 128], BF16)
        eps_t = mk("eps_t", [128, 1], F32)
        vecs = mk("vecs_sb", [128, depth * NVEC], F32)
        ccs = mk("ccs", [128, 32], F32)
        mod = mk("mod", [128, 192], F32)
        G1 = mk("G1", [128, 32], F32)
        G2 = mk("G2", [128, 32], F32)
        rga = mk("rga", [128, 8], F32)
        rgc = mk("rgc", [128, 8], F32)
        hm = mk("hm", [128, 2], F32)
        zf = mk("zf", [128, 32], F32)
        zl = mk("zl", [128, 32], F32)
        bfb = mk("bfb", [128, 32], F32)
        blb = mk("blb", [128, 32], F32)
        yfb = mk("yfb", [128, 32], F32)
        ylb = mk("ylb", [128, 32], F32)
        yfix = mk("yfix", [128, 64], BF16)
        hl = mk("hl", [128, 8], F32)
        hr = mk("hr", [128, 8], F32)
        t8 = [mk(f"t8_{i}", [128, 8], F32) for i in range(2)]
        wbuf = [mk(f"wbuf{i}", [128, 16 * 128], BF16) for i in range(3)]
        w2buf = [mk(f"w2buf{i}", [128, 32 * 128], BF16) for i in range(NW2)]
        pT = [mk(f"pT{i}", [128, 512], BF16) for i in range(3)]
        sqb = [mk(f"sqb{i}", [128, 512], BF16) for i in range(2)]
        st16 = [mk(f"st16_{i}", [128, 512], BF16) for i in range(3)]
        vst = [mk(f"vst{i}", [128, 256], BF16) for i in range(2)]
        rstd = [mk(f"rstd{i}", [128, 512], F32) for i in range(3)]
        tmpm = [mk(f"tmpm{i}", [128, 512], F32) for i in range(2)]
        qn = [mk(f"qn{i}", [128, 512], F32) for i in range(2)]
        t1 = [mk(f"t1_{i}", [128, 512], F32) for i in range(2)]
        t2 = [mk(f"t2_{i}", [128, 512], F32) for i in range(2)]
        cs = [mk(f"cs{i}", [128, 512], F32) for i in range(2)]
        bs = [mk(f"bs{i}", [128, 512], F32) for i in range(2)]
        zb = [mk(f"zb{i}", [128, 516], F32) for i in range(2)]
        cv = [mk(f"cv{i}", [128, 512], F32) for i in range(1)]
        yv = [mk(f"yv{i}", [128, 512], F32) for i in range(1)]
        rec, ta, tb, rl, xc, ost = qn, t1, t2, cs, bs, zb

        arenaX = sb("arenaX", [128, 32 * 1024], BF16)
        arenaY = sb("arenaY", [128, 12800], BF16)
        xg3 = arenaX[:, 0:16384].bitcast(F32).rearrange("p (k n) -> p k n", k=16)
        xg_t = [Tile(f"xg{k}") for k in range(16)]
        X32_t = Tile("X32")
        X40_t = Tile("X40")
        X48_t = Tile("X48")
        cos_ap = arenaX[:, 16384:20480].bitcast(F32)
        sin_ap = arenaX[:, 20480:24576].bitcast(F32)
        og3 = arenaX[:, 16384:20480].rearrange("p (h n) -> p h n", h=8)
        yg3 = arenaX[:, 20480:24576].rearrange("p (h n) -> p h n", h=8)
        hT3 = arenaX[:, 24576:32768].rearrange("p (k n) -> p k n", k=16)
        aT3 = arenaX[:, 0:32768].rearrange("p (j n) -> p j n", j=64)

        def aT_tile(j):
            if j < 32:
                return xg_t[j // 2]
            if j < 40:
                return X32_t
            if j < 48:
                return X40_t
            return X48_t

        yk_t = Tile("yk")
        kh = arenaY[:, 0:4352]
        vh3 = arenaY[:, 4352:8704].rearrange("p (t d) -> p t d", t=34)
        h23 = arenaY[:, 0:8192].rearrange("p (k n) -> p k n", k=16)
        qg3 = arenaY[:, 8704:12800].rearrange("p (h n) -> p h n", h=8)
        qg_t = Tile("qg")

        ps = []
        for i in range(8):
            t = es.enter_context(nc.psum_tensor(f"ps{i}", [128, 512], F32))
            ps.append(Buf(t[:], Tile(f"ps{i}")))

        xres_t = [[Tile(f"xres{g}_{j}") for j in range(16)] for g in range(5)]
        q_d_t = [[Tile(f"qd{g}_{h}") for h in range(8)] for g in range(5)]
        yg_d_t = [[Tile(f"ygd{g}_{i}") for i in range(8)] for g in range(5)]
        kctx_t = [Tile(f"kctx{h}") for h in range(2)]
        vctx_t = [Tile(f"vctx{t}") for t in range(2)]
        kT_in_t = [[[Tile(f"kTin{l}_{g}_{h}") for h in range(2)] for g in range(4)] for l in range(depth)]
        v_in_t = [[Tile(f"vin{l}_{t}") for t in range(16)] for l in range(depth)]
        halo_in_t = [Tile(f"haloin{l}") for l in range(depth)]
        kT_all_t = [Tile(f"kTall{l}") for l in range(depth)]
        v_all_t = [Tile(f"vall{l}") for l in range(depth)]
        halo_all_t = [Tile(f"haloall{l}") for l in range(depth)]
        out_t = [[Tile(f"out{g}_{j}") for j in range(16)] for g in range(4)]

        def ACT(out, in_, func, reads, writes, bias=None, scale=None):
            kw = {}
            if bias is not None:
                kw["bias"] = bias
            if scale is not None:
                kw["scale"] = scale
            K.op(qA, lambda e: e.activation(out=out, in_=in_, func=func, **kw), reads, writes)

        def TT(out, in0, in1, op, reads, writes):
            K.op(qV, lambda e: e.tensor_tensor(out=out, in0=in0, in1=in1, op=op), reads, writes)

        def TS(out, in0, s1, op0, reads, writes):
            K.op(qV, lambda e: e.tensor_scalar(out=out, in0=in0, scalar1=s1, scalar2=None, op0=op0), reads, writes)

        def STT(out, in0, scalar, in1, op0, op1, reads, writes):
            K.op(qV, lambda e: e.scalar_tensor_tensor(out=out, in0=in0, scalar=scalar, in1=in1, op0=op0, op1=op1),
                 reads, writes)

        def VCOPY(out, in_, reads, writes):
            K.op(qV, lambda e: e.tensor_copy(out=out, in_=in_), reads, writes)

        def RECIP(out, in_, reads, writes):
            K.op(qV, lambda e: e.reciprocal(out=out, in_=in_), reads, writes)

        def MEMSET(ap, val, writes):
            K.op(qV, lambda e: e.memset(ap, val), (), writes)

        def MM(out, lhsT, rhs, start, stop, reads, writes, mark=None):
            K.op(qP, lambda e: e.matmul(out, lhsT=lhsT, rhs=rhs, start=start, stop=stop), reads, writes,
                 mark=(stop if mark is None else mark))

        def DMA(q, out, in_, reads, writes):
            K.dma(q, lambda e: e.dma_start(out=out, in_=in_), reads, writes)

        ccount = [0]

        def ALLGATHER(in_ap, out_ap, reads, writes, in_h=None, out_h=None):
            if fake_cc:
                n0 = in_h.shape[0]
                DMA(qG, out_h[0:n0, :], in_h[:, :], reads, writes)
                return
            K.deps(qG, reads, writes)
            s = newsem(f"cc{ccount[0]}")
            ccount[0] += 1
            s.count = 1
            qG.ops.append(lambda e: e.collective_compute(
                "AllGather", ALU.bypass, replica_groups=PAIRS, ins=[in_ap], outs=[out_ap]).then_inc(s.h, 1))
            K.finish((s, 1), reads, writes)
            K.wait(qG, (s, 1))

        def vc(l, off, n):
            return vecs.ap[:, l * NVEC + off:l * NVEC + off + n]

        mod3 = mod.ap.rearrange("p (j s) -> p j s", s=2)
        G1v = G1.ap.rearrange("p (k s) -> p k s", s=2)
        G2v = G2.ap.rearrange("p (k s) -> p k s", s=2)

        rs_ctr = [0]

        def RSTD(psb, N):
            r = rstd[rs_ctr[0] % 3]
            rs_ctr[0] += 1
            ACT(r.ap[:, :N], psb.ap[:, :N], AF.Sqrt, [psb.t, eps_t.t], [r.t], bias=eps_t.ap[:, 0:1], scale=1.0)
            RECIP(r.ap[:, :N], r.ap[:, :N], [r.t], [r.t])
            return r

        wctr = [0]

        def load_w(src3):
            slot = wbuf[wctr[0] % 3]
            wctr[0] += 1
            DMA(qG, slot.ap.rearrange("p (k c) -> p k c", k=16), src3, [], [slot.t])
            return slot

        w2ctr = [0]

        def w2slot():
            s = w2buf[w2ctr[0] % NW2]
            w2ctr[0] += 1
            return s

        MEMSET(ones_d.ap, 1.0 / 2048.0, [ones_d.t])
        MEMSET(ones_h.ap, 1.0 / 128.0, [ones_h.t])
        MEMSET(ones_g.ap, 1.0 / 1024.0, [ones_g.t])
        MEMSET(ones_1.ap, 1.0, [ones_1.t])
        MEMSET(eps_t.ap, EPS, [eps_t.t])
        DMA(qS, vecs.ap, vecs_d.ap(), [], [vecs.t])
        DMA(qS, ccs.ap, cc_d.ap(), [], [ccs.t])
        DMA(qS, hm.ap, hmask_d.ap(), [], [hm.t])
        ACT(ccs.ap, ccs.ap, AF.Silu, [ccs.t], [ccs.t])
        for g, (s0, N, ctx) in enumerate(GROUPS):
            for j in range(16):
                src = cxT_v[:, j, :] if ctx else xT_v[:, j, s0:s0 + N]
                DMA(qS, xres_v[:, j, s0:s0 + N], src, [], [xres_t[g][j]])

        def adaln(l):
            pm = ps[7]
            for j in range(96):
                slot = w2slot()
                wa = slot.ap[:, 0:4096].bitcast(F32).rearrange("p (k c) -> p k c", k=16)
                DMA(qS, wa, w_ada_v[l][:, :, j * 128:(j + 1) * 128], [], [slot.t])
                for k in range(16):
                    MM(pm.ap[:, 2 * j:2 * j + 2], wa[:, k, :], ccs.ap[:, 2 * k:2 * k + 2], k == 0, k == 15,
                       [slot.t, ccs.t], [pm.t])
            pm3 = pm.ap[:, 0:192].rearrange("p (j s) -> p j s", s=2)
            for s in range(2):
                TT(mod3[:, :, s], pm3[:, :, s], vc(l, V_BADA, 96), ALU.add, [pm.t, vecs.t], [mod.t])
            for s in range(2):
                STT(G1v[:, :, s], mod3[:, 16:32, s], 1.0, vc(l, V_N1G, 16), ALU.add, ALU.mult, [mod.t, vecs.t], [G1.t])
                STT(G2v[:, :, s], mod3[:, 64:80, s], 1.0, vc(l, V_N2G, 16), ALU.add, ALU.mult, [mod.t, vecs.t], [G2.t])
            RECIP(rga.ap, vc(l, V_AG, 8), [vecs.t], [rga.t])
            RECIP(rgc.ap, vc(l, V_CG, 8), [vecs.t], [rgc.t])

        chunks = [("q", h, h * 128) for h in range(8)] + [("k", h, 1024 + h * 128) for h in range(2)]
        for i in range(8):
            chunks += [("b", i, 1536 + i * 128), ("c", i, 2560 + i * 128), ("u", i, 3584 + i * 128)]

        mctr = [0]
        nctr = [0]
        sctr = [0]
        qctr = [0]

        def phaseA(l, g, wv3, wv_t):
            s0, N, ctx = GROUPS[g]
            s = 1 if ctx else 0
            for kq in range(4):
                DMA(qS, xg3[:, 4 * kq:4 * kq + 4, 0:N], xres_v[:, 4 * kq:4 * kq + 4, s0:s0 + N],
                    [xres_t[g][j] for j in range(4 * kq, 4 * kq + 4)], [xg_t[j] for j in range(4 * kq, 4 * kq + 4)])
            for k in range(16):
                sq = sqb[k % 2]
                ACT(sq.ap[:, :N], xg3[:, k, :N], AF.Square, [xg_t[k]], [sq.t])
                MM(ps[0].ap[:, :N], ones_d.ap, sq.ap[:, :N], k == 0, k == 15, [sq.t, ones_d.t], [ps[0].t], mark=True)
            rs = RSTD(ps[0], N)
            for k in range(16):
                tm = tmpm[k % 2]
                STT(tm.ap[:, :N], xg3[:, k, :N], G1v[:, k, s:s + 1], rs.ap[:, :N], ALU.mult, ALU.mult,
                    [xg_t[k], G1.t, rs.t], [tm.t])
                ACT(hT3[:, k, :N], tm.ap[:, :N], AF.Identity, [tm.t, mod.t], [X48_t], bias=mod3[:, k, s:s + 1], scale=1.0)
            for tt in range(N // 128):
                pb = ps[1 + tt % 2]
                for k in range(16):
                    MM(pb.ap[:, 0:256], hT3[:, k, tt * 128:(tt + 1) * 128], wv3[:, k, :], k == 0, k == 15,
                       [X48_t, wv_t], [pb.t])
                vs = vst[tt % 2]
                ACT(vs.ap, pb.ap[:, 0:256], AF.Copy, [pb.t], [vs.t])
                if ctx:
                    DMA(qS, vctx_d[tt * 128:(tt + 1) * 128, :], vs.ap, [vs.t], [vctx_t[tt]])
                else:
                    ti = s0 // 128 + tt
                    DMA(qS, v_in[l][ti * 128:(ti + 1) * 128, :], vs.ap, [vs.t], [v_in_t[l][ti]])

            nch = len(chunks)
            slots = {}
            for ci in range(min(2, nch)):
                slots[ci] = load_w(w_in_v[l][:, :, chunks[ci][2]:chunks[ci][2] + 128])
            deferred = []
            cur = {}
            for ci, (kind, idx, col0) in enumerate(chunks):
                slot = slots.pop(ci)
                if ci + 2 < nch:
                    c2 = chunks[ci + 2][2]
                    slots[ci + 2] = load_w(w_in_v[l][:, :, c2:c2 + 128])
                pm = ps[3 + mctr[0] % 3]
                mctr[0] += 1
                w3 = slot.ap.rearrange("p (k c) -> p k c", k=16)
                for k in range(16):
                    MM(pm.ap[:, :N], w3[:, k, :], hT3[:, k, :N], k == 0, k == 15, [slot.t, X48_t], [pm.t])
                for f in deferred:
                    f()
                deferred = []
                if kind in ("q", "k"):
                    sq = sqb[sctr[0] % 2]
                    sctr[0] += 1
                    ACT(sq.ap[:, :N], pm.ap[:, :N], AF.Square, [pm.t], [sq.t])

                    def stage2(kind=kind, idx=idx, pm=pm, sq=sq):
                        nb = ps[6 + nctr[0] % 2]
                        nctr[0] += 1
                        MM(nb.ap[:, :N], ones_h.ap, sq.ap[:, :N], True, True, [sq.t, ones_h.t], [nb.t])
                        r = RSTD(nb, N)
                        qq = qn[qctr[0] % 2]
                        tt1 = t1[qctr[0] % 2]
                        tt2 = t2[qctr[0] % 2]
                        so = st16[qctr[0] % 3]
                        qctr[0] += 1
                        gq = vc(l, V_QG if kind == "q" else V_KG, 1)
                        STT(qq.ap[:, :N], pm.ap[:, :N], gq, r.ap[:, :N], ALU.mult, ALU.mult, [pm.t, vecs.t, r.t], [qq.t])
                        if ctx:
                            ACT(so.ap[:, :N], qq.ap[:, :N], AF.Copy, [qq.t], [so.t])
                        else:
                            TT(tt1.ap[:, :N], qq.ap[:, :N], cos_ap[:, s0:s0 + N], ALU.mult, [qq.t, X32_t], [tt1.t])
                            for (a, b) in ((0, 32), (32, 0), (64, 96), (96, 64)):
                                TT(tt2.ap[a:a + 32, :N], qq.ap[b:b + 32, :N], sin_ap[b:b + 32, s0:s0 + N], ALU.mult,
                                   [qq.t, X40_t], [tt2.t])
                            TT(so.ap[:, :N], tt1.ap[:, :N], tt2.ap[:, :N], ALU.add, [tt1.t, tt2.t], [so.t])
                        if kind == "q":
                            DMA(qS, q_d[idx * 128:(idx + 1) * 128, s0:s0 + N], so.ap[:, :N], [so.t], [q_d_t[g][idx]])
                        elif ctx:
                            DMA(qS, kctx_d[idx * 128:(idx + 1) * 128, :], so.ap[:, :N], [so.t], [kctx_t[idx]])
                        else:
                            DMA(qS, kT_in[l][idx * 128:(idx + 1) * 128, s0:s0 + N], so.ap[:, :N], [so.t],
                                [kT_in_t[l][g][idx]])
                    deferred.append(stage2)
                elif kind == "b":
                    b_ = bs[idx % 2]
                    ACT(b_.ap[:, :N], pm.ap[:, :N], AF.Copy, [pm.t], [b_.t])
                    cur["b"] = b_
                elif kind == "c":
                    c_ = cs[idx % 2]
                    ACT(c_.ap[:, :N], pm.ap[:, :N], AF.Copy, [pm.t], [c_.t])
                    cur["c"] = c_
                else:
                    i = idx
                    b_, c_ = cur["b"], cur["c"]
                    z = zb[i % 2]
                    MEMSET(z.ap[:, 0:1], 0.0, [z.t])
                    MEMSET(z.ap[:, N + 1:N + 2], 0.0, [z.t])
                    TT(z.ap[:, 1:N + 1], pm.ap[:, :N], c_.ap[:, :N], ALU.mult, [pm.t, c_.t], [z.t])
                    c0 = cv[0]
                    y0 = yv[0]
                    TS(c0.ap[:, :N], z.ap[:, 0:N], vc(l, V_CW + i, 1), ALU.mult, [z.t, vecs.t], [c0.t])
                    STT(c0.ap[:, :N], z.ap[:, 1:N + 1], vc(l, V_CW + 8 + i, 1), c0.ap[:, :N], ALU.mult, ALU.add,
                        [z.t, vecs.t, c0.t], [c0.t])
                    STT(c0.ap[:, :N], z.ap[:, 2:N + 2], vc(l, V_CW + 16 + i, 1), c0.ap[:, :N], ALU.mult, ALU.add,
                        [z.t, vecs.t, c0.t], [c0.t])
                    STT(y0.ap[:, :N], c0.ap[:, :N], vc(l, V_CB + i, 1), b_.ap[:, :N], ALU.add, ALU.mult,
                        [c0.t, vecs.t, b_.t], [y0.t])
                    if not ctx:
                        c8 = g * 8 + i
                        ACT(zf.ap[:, c8:c8 + 1], z.ap[:, 1:2], AF.Copy, [z.t], [zf.t])
                        ACT(zl.ap[:, c8:c8 + 1], z.ap[:, N:N + 1], AF.Copy, [z.t], [zl.t])
                        ACT(bfb.ap[:, c8:c8 + 1], b_.ap[:, 0:1], AF.Copy, [b_.t], [bfb.t])
                        ACT(blb.ap[:, c8:c8 + 1], b_.ap[:, N - 1:N], AF.Copy, [b_.t], [blb.t])
                        ACT(yfb.ap[:, c8:c8 + 1], y0.ap[:, 0:1], AF.Copy, [y0.t], [yfb.t])
                        ACT(ylb.ap[:, c8:c8 + 1], y0.ap[:, N - 1:N], AF.Copy, [y0.t], [ylb.t])
                    so = st16[qctr[0] % 3]
                    qctr[0] += 1
                    TS(so.ap[:, :N], y0.ap[:, :N], vc(l, V_CG + i, 1), ALU.mult, [y0.t, vecs.t], [so.t])
                    DMA(qS, yg_d[i * 128:(i + 1) * 128, s0:s0 + N], so.ap[:, :N], [so.t], [yg_d_t[g][i]])
            for f in deferred:
                f()

        def exchange(l):
            DMA(qS, halo_in[l][:, 0:8], zf.ap[:, 0:8], [zf.t], [halo_in_t[l]])
            DMA(qS, halo_in[l][:, 8:16], zl.ap[:, 24:32], [zl.t], [halo_in_t[l]])
            ALLGATHER(kT_in[l].ap().opt(), kT_all[l].ap().opt(),
                      [t for gg in kT_in_t[l] for t in gg], [kT_all_t[l]], kT_in[l], kT_all[l])
            ALLGATHER(v_in[l].ap().opt(), v_all[l].ap().opt(), v_in_t[l], [v_all_t[l]], v_in[l], v_all[l])
            ALLGATHER(halo_in[l].ap().opt(), halo_all[l].ap().opt(), [halo_in_t[l]], [halo_all_t[l]], halo_in[l], halo_all[l])
            DMA(qS, hl.ap, halo_all[l][0:128, 8:16], [halo_all_t[l]], [hl.t])
            DMA(qS, hr.ap, halo_all[l][128:256, 0:8], [halo_all_t[l]], [hr.t])
            TS(hl.ap, hl.ap, hm.ap[:, 0:1], ALU.mult, [hl.t, hm.t], [hl.t])
            TS(hr.ap, hr.ap, hm.ap[:, 1:2], ALU.mult, [hr.t, hm.t], [hr.t])
            yfix4 = yfix.ap.rearrange("p (g e i) -> p g e i", g=4, e=2)
            for g in range(4):
                g8 = slice(g * 8, g * 8 + 8)
                left = hl.ap if g == 0 else zl.ap[:, (g - 1) * 8:g * 8]
                lt = hl.t if g == 0 else zl.t
                a = t8[0]
                TT(a.ap, bfb.ap[:, g8], vc(l, V_CW, 8), ALU.mult, [bfb.t, vecs.t], [a.t])
                TT(a.ap, a.ap, left, ALU.mult, [a.t, lt], [a.t])
                TT(a.ap, a.ap, yfb.ap[:, g8], ALU.add, [a.t, yfb.t], [a.t])
                TT(yfix4[:, g, 0, :], a.ap, vc(l, V_CG, 8), ALU.mult, [a.t, vecs.t], [yfix.t])
                right = hr.ap if g == 3 else zf.ap[:, (g + 1) * 8:(g + 2) * 8]
                rt = hr.t if g == 3 else zf.t
                b = t8[1]
                TT(b.ap, blb.ap[:, g8], vc(l, V_CW + 16, 8), ALU.mult, [blb.t, vecs.t], [b.t])
                TT(b.ap, b.ap, right, ALU.mult, [b.t, rt], [b.t])
                TT(b.ap, b.ap, ylb.ap[:, g8], ALU.add, [b.t, ylb.t], [b.t])
                TT(yfix4[:, g, 1, :], b.ap, vc(l, V_CG, 8), ALU.mult, [b.t, vecs.t], [yfix.t])
            return yfix4

        uctr = [0]
        tctr = [0]
        pctr = [0]

        def attention(l, g):
            s0, N, ctx = GROUPS[g]
            DMA(qS, qg3[:, :, :N], q_d_v[:, :, s0:s0 + N], q_d_t[g], [qg_t])
            kts = [32, 33] if ctx else list(range(34))
            n = len(kts)
            for hd in range(2):
                if not ctx:
                    DMA(qS, kh[:, 0:2048], kT_all[l][hd * 128:(hd + 1) * 128, :], [kT_all_t[l]], [yk_t])
                    DMA(qS, kh[:, 2048:4096], kT_all[l][256 + hd * 128:256 + (hd + 1) * 128, :], [kT_all_t[l]], [yk_t])
                    DMA(qS, vh3[:, 0:32, :], v_all_v[l][:, :, hd * 128:(hd + 1) * 128], [v_all_t[l]], [yk_t])
                DMA(qS, kh[:, 4096:4352], kctx_d[hd * 128:(hd + 1) * 128, :], [kctx_t[hd]], [yk_t])
                DMA(qS, vh3[:, 32:34, :], vctx_v[:, :, hd * 128:(hd + 1) * 128], vctx_t, [yk_t])
                for r in range(4):
                    head = hd * 4 + r
                    ob = ps[3 + uctr[0] % 2]
                    db = ps[5 + uctr[0] % 2]
                    uctr[0] += 1
                    pts = {}

                    def S(i):
                        kt = kts[i]
                        tb_ = ps[tctr[0] % 3]
                        tctr[0] += 1
                        MM(tb_.ap[:, :N], kh[:, kt * 128:(kt + 1) * 128], qg3[:, head, :N], True, True,
                           [yk_t, qg_t], [tb_.t])
                        p = pT[pctr[0] % 3]
                        pctr[0] += 1
                        ACT(p.ap[:, :N], tb_.ap[:, :N], AF.Exp, [tb_.t], [p.t], scale=SCALE)
                        pts[i] = p

                    S(0)
                    if n > 1:
                        S(1)
                    for i in range(n):
                        if i + 2 < n:
                            S(i + 2)
                        p = pts.pop(i)
                        kt = kts[i]
                        MM(ob.ap[:, :N], vh3[:, kt, :], p.ap[:, :N], i == 0, i == n - 1, [yk_t, p.t], [ob.t])
                        MM(db.ap[:, :N], ones_1.ap, p.ap[:, :N], i == 0, i == n - 1, [ones_1.t, p.t], [db.t], mark=True)
                    rc = rec[head % 2]
                    RECIP(rc.ap[:, :N], db.ap[:, :N], [db.t], [rc.t])
                    STT(og3[:, head, :N], ob.ap[:, :N], vc(l, V_AG + head, 1), rc.ap[:, :N], ALU.mult, ALU.mult,
                        [ob.t, vecs.t, rc.t], [X32_t])

        def phaseB(l, g, yfix4, last):
            s0, N, ctx = GROUPS[g]
            s = 1 if ctx else 0
            attention(l, g)
            DMA(qS, yg3[:, :, :N], yg_d_v[:, :, s0:s0 + N], yg_d_t[g], [X40_t])
            if not ctx:
                VCOPY(yg3[:, :, 0], yfix4[:, g, 0, :], [yfix.t], [X40_t])
                VCOPY(yg3[:, :, N - 1], yfix4[:, g, 1, :], [yfix.t], [X40_t])
            for kq in range(4):
                DMA(qS, xg3[:, 4 * kq:4 * kq + 4, 0:N], xres_v[:, 4 * kq:4 * kq + 4, s0:s0 + N],
                    [xres_t[g][j] for j in range(4 * kq, 4 * kq + 4)], [xg_t[j] for j in range(4 * kq, 4 * kq + 4)])
            for i in range(8):
                sq = sqb[i % 2]
                ACT(sq.ap[:, :N], og3[:, i, :N], AF.Square, [X32_t, rga.t], [sq.t], scale=rga.ap[:, i:i + 1])
                MM(ps[6].ap[:, :N], ones_g.ap, sq.ap[:, :N], i == 0, i == 7, [sq.t, ones_g.t], [ps[6].t], mark=True)
            for i in range(8):
                sq = sqb[i % 2]
                ACT(sq.ap[:, :N], yg3[:, i, :N], AF.Square, [X40_t, rgc.t], [sq.t], scale=rgc.ap[:, i:i + 1])
                MM(ps[7].ap[:, :N], ones_g.ap, sq.ap[:, :N], i == 0, i == 7, [sq.t, ones_g.t], [ps[7].t], mark=True)
            rs_a = RSTD(ps[6], N)
            rs_c = RSTD(ps[7], N)
            slots = {}
            for j in range(2):
                slots[j] = load_w(w_out_v[l][:, :, j * 128:(j + 1) * 128])
            for j in range(16):
                slot = slots.pop(j)
                if j + 2 < 16:
                    slots[j + 2] = load_w(w_out_v[l][:, :, (j + 2) * 128:(j + 3) * 128])
                w3 = slot.ap.rearrange("p (k c) -> p k c", k=16)
                pa = ps[0 + j % 2]
                pc = ps[2 + j % 2]
                for k in range(8):
                    MM(pa.ap[:, :N], w3[:, k, :], og3[:, k, :N], k == 0, k == 7, [slot.t, X32_t], [pa.t])
                for k in range(8):
                    MM(pc.ap[:, :N], w3[:, 8 + k, :], yg3[:, k, :N], k == 0, k == 7, [slot.t, X40_t], [pc.t])
                a = ta[j % 2]
                b = tb[j % 2]
                TT(a.ap[:, :N], pa.ap[:, :N], rs_a.ap[:, :N], ALU.mult, [pa.t, rs_a.t], [a.t])
                TT(b.ap[:, :N], pc.ap[:, :N], rs_c.ap[:, :N], ALU.mult, [pc.t, rs_c.t], [b.t])
                TT(a.ap[:, :N], a.ap[:, :N], b.ap[:, :N], ALU.add, [a.t, b.t], [a.t])
                STT(xg3[:, j, :N], a.ap[:, :N], mod3[:, 32 + j, s:s + 1], xg3[:, j, :N], ALU.mult, ALU.add,
                    [a.t, mod.t, xg_t[j]], [xg_t[j]])
                sq = sqb[j % 2]
                ACT(sq.ap[:, :N], xg3[:, j, :N], AF.Square, [xg_t[j]], [sq.t])
                MM(ps[4].ap[:, :N], ones_d.ap, sq.ap[:, :N], j == 0, j == 15, [sq.t, ones_d.t], [ps[4].t], mark=True)
                DMA(qS, xres_v[:, j, s0:s0 + N], xg3[:, j, :N], [xg_t[j]], [xres_t[g][j]])
            rs2 = RSTD(ps[4], N)
            for k in range(16):
                tm = tmpm[k % 2]
                STT(tm.ap[:, :N], xg3[:, k, :N], G2v[:, k, s:s + 1], rs2.ap[:, :N], ALU.mult, ALU.mult,
                    [xg_t[k], G2.t, rs2.t], [tm.t])
                ACT(h23[:, k, :N], tm.ap[:, :N], AF.Identity, [tm.t, mod.t], [yk_t], bias=mod3[:, 48 + k, s:s + 1],
                    scale=1.0)
            slots = {}
            for j in range(2):
                slots[j] = load_w(w1_v[l][:, :, j * 128:(j + 1) * 128])
            for j in range(64):
                slot = slots.pop(j)
                if j + 2 < 64:
                    slots[j + 2] = load_w(w1_v[l][:, :, (j + 2) * 128:(j + 3) * 128])
                w3 = slot.ap.rearrange("p (k c) -> p k c", k=16)
                pb = ps[5 + j % 3]
                for k in range(16):
                    MM(pb.ap[:, :N], w3[:, k, :], h23[:, k, :N], k == 0, k == 15, [slot.t, yk_t], [pb.t])
                r_ = rl[j % 2]
                ACT(r_.ap[:, :N], pb.ap[:, :N], AF.Relu, [pb.t], [r_.t])
                TT(aT3[:, j, :N], r_.ap[:, :N], r_.ap[:, :N], ALU.mult, [r_.t], [aT_tile(j)])
            def load_w2(j):
                res = []
                for hh in range(2):
                    sl = w2slot()
                    DMA(qG, sl.ap.rearrange("p (k c) -> p k c", k=32),
                        w2_v[l][:, hh * 32:(hh + 1) * 32, j * 128:(j + 1) * 128], [], [sl.t])
                    res.append(sl)
                return res
            pend = {0: load_w2(0)}
            for j in range(16):
                sls = pend.pop(j)
                if j + 1 < 16:
                    pend[j + 1] = load_w2(j + 1)
                pb = ps[0 + j % 2]
                x_ = xc[j % 2]
                DMA(qS, x_.ap[:, :N], xres_v[:, j, s0:s0 + N], [xres_t[g][j]], [x_.t])
                for kk in range(64):
                    sl = sls[kk // 32]
                    w3 = sl.ap.rearrange("p (k c) -> p k c", k=32)
                    MM(pb.ap[:, :N], w3[:, kk % 32, :], aT3[:, kk, :N], kk == 0, kk == 63, [sl.t, aT_tile(kk)], [pb.t])
                o_ = ost[j % 2]
                STT(o_.ap[:, :N], pb.ap[:, :N], mod3[:, 80 + j, s:s + 1], x_.ap[:, :N], ALU.mult, ALU.add,
                    [pb.t, mod.t, x_.t], [o_.t])
                if last:
                    DMA(qS, outT_v[:, j, s0:s0 + N], o_.ap[:, :N], [o_.t], [out_t[g][j]])
                else:
                    DMA(qS, xres_v[:, j, s0:s0 + N], o_.ap[:, :N], [o_.t], [xres_t[g][j]])

        def program():
            for l in range(depth):
                last = (l == depth - 1)
                if stage == 0:
                    return
                adaln(l)
                if stage == 1:
                    return
                DMA(qS, cos_ap, ropec_d.ap(), [], [X32_t])
                DMA(qS, sin_ap, ropes_d.ap(), [], [X40_t])
                wvs = w2slot()
                wv3 = wvs.ap.rearrange("p (k c) -> p k c", k=16)
                DMA(qG, wv3, w_in_v[l][:, :, 1280:1536], [], [wvs.t])
                for g in range(5):
                    phaseA(l, g, wv3, wvs.t)
                    if stage == 2:
                        return
                if stage == 3:
                    return
                yfix4 = exchange(l)
                if stage == 4:
                    return
                for g in range(5):
                    if last and GROUPS[g][2]:
                        continue
                    phaseB(l, g, yfix4, last)
                    if stage == 5:
                        return
        program()
        for g in range(4):
            for j in range(16):
                if out_t[g][j].w is not None:
                    K.wait(qS, out_t[g][j].w)
        for q in (qP, qA, qV, qG):
            if q.sem.count > 0:
                K.wait(qS, (q.sem, q.sem.count))
        for s_ in qS.dsems + qG.dsems:
            if s_.count > 0:
                K.wait(qS, (s_, s_.count))

        block = es.enter_context(nc.Block())

        @block.tensor
        def _(e):
            for f in qP.ops:
                f(e)

        @block.scalar
        def _(e):
            for f in qA.ops:
                f(e)

        @block.vector
        def _(e):
            for f in qV.ops:
                f(e)

        @block.gpsimd
        def _(e):
            for f in qG.ops:
                f(e)

        @block.sync
        def _(e):
            for f in qS.ops:
                f(e)
    counts = {q.name: len(q.ops) for q in (qP, qA, qV, qG, qS)}
    print("instruction counts", counts, flush=True)
    return nc


def rope_tables(h):
    t = np.arange(h * NLAT, (h + 1) * NLAT)
    row = (t // 64).astype(np.float32)
    col = (t % 64).astype(np.float32)
    inv = np.float32(10000.0) ** (-(np.arange(0, 64, 2, dtype=np.float32)) / np.float32(64))
    ang_r = row[:, None] * inv[None, :]
    ang_c = col[:, None] * inv[None, :]
    ang_r = np.concatenate([ang_r, ang_r], -1)
    ang_c = np.concatenate([ang_c, ang_c], -1)
    ang = np.concatenate([ang_r, ang_c], -1).astype(np.float32)
    cos = np.cos(ang).astype(np.float32).T
    sin = np.sin(ang).astype(np.float32).T
    sgn = np.ones((128, 1), np.float32)
    sgn[0:32] = -1.0
    sgn[64:96] = -1.0
    ss = sin * sgn
    sh = np.empty_like(ss)
    sh[32:64] = ss[0:32]
    sh[0:32] = ss[32:64]
    sh[96:128] = ss[64:96]
    sh[64:96] = ss[96:128]
    return np.ascontiguousarray(cos), np.ascontiguousarray(sh)


def prep(inputs, depth):
    f = lambda a: np.asarray(a, dtype=np.float32)
    x, c, ctx, c_ctx = f(inputs["x"]), f(inputs["c"]), f(inputs["ctx"]), f(inputs["c_ctx"])
    vec_layers = []
    for l in range(depth):
        cols = [f(inputs["b_ada"])[l].reshape(96, 128).T,
                f(inputs["norm1_g"])[l].reshape(16, 128).T,
                f(inputs["norm2_g"])[l].reshape(16, 128).T,
                f(inputs["q_norm_g"])[l].reshape(1, 128).T,
                f(inputs["k_norm_g"])[l].reshape(1, 128).T]
        cw = f(inputs["conv_w"])[l]
        for tap in range(3):
            cols.append(cw[tap].reshape(8, 128).T)
        cols.append(f(inputs["conv_b"])[l].reshape(8, 128).T)
        cols.append(f(inputs["attn_out_g"])[l].reshape(8, 128).T)
        cols.append(f(inputs["conv_out_g"])[l].reshape(8, 128).T)
        v = np.concatenate(cols, axis=1)
        assert v.shape == (128, NVEC)
        vec_layers.append(v)
    vecs = np.ascontiguousarray(np.concatenate(vec_layers, axis=1))
    shared = {
        "vecs": vecs,
        "w_ada": np.ascontiguousarray(f(inputs["w_ada"])[:depth]),
        "w_in": np.ascontiguousarray(f(inputs["w_in"])[:depth]),
        "w_out": np.ascontiguousarray(f(inputs["w_out"])[:depth]),
        "w1": np.ascontiguousarray(f(inputs["w_mlp_in"])[:depth]),
        "w2": np.ascontiguousarray(f(inputs["w_mlp_out"])[:depth]),
    }
    tabs = [rope_tables(0), rope_tables(1)]
    in_maps = []
    for r in range(8):
        b, h = r // 2, r % 2
        cc = np.stack([c[b], c_ctx], -1).reshape(16, 128, 2).transpose(1, 0, 2).reshape(128, 32)
        hmask = np.zeros((128, 2), np.float32)
        hmask[:, 0] = 1.0 if h == 1 else 0.0
        hmask[:, 1] = 1.0 if h == 0 else 0.0
        m = dict(shared)
        m["xT"] = np.ascontiguousarray(x[b, h * NLAT:(h + 1) * NLAT, :].T)
        m["cxT"] = np.ascontiguousarray(ctx[b].T)
        m["cc"] = np.ascontiguousarray(cc)
        m["ropec"] = tabs[h][0]
        m["ropes"] = tabs[h][1]
        m["hmask"] = hmask
        in_maps.append(m)
    return in_maps


_NC_CACHE = {}


def run(inputs, depth=DEPTH, stage=99):
    if depth not in _NC_CACHE:
        _NC_CACHE[depth] = build(depth, stage)
    nc = _NC_CACHE[depth]
    in_maps = prep(inputs, depth)
    res = run_bass_kernel_spmd(nc, in_maps, core_ids=list(range(8)))
    out = np.empty((4, 4096, D), np.float32)
    for r in range(8):
        b, h = r // 2, r % 2
        out[b, h * NLAT:(h + 1) * NLAT, :] = np.asarray(res.results[r]["outT"]).T
    return out


def kernel(**inputs):
    return run(inputs, DEPTH)
```

```python
import numpy as np
from contextlib import ExitStack
import concourse.bass as bass
import concourse.mybir as mybir
from concourse.bass_utils import run_bass_kernel_spmd

F32 = mybir.dt.float32
BF16 = mybir.dt.bfloat16
ALU = mybir.AluOpType
AF = mybir.ActivationFunctionType

D = 2048
KC = 16
NLAT = 2048
NCTX = 256
NTOK = NLAT + NCTX
DEPTH = 4
EPS = 1e-6
SCALE = 128.0 ** -0.5
NVEC = 178
V_BADA, V_N1G, V_N2G, V_QG, V_KG, V_CW, V_CB, V_AG, V_CG = 0, 96, 112, 128, 129, 130, 154, 162, 170
GROUPS = [(0, 512, False), (512, 512, False), (1024, 512, False), (1536, 512, False), (2048, 256, True)]
PAIRS = [[0, 1], [2, 3], [4, 5], [6, 7]]
NW2 = 4


class Tile:
    __slots__ = ("name", "w", "r")

    def __init__(self, name):
        self.name = name
        self.w = None
        self.r = {}


class Sem:
    __slots__ = ("h", "count")

    def __init__(self, h):
        self.h = h
        self.count = 0


class Queue:
    def __init__(self, name, kind):
        self.name = name
        self.kind = kind
        self.ops = []
        self.seen = {}
        self.sem = None
        self.dsems = []
        self.di = 0


class Tracker:
    def wait(self, q, tok):
        s, v = tok
        if q.seen.get(s, 0) >= v:
            return
        q.seen[s] = v
        q.ops.append(lambda e, s=s, v=v: e.wait_ge(s.h, v))

    def deps(self, q, reads, writes):
        toks = []
        for t in reads:
            if t.w is not None:
                toks.append(t.w)
        for t in writes:
            if t.w is not None:
                toks.append(t.w)
            toks.extend(t.r.items())
        for tok in toks:
            if q.kind == "pe" and tok[0] is q.sem:
                continue
            self.wait(q, tok)

    def finish(self, tok, reads, writes):
        s, v = tok
        for t in reads:
            if t.r.get(s, 0) < v:
                t.r[s] = v
        for t in writes:
            t.w = tok
            t.r = {}

    def op(self, q, f, reads=(), writes=(), mark=True):
        self.deps(q, reads, writes)
        s = q.sem
        if mark:
            s.count += 1
            v = s.count
            q.ops.append(lambda e, f=f, s=s: f(e).then_inc(s.h, 1))
        else:
            v = s.count + 1
            q.ops.append(lambda e, f=f: f(e))
        self.finish((s, v), reads, writes)

    def dma(self, q, f, reads=(), writes=()):
        self.deps(q, reads, writes)
        s = q.dsems[q.di]
        q.di = (q.di + 1) % len(q.dsems)
        if s.count > 0:
            self.wait(q, (s, s.count))
        s.count += 16
        v = s.count
        q.ops.append(lambda e, f=f, s=s: f(e).then_inc(s.h, 16))
        self.finish((s, v), reads, writes)


def build(depth=DEPTH, stage=99, fake_cc=False):
    nc = bass.Bass("TRN2", target_bir_lowering=False)
    K = Tracker()

    def din(name, shape, dt=F32):
        return nc.dram_tensor(name, shape, dt, kind="ExternalInput")

    xT = din("xT", [D, NLAT])
    cxT = din("cxT", [D, NCTX])
    cc_d = din("cc", [128, 32])
    ropec_d = din("ropec", [128, NLAT])
    ropes_d = din("ropes", [128, NLAT])
    hmask_d = din("hmask", [128, 2])
    vecs_d = din("vecs", [128, depth * NVEC])
    w_ada = din("w_ada", [depth, D, 6 * D])
    w_in = din("w_in", [depth, D, 4608])
    w_out = din("w_out", [depth, D, D])
    w1 = din("w1", [depth, D, 4 * D])
    w2 = din("w2", [depth, 4 * D, D])
    outT = nc.dram_tensor("outT", [D, NLAT], F32, kind="ExternalOutput")

    xres = nc.dram_tensor("xres", [D, NTOK], F32)
    q_d = nc.dram_tensor("q_d", [1024, NTOK], BF16)
    yg_d = nc.dram_tensor("yg_d", [1024, NTOK], BF16)
    kctx_d = nc.dram_tensor("kctx_d", [256, NCTX], BF16)
    vctx_d = nc.dram_tensor("vctx_d", [NCTX, 256], BF16)
    kT_in = [nc.dram_tensor(f"kT_in{l}", [256, NLAT], BF16) for l in range(depth)]
    v_in = [nc.dram_tensor(f"v_in{l}", [NLAT, 256], BF16) for l in range(depth)]
    halo_in = [nc.dram_tensor(f"halo_in{l}", [128, 16], F32) for l in range(depth)]
    kT_all = [nc.dram_tensor(f"kT_all{l}", [512, NLAT], BF16) for l in range(depth)]
    v_all = [nc.dram_tensor(f"v_all{l}", [2 * NLAT, 256], BF16) for l in range(depth)]
    halo_all = [nc.dram_tensor(f"halo_all{l}", [256, 16], F32) for l in range(depth)]

    xres_v = xres.ap().rearrange("(k p) t -> p k t", p=128)
    outT_v = outT.ap().rearrange("(k p) t -> p k t", p=128)
    xT_v = xT.ap().rearrange("(k p) t -> p k t", p=128)
    cxT_v = cxT.ap().rearrange("(k p) t -> p k t", p=128)
    q_d_v = q_d.ap().rearrange("(h p) t -> p h t", p=128)
    yg_d_v = yg_d.ap().rearrange("(h p) t -> p h t", p=128)
    vctx_v = vctx_d.ap().rearrange("(t p) c -> p t c", p=128)
    w_in_v = [w_in[l].rearrange("(k p) c -> p k c", p=128) for l in range(depth)]
    w_out_v = [w_out[l].rearrange("(k p) c -> p k c", p=128) for l in range(depth)]
    w1_v = [w1[l].rearrange("(k p) c -> p k c", p=128) for l in range(depth)]
    w2_v = [w2[l].rearrange("(k p) c -> p k c", p=128) for l in range(depth)]
    w_ada_v = [w_ada[l].rearrange("(k p) c -> p k c", p=128) for l in range(depth)]
    v_all_v = [v_all[l].ap().rearrange("(t p) c -> p t c", p=128) for l in range(depth)]

    with ExitStack() as es:
        def sb(name, shape, dt):
            return es.enter_context(nc.sbuf_tensor(name, shape, dt))

        def newsem(name):
            return Sem(es.enter_context(nc.semaphore(name)))

        qP = Queue("pe", "pe")
        qA = Queue("act", "act")
        qV = Queue("dve", "dve")
        qG = Queue("pool", "pool")
        qS = Queue("sp", "sp")
        for q in (qP, qA, qV, qG, qS):
            q.sem = newsem("s_" + q.name)
        qS.dsems = [newsem(f"ds{i}") for i in range(16)]
        qG.dsems = [newsem(f"dg{i}") for i in range(8)]

        class Buf:
            def __init__(self, ap, t):
                self.ap = ap
                self.t = t

        def mk(name, shape, dt):
            t = sb(name, shape, dt)
            return Buf(t[:], Tile(name))

        ones_d = mk("ones_d", [128, 128], BF16)
        ones_h = mk("ones_h", [128, 128], BF16)
        ones_g = mk("ones_g", [128, 128], BF16)
        ones_1 = mk("ones_1", [128, 128], BF16)
        eps_t = mk("eps_t", [128, 1], F32)
        vecs = mk("vecs_sb", [128, depth * NVEC], F32)
        ccs = mk("ccs", [128, 32], F32)
        mod = mk("mod", [128, 192], F32)
        G1 = mk("G1", [128, 32], F32)
        G2 = mk("G2", [128, 32], F32)
        rga = mk("rga", [128, 8], F32)
        rgc = mk("rgc", [128, 8], F32)
        hm = mk("hm", [128, 2], F32)
        zf = mk("zf", [128, 32], F32)
        zl = mk("zl", [128, 32], F32)
        bfb = mk("bfb", [128, 32], F32)
        blb = mk("blb", [128, 32], F32)
        yfb = mk("yfb", [128, 32], F32)
        ylb = mk("ylb", [128, 32], F32)
        yfix = mk("yfix", [128, 64], BF16)
        hl = mk("hl", [128, 8], F32)
        hr = mk("hr", [128, 8], F32)
        t8 = [mk(f"t8_{i}", [128, 8], F32) for i in range(2)]
        wbuf = [mk(f"wbuf{i}", [128, 16 * 256], BF16) for i in range(3)]
        w2buf = [mk(f"w2buf{i}", [128, 32 * 128], BF16) for i in range(NW2)]
        pT = [mk(f"pT{i}", [128, 512], BF16) for i in range(3)]
        sqb = [mk(f"sqb{i}", [128, 512], BF16) for i in range(2)]
        st16 = [mk(f"st16_{i}", [128, 512], BF16) for i in range(3)]
        vst = [mk(f"vst{i}", [128, 256], BF16) for i in range(2)]
        rstd = [mk(f"rstd{i}", [128, 512], F32) for i in range(3)]
        tmpm = [mk(f"tmpm{i}", [128, 512], F32) for i in range(2)]
        qn = [mk(f"qn{i}", [128, 512], F32) for i in range(2)]
        t1 = [mk(f"t1_{i}", [128, 512], F32) for i in range(2)]
        t2 = [mk(f"t2_{i}", [128, 512], F32) for i in range(2)]
        cs = [mk(f"cs{i}", [128, 512], F32) for i in range(2)]
        bs = [mk(f"bs{i}", [128, 512], F32) for i in range(2)]
        zb = [mk(f"zb{i}", [128, 516], F32) for i in range(2)]
        cv = [mk(f"cv{i}", [128, 512], F32) for i in range(1)]
        yv = [mk(f"yv{i}", [128, 512], F32) for i in range(1)]
        rec, ta, tb, rl, xc, ost = qn, t1, t2, cs, bs, zb

        arenaX = sb("arenaX", [128, 32 * 1024], BF16)
        arenaY = sb("arenaY", [128, 12800], BF16)
        xg3 = arenaX[:, 0:16384].bitcast(F32).rearrange("p (k n) -> p k n", k=16)
        xg_t = [Tile(f"xg{k}") for k in range(16)]
        X32_t = Tile("X32")
        X40_t = Tile("X40")
        X48_t = Tile("X48")
        cos_ap = arenaX[:, 16384:20480].bitcast(F32)
        sin_ap = arenaX[:, 20480:24576].bitcast(F32)
        og3 = arenaX[:, 16384:20480].rearrange("p (h n) -> p h n", h=8)
        yg3 = arenaX[:, 20480:24576].rearrange("p (h n) -> p h n", h=8)
        hT3 = arenaX[:, 24576:32768].rearrange("p (k n) -> p k n", k=16)
        aT3 = arenaX[:, 0:32768].rearrange("p (j n) -> p j n", j=64)

        def aT_tile(j):
            if j < 32:
                return xg_t[j // 2]
            if j < 40:
                return X32_t
            if j < 48:
                return X40_t
            return X48_t

        yk_t = Tile("yk")
        kh = arenaY[:, 0:4352]
        vh3 = arenaY[:, 4352:8704].rearrange("p (t d) -> p t d", t=34)
        h23 = arenaY[:, 0:8192].rearrange("p (k n) -> p k n", k=16)
        qg3 = arenaY[:, 8704:12800].rearrange("p (h n) -> p h n", h=8)
        qg_t = Tile("qg")

        ps = []
        for i in range(8):
            t = es.enter_context(nc.psum_tensor(f"ps{i}", [128, 512], F32))
            ps.append(Buf(t[:], Tile(f"ps{i}")))

        xres_t = [[Tile(f"xres{g}_{j}") for j in range(16)] for g in range(5)]
        q_d_t = [[Tile(f"qd{g}_{h}") for h in range(8)] for g in range(5)]
        yg_d_t = [[Tile(f"ygd{g}_{i}") for i in range(8)] for g in range(5)]
        kctx_t = [Tile(f"kctx{h}") for h in range(2)]
        vctx_t = [Tile(f"vctx{t}") for t in range(2)]
        kT_in_t = [[[Tile(f"kTin{l}_{g}_{h}") for h in range(2)] for g in range(4)] for l in range(depth)]
        v_in_t = [[Tile(f"vin{l}_{t}") for t in range(16)] for l in range(depth)]
        halo_in_t = [Tile(f"haloin{l}") for l in range(depth)]
        kT_all_t = [Tile(f"kTall{l}") for l in range(depth)]
        v_all_t = [Tile(f"vall{l}") for l in range(depth)]
        halo_all_t = [Tile(f"haloall{l}") for l in range(depth)]
        out_t = [[Tile(f"out{g}_{j}") for j in range(16)] for g in range(4)]

        def ACT(out, in_, func, reads, writes, bias=None, scale=None):
            kw = {}
            if bias is not None:
                kw["bias"] = bias
            if scale is not None:
                kw["scale"] = scale
            K.op(qA, lambda e: e.activation(out=out, in_=in_, func=func, **kw), reads, writes)

        def TT(out, in0, in1, op, reads, writes):
            K.op(qV, lambda e: e.tensor_tensor(out=out, in0=in0, in1=in1, op=op), reads, writes)

        def TS(out, in0, s1, op0, reads, writes):
            K.op(qV, lambda e: e.tensor_scalar(out=out, in0=in0, scalar1=s1, scalar2=None, op0=op0), reads, writes)

        def STT(out, in0, scalar, in1, op0, op1, reads, writes):
            K.op(qV, lambda e: e.scalar_tensor_tensor(out=out, in0=in0, scalar=scalar, in1=in1, op0=op0, op1=op1),
                 reads, writes)

        def VCOPY(out, in_, reads, writes):
            K.op(qV, lambda e: e.tensor_copy(out=out, in_=in_), reads, writes)

        def RECIP(out, in_, reads, writes):
            K.op(qV, lambda e: e.reciprocal(out=out, in_=in_), reads, writes)

        def MEMSET(ap, val, writes):
            K.op(qV, lambda e: e.memset(ap, val), (), writes)

        def MM(out, lhsT, rhs, start, stop, reads, writes, mark=None):
            K.op(qP, lambda e: e.matmul(out, lhsT=lhsT, rhs=rhs, start=start, stop=stop), reads, writes,
                 mark=(stop if mark is None else mark))

        def DMA(q, out, in_, reads, writes):
            K.dma(q, lambda e: e.dma_start(out=out, in_=in_), reads, writes)

        ccount = [0]

        def ALLGATHER(in_ap, out_ap, reads, writes, in_h=None, out_h=None):
            if fake_cc:
                n0 = in_h.shape[0]
                DMA(qG, out_h[0:n0, :], in_h[:, :], reads, writes)
                return
            K.deps(qG, reads, writes)
            s = newsem(f"cc{ccount[0]}")
            ccount[0] += 1
            s.count = 1
            qG.ops.append(lambda e: e.collective_compute(
                "AllGather", ALU.bypass, replica_groups=PAIRS, ins=[in_ap], outs=[out_ap]).then_inc(s.h, 1))
            K.finish((s, 1), reads, writes)
            K.wait(qG, (s, 1))

        def vc(l, off, n):
            return vecs.ap[:, l * NVEC + off:l * NVEC + off + n]

        mod3 = mod.ap.rearrange("p (j s) -> p j s", s=2)
        G1v = G1.ap.rearrange("p (k s) -> p k s", s=2)
        G2v = G2.ap.rearrange("p (k s) -> p k s", s=2)

        rs_ctr = [0]

        def RSTD(psb, N):
            r = rstd[rs_ctr[0] % 3]
            rs_ctr[0] += 1
            ACT(r.ap[:, :N], psb.ap[:, :N], AF.Sqrt, [psb.t, eps_t.t], [r.t], bias=eps_t.ap[:, 0:1], scale=1.0)
            RECIP(r.ap[:, :N], r.ap[:, :N], [r.t], [r.t])
            return r

        wctr = [0]

        def load_w(src3):
            slot = wbuf[wctr[0] % 3]
            wctr[0] += 1
            DMA(qG, slot.ap.rearrange("p (k c) -> p k c", k=16), src3, [], [slot.t])
            return slot

        w2ctr = [0]

        def w2slot():
            s = w2buf[w2ctr[0] % NW2]
            w2ctr[0] += 1
            return s

        MEMSET(ones_d.ap, 1.0 / 2048.0, [ones_d.t])
        MEMSET(ones_h.ap, 1.0 / 128.0, [ones_h.t])
        MEMSET(ones_g.ap, 1.0 / 1024.0, [ones_g.t])
        MEMSET(ones_1.ap, 1.0, [ones_1.t])
        MEMSET(eps_t.ap, EPS, [eps_t.t])
        DMA(qS, vecs.ap, vecs_d.ap(), [], [vecs.t])
        DMA(qS, ccs.ap, cc_d.ap(), [], [ccs.t])
        DMA(qS, hm.ap, hmask_d.ap(), [], [hm.t])
        ACT(ccs.ap, ccs.ap, AF.Silu, [ccs.t], [ccs.t])
        for g, (s0, N, ctx) in enumerate(GROUPS):
            for j in range(16):
                src = cxT_v[:, j, :] if ctx else xT_v[:, j, s0:s0 + N]
                DMA(qS, xres_v[:, j, s0:s0 + N], src, [], [xres_t[g][j]])

        def adaln(l):
            pm = ps[7]
            for j in range(96):
                slot = w2slot()
                wa = slot.ap[:, 0:4096].bitcast(F32).rearrange("p (k c) -> p k c", k=16)
                DMA(qS, wa, w_ada_v[l][:, :, j * 128:(j + 1) * 128], [], [slot.t])
                for k in range(16):
                    MM(pm.ap[:, 2 * j:2 * j + 2], wa[:, k, :], ccs.ap[:, 2 * k:2 * k + 2], k == 0, k == 15,
                       [slot.t, ccs.t], [pm.t])
            pm3 = pm.ap[:, 0:192].rearrange("p (j s) -> p j s", s=2)
            for s in range(2):
                TT(mod3[:, :, s], pm3[:, :, s], vc(l, V_BADA, 96), ALU.add, [pm.t, vecs.t], [mod.t])
            for s in range(2):
                STT(G1v[:, :, s], mod3[:, 16:32, s], 1.0, vc(l, V_N1G, 16), ALU.add, ALU.mult, [mod.t, vecs.t], [G1.t])
                STT(G2v[:, :, s], mod3[:, 64:80, s], 1.0, vc(l, V_N2G, 16), ALU.add, ALU.mult, [mod.t, vecs.t], [G2.t])
            RECIP(rga.ap, vc(l, V_AG, 8), [vecs.t], [rga.t])
            RECIP(rgc.ap, vc(l, V_CG, 8), [vecs.t], [rgc.t])

        chunks = [("q", h, h * 128) for h in range(8)] + [("k", h, 1024 + h * 128) for h in range(2)]
        for i in range(8):
            chunks += [("b", i, 1536 + i * 128), ("c", i, 2560 + i * 128), ("u", i, 3584 + i * 128)]

        pairsA = [("q", (0, 1), 0), ("q", (2, 3), 256), ("q", (4, 5), 512), ("q", (6, 7), 768), ("k", (0, 1), 1024)]
        for ip in range(4):
            pairsA += [("b", (2 * ip, 2 * ip + 1), 1536 + ip * 256), ("c", (2 * ip, 2 * ip + 1), 2560 + ip * 256),
                       ("u", (2 * ip, 2 * ip + 1), 3584 + ip * 256)]
        mctr = [0]
        nctr = [0]
        sctr = [0]
        qctr = [0]

        def chunk_epilogue(l, g, kind, idx, pm, deferred, cur_b, cur_c):
            s0, N, ctx = GROUPS[g]
            if kind in ("q", "k"):
                sq = sqb[sctr[0] % 2]
                sctr[0] += 1
                ACT(sq.ap[:, :N], pm.ap[:, :N], AF.Square, [pm.t], [sq.t])

                def stage2(kind=kind, idx=idx, pm=pm, sq=sq):
                    nb = ps[6 + nctr[0] % 2]
                    nctr[0] += 1
                    MM(nb.ap[:, :N], ones_h.ap, sq.ap[:, :N], True, True, [sq.t, ones_h.t], [nb.t])
                    r = RSTD(nb, N)
                    qq = qn[qctr[0] % 2]
                    tt1 = t1[qctr[0] % 2]
                    tt2 = t2[qctr[0] % 2]
                    so = st16[qctr[0] % 3]
                    qctr[0] += 1
                    gq = vc(l, V_QG if kind == "q" else V_KG, 1)
                    STT(qq.ap[:, :N], pm.ap[:, :N], gq, r.ap[:, :N], ALU.mult, ALU.mult, [pm.t, vecs.t, r.t], [qq.t])
                    if ctx:
                        ACT(so.ap[:, :N], qq.ap[:, :N], AF.Copy, [qq.t], [so.t])
                    else:
                        TT(tt1.ap[:, :N], qq.ap[:, :N], cos_ap[:, s0:s0 + N], ALU.mult, [qq.t, X32_t], [tt1.t])
                        for (a, b) in ((0, 32), (32, 0), (64, 96), (96, 64)):
                            TT(tt2.ap[a:a + 32, :N], qq.ap[b:b + 32, :N], sin_ap[b:b + 32, s0:s0 + N], ALU.mult,
                               [qq.t, X40_t], [tt2.t])
                        TT(so.ap[:, :N], tt1.ap[:, :N], tt2.ap[:, :N], ALU.add, [tt1.t, tt2.t], [so.t])
                    if kind == "q":
                        DMA(qS, q_d[idx * 128:(idx + 1) * 128, s0:s0 + N], so.ap[:, :N], [so.t], [q_d_t[g][idx]])
                    elif ctx:
                        DMA(qS, kctx_d[idx * 128:(idx + 1) * 128, :], so.ap[:, :N], [so.t], [kctx_t[idx]])
                    else:
                        DMA(qS, kT_in[l][idx * 128:(idx + 1) * 128, s0:s0 + N], so.ap[:, :N], [so.t],
                            [kT_in_t[l][g][idx]])
                deferred.append(stage2)
            elif kind == "b":
                b_ = bs[idx % 2]
                ACT(b_.ap[:, :N], pm.ap[:, :N], AF.Copy, [pm.t], [b_.t])
                cur_b[idx] = b_
            elif kind == "c":
                c_ = cs[idx % 2]
                ACT(c_.ap[:, :N], pm.ap[:, :N], AF.Copy, [pm.t], [c_.t])
                cur_c[idx] = c_
            else:
                i = idx
                b_, c_ = cur_b[idx], cur_c[idx]
                z = zb[i % 2]
                MEMSET(z.ap[:, 0:1], 0.0, [z.t])
                MEMSET(z.ap[:, N + 1:N + 2], 0.0, [z.t])
                TT(z.ap[:, 1:N + 1], pm.ap[:, :N], c_.ap[:, :N], ALU.mult, [pm.t, c_.t], [z.t])
                c0 = cv[0]
                y0 = yv[0]
                TS(c0.ap[:, :N], z.ap[:, 0:N], vc(l, V_CW + i, 1), ALU.mult, [z.t, vecs.t], [c0.t])
                STT(c0.ap[:, :N], z.ap[:, 1:N + 1], vc(l, V_CW + 8 + i, 1), c0.ap[:, :N], ALU.mult, ALU.add,
                    [z.t, vecs.t, c0.t], [c0.t])
                STT(c0.ap[:, :N], z.ap[:, 2:N + 2], vc(l, V_CW + 16 + i, 1), c0.ap[:, :N], ALU.mult, ALU.add,
                    [z.t, vecs.t, c0.t], [c0.t])
                STT(y0.ap[:, :N], c0.ap[:, :N], vc(l, V_CB + i, 1), b_.ap[:, :N], ALU.add, ALU.mult,
                    [c0.t, vecs.t, b_.t], [y0.t])
                if not ctx:
                    c8 = g * 8 + i
                    ACT(zf.ap[:, c8:c8 + 1], z.ap[:, 1:2], AF.Copy, [z.t], [zf.t])
                    ACT(zl.ap[:, c8:c8 + 1], z.ap[:, N:N + 1], AF.Copy, [z.t], [zl.t])
                    ACT(bfb.ap[:, c8:c8 + 1], b_.ap[:, 0:1], AF.Copy, [b_.t], [bfb.t])
                    ACT(blb.ap[:, c8:c8 + 1], b_.ap[:, N - 1:N], AF.Copy, [b_.t], [blb.t])
                    ACT(yfb.ap[:, c8:c8 + 1], y0.ap[:, 0:1], AF.Copy, [y0.t], [yfb.t])
                    ACT(ylb.ap[:, c8:c8 + 1], y0.ap[:, N - 1:N], AF.Copy, [y0.t], [ylb.t])
                so = st16[qctr[0] % 3]
                qctr[0] += 1
                TS(so.ap[:, :N], y0.ap[:, :N], vc(l, V_CG + i, 1), ALU.mult, [y0.t, vecs.t], [so.t])
                DMA(qS, yg_d[i * 128:(i + 1) * 128, s0:s0 + N], so.ap[:, :N], [so.t], [yg_d_t[g][i]])

        def phaseA(l, g, wv3, wv_t):
            s0, N, ctx = GROUPS[g]
            s = 1 if ctx else 0
            for kq in range(4):
                DMA(qS, xg3[:, 4 * kq:4 * kq + 4, 0:N], xres_v[:, 4 * kq:4 * kq + 4, s0:s0 + N],
                    [xres_t[g][j] for j in range(4 * kq, 4 * kq + 4)], [xg_t[j] for j in range(4 * kq, 4 * kq + 4)])
            for k in range(16):
                sq = sqb[k % 2]
                ACT(sq.ap[:, :N], xg3[:, k, :N], AF.Square, [xg_t[k]], [sq.t])
                MM(ps[0].ap[:, :N], ones_d.ap, sq.ap[:, :N], k == 0, k == 15, [sq.t, ones_d.t], [ps[0].t], mark=True)
            rs = RSTD(ps[0], N)
            for k in range(16):
                tm = tmpm[k % 2]
                STT(tm.ap[:, :N], xg3[:, k, :N], G1v[:, k, s:s + 1], rs.ap[:, :N], ALU.mult, ALU.mult,
                    [xg_t[k], G1.t, rs.t], [tm.t])
                ACT(hT3[:, k, :N], tm.ap[:, :N], AF.Identity, [tm.t, mod.t], [X48_t], bias=mod3[:, k, s:s + 1], scale=1.0)
            for tt in range(N // 128):
                pb = ps[1 + tt % 2]
                for k in range(16):
                    MM(pb.ap[:, 0:256], hT3[:, k, tt * 128:(tt + 1) * 128], wv3[:, k, :], k == 0, k == 15,
                       [X48_t, wv_t], [pb.t])
                vs = vst[tt % 2]
                ACT(vs.ap, pb.ap[:, 0:256], AF.Copy, [pb.t], [vs.t])
                if ctx:
                    DMA(qS, vctx_d[tt * 128:(tt + 1) * 128, :], vs.ap, [vs.t], [vctx_t[tt]])
                else:
                    ti = s0 // 128 + tt
                    DMA(qS, v_in[l][ti * 128:(ti + 1) * 128, :], vs.ap, [vs.t], [v_in_t[l][ti]])

            npair = len(pairsA)
            slots = {}
            for pi in range(min(2, npair)):
                slots[pi] = load_w(w_in_v[l][:, :, pairsA[pi][2]:pairsA[pi][2] + 256])
            deferred = []
            cur_b = {}
            cur_c = {}
            for pi, (kind, idxs, colp) in enumerate(pairsA):
                slot = slots.pop(pi)
                if pi + 2 < npair:
                    c2 = pairsA[pi + 2][2]
                    slots[pi + 2] = load_w(w_in_v[l][:, :, c2:c2 + 256])
                w3p = slot.ap.rearrange("p (k c) -> p k c", k=16)
                for cc in range(2):
                    idx = idxs[cc]
                    pm = ps[3 + mctr[0] % 3]
                    mctr[0] += 1
                    for k in range(16):
                        MM(pm.ap[:, :N], w3p[:, k, cc * 128:(cc + 1) * 128], hT3[:, k, :N], k == 0, k == 15,
                           [slot.t, X48_t], [pm.t])
                    for f in deferred:
                        f()
                    deferred = []
                    chunk_epilogue(l, g, kind, idx, pm, deferred, cur_b, cur_c)
            for f in deferred:
                f()

        def exchange(l):
            DMA(qS, halo_in[l][:, 0:8], zf.ap[:, 0:8], [zf.t], [halo_in_t[l]])
            DMA(qS, halo_in[l][:, 8:16], zl.ap[:, 24:32], [zl.t], [halo_in_t[l]])
            ALLGATHER(kT_in[l].ap().opt(), kT_all[l].ap().opt(),
                      [t for gg in kT_in_t[l] for t in gg], [kT_all_t[l]], kT_in[l], kT_all[l])
            ALLGATHER(v_in[l].ap().opt(), v_all[l].ap().opt(), v_in_t[l], [v_all_t[l]], v_in[l], v_all[l])
            ALLGATHER(halo_in[l].ap().opt(), halo_all[l].ap().opt(), [halo_in_t[l]], [halo_all_t[l]], halo_in[l], halo_all[l])
            DMA(qS, hl.ap, halo_all[l][0:128, 8:16], [halo_all_t[l]], [hl.t])
            DMA(qS, hr.ap, halo_all[l][128:256, 0:8], [halo_all_t[l]], [hr.t])
            TS(hl.ap, hl.ap, hm.ap[:, 0:1], ALU.mult, [hl.t, hm.t], [hl.t])
            TS(hr.ap, hr.ap, hm.ap[:, 1:2], ALU.mult, [hr.t, hm.t], [hr.t])
            yfix4 = yfix.ap.rearrange("p (g e i) -> p g e i", g=4, e=2)
            for g in range(4):
                g8 = slice(g * 8, g * 8 + 8)
                left = hl.ap if g == 0 else zl.ap[:, (g - 1) * 8:g * 8]
                lt = hl.t if g == 0 else zl.t
                a = t8[0]
                TT(a.ap, bfb.ap[:, g8], vc(l, V_CW, 8), ALU.mult, [bfb.t, vecs.t], [a.t])
                TT(a.ap, a.ap, left, ALU.mult, [a.t, lt], [a.t])
                TT(a.ap, a.ap, yfb.ap[:, g8], ALU.add, [a.t, yfb.t], [a.t])
                TT(yfix4[:, g, 0, :], a.ap, vc(l, V_CG, 8), ALU.mult, [a.t, vecs.t], [yfix.t])
                right = hr.ap if g == 3 else zf.ap[:, (g + 1) * 8:(g + 2) * 8]
                rt = hr.t if g == 3 else zf.t
                b = t8[1]
                TT(b.ap, blb.ap[:, g8], vc(l, V_CW + 16, 8), ALU.mult, [blb.t, vecs.t], [b.t])
                TT(b.ap, b.ap, right, ALU.mult, [b.t, rt], [b.t])
                TT(b.ap, b.ap, ylb.ap[:, g8], ALU.add, [b.t, ylb.t], [b.t])
                TT(yfix4[:, g, 1, :], b.ap, vc(l, V_CG, 8), ALU.mult, [b.t, vecs.t], [yfix.t])
            return yfix4

        uctr = [0]
        tctr = [0]
        pctr = [0]

        def attention(l, g):
            s0, N, ctx = GROUPS[g]
            DMA(qS, qg3[:, :, :N], q_d_v[:, :, s0:s0 + N], q_d_t[g], [qg_t])
            kts = [32, 33] if ctx else list(range(34))
            n = len(kts)
            for hd in range(2):
                if not ctx:
                    DMA(qS, kh[:, 0:2048], kT_all[l][hd * 128:(hd + 1) * 128, :], [kT_all_t[l]], [yk_t])
                    DMA(qS, kh[:, 2048:4096], kT_all[l][256 + hd * 128:256 + (hd + 1) * 128, :], [kT_all_t[l]], [yk_t])
                    DMA(qS, vh3[:, 0:32, :], v_all_v[l][:, :, hd * 128:(hd + 1) * 128], [v_all_t[l]], [yk_t])
                DMA(qS, kh[:, 4096:4352], kctx_d[hd * 128:(hd + 1) * 128, :], [kctx_t[hd]], [yk_t])
                DMA(qS, vh3[:, 32:34, :], vctx_v[:, :, hd * 128:(hd + 1) * 128], vctx_t, [yk_t])
                for r in range(4):
                    head = hd * 4 + r
                    ob = ps[3 + uctr[0] % 2]
                    db = ps[5 + uctr[0] % 2]
                    uctr[0] += 1
                    pts = {}

                    def S(i):
                        kt = kts[i]
                        tb_ = ps[tctr[0] % 3]
                        tctr[0] += 1
                        MM(tb_.ap[:, :N], kh[:, kt * 128:(kt + 1) * 128], qg3[:, head, :N], True, True,
                           [yk_t, qg_t], [tb_.t])
                        p = pT[pctr[0] % 3]
                        pctr[0] += 1
                        ACT(p.ap[:, :N], tb_.ap[:, :N], AF.Exp, [tb_.t], [p.t], scale=SCALE)
                        pts[i] = p

                    S(0)
                    if n > 1:
                        S(1)
                    for i in range(n):
                        if i + 2 < n:
                            S(i + 2)
                        p = pts.pop(i)
                        kt = kts[i]
                        MM(ob.ap[:, :N], vh3[:, kt, :], p.ap[:, :N], i == 0, i == n - 1, [yk_t, p.t], [ob.t])
                        MM(db.ap[:, :N], ones_1.ap, p.ap[:, :N], i == 0, i == n - 1, [ones_1.t, p.t], [db.t], mark=True)
                    rc = rec[head % 2]
                    RECIP(rc.ap[:, :N], db.ap[:, :N], [db.t], [rc.t])
                    STT(og3[:, head, :N], ob.ap[:, :N], vc(l, V_AG + head, 1), rc.ap[:, :N], ALU.mult, ALU.mult,
                        [ob.t, vecs.t, rc.t], [X32_t])

        def phaseB(l, g, yfix4, last):
            s0, N, ctx = GROUPS[g]
            s = 1 if ctx else 0
            attention(l, g)
            DMA(qS, yg3[:, :, :N], yg_d_v[:, :, s0:s0 + N], yg_d_t[g], [X40_t])
            if not ctx:
                VCOPY(yg3[:, :, 0], yfix4[:, g, 0, :], [yfix.t], [X40_t])
                VCOPY(yg3[:, :, N - 1], yfix4[:, g, 1, :], [yfix.t], [X40_t])
            for kq in range(4):
                DMA(qS, xg3[:, 4 * kq:4 * kq + 4, 0:N], xres_v[:, 4 * kq:4 * kq + 4, s0:s0 + N],
                    [xres_t[g][j] for j in range(4 * kq, 4 * kq + 4)], [xg_t[j] for j in range(4 * kq, 4 * kq + 4)])
            for i in range(8):
                sq = sqb[i % 2]
                ACT(sq.ap[:, :N], og3[:, i, :N], AF.Square, [X32_t, rga.t], [sq.t], scale=rga.ap[:, i:i + 1])
                MM(ps[6].ap[:, :N], ones_g.ap, sq.ap[:, :N], i == 0, i == 7, [sq.t, ones_g.t], [ps[6].t], mark=True)
            for i in range(8):
                sq = sqb[i % 2]
                ACT(sq.ap[:, :N], yg3[:, i, :N], AF.Square, [X40_t, rgc.t], [sq.t], scale=rgc.ap[:, i:i + 1])
                MM(ps[7].ap[:, :N], ones_g.ap, sq.ap[:, :N], i == 0, i == 7, [sq.t, ones_g.t], [ps[7].t], mark=True)
            rs_a = RSTD(ps[6], N)
            rs_c = RSTD(ps[7], N)
            slots = {}
            for jp in range(2):
                slots[jp] = load_w(w_out_v[l][:, :, jp * 256:(jp + 1) * 256])
            for j in range(16):
                jp, cc = j // 2, j % 2
                if cc == 0:
                    slot = slots.pop(jp)
                    if jp + 2 < 8:
                        slots[jp + 2] = load_w(w_out_v[l][:, :, (jp + 2) * 256:(jp + 3) * 256])
                    w3p = slot.ap.rearrange("p (k c) -> p k c", k=16)
                w3 = w3p[:, :, cc * 128:(cc + 1) * 128]
                pa = ps[0 + j % 2]
                pc = ps[2 + j % 2]
                for k in range(8):
                    MM(pa.ap[:, :N], w3[:, k, :], og3[:, k, :N], k == 0, k == 7, [slot.t, X32_t], [pa.t])
                for k in range(8):
                    MM(pc.ap[:, :N], w3[:, 8 + k, :], yg3[:, k, :N], k == 0, k == 7, [slot.t, X40_t], [pc.t])
                a = ta[j % 2]
                b = tb[j % 2]
                TT(a.ap[:, :N], pa.ap[:, :N], rs_a.ap[:, :N], ALU.mult, [pa.t, rs_a.t], [a.t])
                TT(b.ap[:, :N], pc.ap[:, :N], rs_c.ap[:, :N], ALU.mult, [pc.t, rs_c.t], [b.t])
                TT(a.ap[:, :N], a.ap[:, :N], b.ap[:, :N], ALU.add, [a.t, b.t], [a.t])
                STT(xg3[:, j, :N], a.ap[:, :N], mod3[:, 32 + j, s:s + 1], xg3[:, j, :N], ALU.mult, ALU.add,
                    [a.t, mod.t, xg_t[j]], [xg_t[j]])
                sq = sqb[j % 2]
                ACT(sq.ap[:, :N], xg3[:, j, :N], AF.Square, [xg_t[j]], [sq.t])
                MM(ps[4].ap[:, :N], ones_d.ap, sq.ap[:, :N], j == 0, j == 15, [sq.t, ones_d.t], [ps[4].t], mark=True)
                DMA(qS, xres_v[:, j, s0:s0 + N], xg3[:, j, :N], [xg_t[j]], [xres_t[g][j]])
            rs2 = RSTD(ps[4], N)
            for k in range(16):
                tm = tmpm[k % 2]
                STT(tm.ap[:, :N], xg3[:, k, :N], G2v[:, k, s:s + 1], rs2.ap[:, :N], ALU.mult, ALU.mult,
                    [xg_t[k], G2.t, rs2.t], [tm.t])
                ACT(h23[:, k, :N], tm.ap[:, :N], AF.Identity, [tm.t, mod.t], [yk_t], bias=mod3[:, 48 + k, s:s + 1],
                    scale=1.0)
            slots = {}
            for jp in range(2):
                slots[jp] = load_w(w1_v[l][:, :, jp * 256:(jp + 1) * 256])
            for j in range(64):
                jp, cc = j // 2, j % 2
                if cc == 0:
                    slot = slots.pop(jp)
                    if jp + 2 < 32:
                        slots[jp + 2] = load_w(w1_v[l][:, :, (jp + 2) * 256:(jp + 3) * 256])
                    w3p = slot.ap.rearrange("p (k c) -> p k c", k=16)
                w3 = w3p[:, :, cc * 128:(cc + 1) * 128]
                pb = ps[5 + j % 3]
                for k in range(16):
                    MM(pb.ap[:, :N], w3[:, k, :], h23[:, k, :N], k == 0, k == 15, [slot.t, yk_t], [pb.t])
                r_ = rl[j % 2]
                ACT(r_.ap[:, :N], pb.ap[:, :N], AF.Relu, [pb.t], [r_.t])
                TT(aT3[:, j, :N], r_.ap[:, :N], r_.ap[:, :N], ALU.mult, [r_.t], [aT_tile(j)])
            def load_w2(jp, t):
                sl = w2slot()
                DMA(qG, sl.ap.rearrange("p (k c) -> p k c", k=16),
                    w2_v[l][:, t * 16:(t + 1) * 16, jp * 256:(jp + 1) * 256], [], [sl.t])
                return sl
            tiles = [(jp, t) for jp in range(8) for t in range(4)]
            pend = {}
            for ti in range(min(3, len(tiles))):
                pend[ti] = load_w2(*tiles[ti])
            for ti, (jp, t) in enumerate(tiles):
                sl = pend.pop(ti)
                if ti + 3 < len(tiles):
                    pend[ti + 3] = load_w2(*tiles[ti + 3])
                w3p = sl.ap.rearrange("p (k c) -> p k c", k=16)
                pbs = [ps[(jp % 2) * 2 + cc] for cc in range(2)]
                for kk in range(16):
                    kg = t * 16 + kk
                    for cc in range(2):
                        MM(pbs[cc].ap[:, :N], w3p[:, kk, cc * 128:(cc + 1) * 128], aT3[:, kg, :N],
                           kg == 0, kg == 63, [sl.t, aT_tile(kg)], [pbs[cc].t])
                if t == 3:
                    for cc in range(2):
                        j = 2 * jp + cc
                        pb = pbs[cc]
                        x_ = xc[j % 2]
                        DMA(qS, x_.ap[:, :N], xres_v[:, j, s0:s0 + N], [xres_t[g][j]], [x_.t])
                        o_ = ost[j % 2]
                        STT(o_.ap[:, :N], pb.ap[:, :N], mod3[:, 80 + j, s:s + 1], x_.ap[:, :N], ALU.mult, ALU.add,
                            [pb.t, mod.t, x_.t], [o_.t])
                        if last:
                            DMA(qS, outT_v[:, j, s0:s0 + N], o_.ap[:, :N], [o_.t], [out_t[g][j]])
                        else:
                            DMA(qS, xres_v[:, j, s0:s0 + N], o_.ap[:, :N], [o_.t], [xres_t[g][j]])

        def program():
            for l in range(depth):
                last = (l == depth - 1)
                if stage == 0:
                    return
                adaln(l)
                if stage == 1:
                    return
                DMA(qS, cos_ap, ropec_d.ap(), [], [X32_t])
                DMA(qS, sin_ap, ropes_d.ap(), [], [X40_t])
                wvs = w2slot()
                wv3 = wvs.ap.rearrange("p (k c) -> p k c", k=16)
                DMA(qG, wv3, w_in_v[l][:, :, 1280:1536], [], [wvs.t])
                for g in range(5):
                    phaseA(l, g, wv3, wvs.t)
                    if stage == 2:
                        return
                if stage == 3:
                    return
                yfix4 = exchange(l)
                if stage == 4:
                    return
                for g in range(5):
                    if last and GROUPS[g][2]:
                        continue
                    phaseB(l, g, yfix4, last)
                    if stage == 5:
                        return
        program()
        for g in range(4):
            for j in range(16):
                if out_t[g][j].w is not None:
                    K.wait(qS, out_t[g][j].w)
        for q in (qP, qA, qV, qG):
            if q.sem.count > 0:
                K.wait(qS, (q.sem, q.sem.count))
        for s_ in qS.dsems + qG.dsems:
            if s_.count > 0:
                K.wait(qS, (s_, s_.count))

        block = es.enter_context(nc.Block())

        @block.tensor
        def _(e):
            for f in qP.ops:
                f(e)

        @block.scalar
        def _(e):
            for f in qA.ops:
                f(e)

        @block.vector
        def _(e):
            for f in qV.ops:
                f(e)

        @block.gpsimd
        def _(e):
            for f in qG.ops:
                f(e)

        @block.sync
        def _(e):
            for f in qS.ops:
                f(e)
    counts = {q.name: len(q.ops) for q in (qP, qA, qV, qG, qS)}
    print("instruction counts", counts, flush=True)
    return nc


def rope_tables(h):
    t = np.arange(h * NLAT, (h + 1) * NLAT)
    row = (t // 64).astype(np.float32)
    col = (t % 64).astype(np.float32)
    inv = np.float32(10000.0) ** (-(np.arange(0, 64, 2, dtype=np.float32)) / np.float32(64))
    ang_r = row[:, None] * inv[None, :]
    ang_c = col[:, None] * inv[None, :]
    ang_r = np.concatenate([ang_r, ang_r], -1)
    ang_c = np.concatenate([ang_c, ang_c], -1)
    ang = np.concatenate([ang_r, ang_c], -1).astype(np.float32)
    cos = np.cos(ang).astype(np.float32).T
    sin = np.sin(ang).astype(np.float32).T
    sgn = np.ones((128, 1), np.float32)
    sgn[0:32] = -1.0
    sgn[64:96] = -1.0
    ss = sin * sgn
    sh = np.empty_like(ss)
    sh[32:64] = ss[0:32]
    sh[0:32] = ss[32:64]
    sh[96:128] = ss[64:96]
    sh[64:96] = ss[96:128]
    return np.ascontiguousarray(cos), np.ascontiguousarray(sh)


def prep(inputs, depth):
    f = lambda a: np.asarray(a, dtype=np.float32)
    x, c, ctx, c_ctx = f(inputs["x"]), f(inputs["c"]), f(inputs["ctx"]), f(inputs["c_ctx"])
    vec_layers = []
    for l in range(depth):
        cols = [f(inputs["b_ada"])[l].reshape(96, 128).T,
                f(inputs["norm1_g"])[l].reshape(16, 128).T,
                f(inputs["norm2_g"])[l].reshape(16, 128).T,
                f(inputs["q_norm_g"])[l].reshape(1, 128).T,
                f(inputs["k_norm_g"])[l].reshape(1, 128).T]
        cw = f(inputs["conv_w"])[l]
        for tap in range(3):
            cols.append(cw[tap].reshape(8, 128).T)
        cols.append(f(inputs["conv_b"])[l].reshape(8, 128).T)
        cols.append(f(inputs["attn_out_g"])[l].reshape(8, 128).T)
        cols.append(f(inputs["conv_out_g"])[l].reshape(8, 128).T)
        v = np.concatenate(cols, axis=1)
        assert v.shape == (128, NVEC)
        vec_layers.append(v)
    vecs = np.ascontiguousarray(np.concatenate(vec_layers, axis=1))
    shared = {
        "vecs": vecs,
        "w_ada": np.ascontiguousarray(f(inputs["w_ada"])[:depth]),
        "w_in": np.ascontiguousarray(f(inputs["w_in"])[:depth]),
        "w_out": np.ascontiguousarray(f(inputs["w_out"])[:depth]),
        "w1": np.ascontiguousarray(f(inputs["w_mlp_in"])[:depth]),
        "w2": np.ascontiguousarray(f(inputs["w_mlp_out"])[:depth]),
    }
    tabs = [rope_tables(0), rope_tables(1)]
    in_maps = []
    for r in range(8):
        b, h = r // 2, r % 2
        cc = np.stack([c[b], c_ctx], -1).reshape(16, 128, 2).transpose(1, 0, 2).reshape(128, 32)
        hmask = np.zeros((128, 2), np.float32)
        hmask[:, 0] = 1.0 if h == 1 else 0.0
        hmask[:, 1] = 1.0 if h == 0 else 0.0
        m = dict(shared)
        m["xT"] = np.ascontiguousarray(x[b, h * NLAT:(h + 1) * NLAT, :].T)
        m["cxT"] = np.ascontiguousarray(ctx[b].T)
        m["cc"] = np.ascontiguousarray(cc)
        m["ropec"] = tabs[h][0]
        m["ropes"] = tabs[h][1]
        m["hmask"] = hmask
        in_maps.append(m)
    return in_maps


_NC_CACHE = {}


def run(inputs, depth=DEPTH, stage=99):
    if depth not in _NC_CACHE:
        _NC_CACHE[depth] = build(depth, stage)
    nc = _NC_CACHE[depth]
    in_maps = prep(inputs, depth)
    res = run_bass_kernel_spmd(nc, in_maps, core_ids=list(range(8)))
    out = np.empty((4, 4096, D), np.float32)
    for r in range(8):
        b, h = r // 2, r % 2
        out[b, h * NLAT:(h + 1) * NLAT, :] = np.asarray(res.results[r]["outT"]).T
    return out


def kernel(**inputs):
    return run(inputs, DEPTH)
```

```python
import numpy as np
from contextlib import ExitStack
import concourse.bass as bass
import concourse.mybir as mybir
from concourse.bass_utils import run_bass_kernel_spmd

F32 = mybir.dt.float32
BF16 = mybir.dt.bfloat16
ALU = mybir.AluOpType
AF = mybir.ActivationFunctionType

D = 2048
KC = 16
NLAT = 2048
NCTX = 256
NTOK = NLAT + NCTX
DEPTH = 4
EPS = 1e-6
SCALE = 128.0 ** -0.5
NVEC = 178
V_BADA, V_N1G, V_N2G, V_QG, V_KG, V_CW, V_CB, V_AG, V_CG = 0, 96, 112, 128, 129, 130, 154, 162, 170
GROUPS = [(0, 512, False), (512, 512, False), (1024, 512, False), (1536, 512, False), (2048, 256, True)]
PAIRS = [[0, 1], [2, 3], [4, 5], [6, 7]]
NW2 = 4


class Tile:
    __slots__ = ("name", "w", "r")

    def __init__(self, name):
        self.name = name
        self.w = None
        self.r = {}


class Sem:
    __slots__ = ("h", "count")

    def __init__(self, h):
        self.h = h
        self.count = 0


class Queue:
    def __init__(self, name, kind):
        self.name = name
        self.kind = kind
        self.ops = []
        self.seen = {}
        self.sem = None
        self.dsems = []
        self.di = 0


class Tracker:
    def wait(self, q, tok):
        s, v = tok
        if q.seen.get(s, 0) >= v:
            return
        q.seen[s] = v
        q.ops.append(lambda e, s=s, v=v: e.wait_ge(s.h, v))

    def deps(self, q, reads, writes):
        toks = []
        for t in reads:
            if t.w is not None:
                toks.append(t.w)
        for t in writes:
            if t.w is not None:
                toks.append(t.w)
            toks.extend(t.r.items())
        for tok in toks:
            if q.kind == "pe" and tok[0] is q.sem:
                continue
            self.wait(q, tok)

    def finish(self, tok, reads, writes):
        s, v = tok
        for t in reads:
            if t.r.get(s, 0) < v:
                t.r[s] = v
        for t in writes:
            t.w = tok
            t.r = {}

    def op(self, q, f, reads=(), writes=(), mark=True):
        self.deps(q, reads, writes)
        s = q.sem
        if mark:
            s.count += 1
            v = s.count
            q.ops.append(lambda e, f=f, s=s: f(e).then_inc(s.h, 1))
        else:
            v = s.count + 1
            q.ops.append(lambda e, f=f: f(e))
        self.finish((s, v), reads, writes)

    def dma(self, q, f, reads=(), writes=()):
        self.deps(q, reads, writes)
        s = q.dsems[q.di]
        q.di = (q.di + 1) % len(q.dsems)
        if s.count > 0:
            self.wait(q, (s, s.count))
        s.count += 16
        v = s.count
        q.ops.append(lambda e, f=f, s=s: f(e).then_inc(s.h, 16))
        self.finish((s, v), reads, writes)


def build(depth=DEPTH, stage=99, fake_cc=False):
    nc = bass.Bass("TRN2", target_bir_lowering=False)
    K = Tracker()

    def din(name, shape, dt=F32):
        return nc.dram_tensor(name, shape, dt, kind="ExternalInput")

    xT = din("xT", [D, NLAT])
    cxT = din("cxT", [D, NCTX])
    cc_d = din("cc", [128, 32])
    ropec_d = din("ropec", [128, NLAT])
    ropes_d = din("ropes", [128, NLAT])
    hmask_d = din("hmask", [128, 2])
    vecs_d = din("vecs", [128, depth * NVEC])
    w_ada = din("w_ada", [depth, D, 6 * D])
    w_in = din("w_in", [depth, D, 4608])
    w_out = din("w_out", [depth, D, D])
    w1 = din("w1", [depth, D, 4 * D])
    w2 = din("w2", [depth, 4 * D, D])
    outT = nc.dram_tensor("outT", [D, NLAT], F32, kind="ExternalOutput")

    xres = nc.dram_tensor("xres", [D, NTOK], F32)
    q_d = nc.dram_tensor("q_d", [1024, NTOK], BF16)
    yg_d = nc.dram_tensor("yg_d", [1024, NTOK], BF16)
    kctx_d = nc.dram_tensor("kctx_d", [256, NCTX], BF16)
    vctx_d = nc.dram_tensor("vctx_d", [NCTX, 256], BF16)
    kT_in = [nc.dram_tensor(f"kT_in{l}", [256, NLAT], BF16) for l in range(depth)]
    v_in = [nc.dram_tensor(f"v_in{l}", [NLAT, 256], BF16) for l in range(depth)]
    halo_in = [nc.dram_tensor(f"halo_in{l}", [128, 16], F32) for l in range(depth)]
    kT_all = [nc.dram_tensor(f"kT_all{l}", [512, NLAT], BF16) for l in range(depth)]
    v_all = [nc.dram_tensor(f"v_all{l}", [2 * NLAT, 256], BF16) for l in range(depth)]
    halo_all = [nc.dram_tensor(f"halo_all{l}", [256, 16], F32) for l in range(depth)]

    xres_v = xres.ap().rearrange("(k p) t -> p k t", p=128)
    outT_v = outT.ap().rearrange("(k p) t -> p k t", p=128)
    xT_v = xT.ap().rearrange("(k p) t -> p k t", p=128)
    cxT_v = cxT.ap().rearrange("(k p) t -> p k t", p=128)
    q_d_v = q_d.ap().rearrange("(h p) t -> p h t", p=128)
    yg_d_v = yg_d.ap().rearrange("(h p) t -> p h t", p=128)
    vctx_v = vctx_d.ap().rearrange("(t p) c -> p t c", p=128)
    w_in_v = [w_in[l].rearrange("(k p) c -> p k c", p=128) for l in range(depth)]
    w_out_v = [w_out[l].rearrange("(k p) c -> p k c", p=128) for l in range(depth)]
    w1_v = [w1[l].rearrange("(k p) c -> p k c", p=128) for l in range(depth)]
    w2_v = [w2[l].rearrange("(k p) c -> p k c", p=128) for l in range(depth)]
    w_ada_v = [w_ada[l].rearrange("(k p) c -> p k c", p=128) for l in range(depth)]
    v_all_v = [v_all[l].ap().rearrange("(t p) c -> p t c", p=128) for l in range(depth)]

    with ExitStack() as es:
        def sb(name, shape, dt):
            return es.enter_context(nc.sbuf_tensor(name, shape, dt))

        def newsem(name):
            return Sem(es.enter_context(nc.semaphore(name)))

        qP = Queue("pe", "pe")
        qA = Queue("act", "act")
        qV = Queue("dve", "dve")
        qG = Queue("pool", "pool")
        qS = Queue("sp", "sp")
        for q in (qP, qA, qV, qG, qS):
            q.sem = newsem("s_" + q.name)
        qS.dsems = [newsem(f"ds{i}") for i in range(16)]
        qG.dsems = [newsem(f"dg{i}") for i in range(8)]

        class Buf:
            def __init__(self, ap, t):
                self.ap = ap
                self.t = t

        def mk(name, shape, dt):
            t = sb(name, shape, dt)
            return Buf(t[:], Tile(name))

        ones_d = mk("ones_d", [128, 128], BF16)
        ones_h = mk("ones_h", [128, 128], BF16)
        ones_g = mk("ones_g", [128, 128], BF16)
        ones_1 = mk("ones_1", [128, 128], BF16)
        eps_t = mk("eps_t", [128, 1], F32)
        vecs = mk("vecs_sb", [128, depth * NVEC], F32)
        ccs = mk("ccs", [128, 32], F32)
        mod = mk("mod", [128, 192], F32)
        G1 = mk("G1", [128, 32], F32)
        G2 = mk("G2", [128, 32], F32)
        rga = mk("rga", [128, 8], F32)
        rgc = mk("rgc", [128, 8], F32)
        hm = mk("hm", [128, 2], F32)
        zf = mk("zf", [128, 32], F32)
        zl = mk("zl", [128, 32], F32)
        bfb = mk("bfb", [128, 32], F32)
        blb = mk("blb", [128, 32], F32)
        yfb = mk("yfb", [128, 32], F32)
        ylb = mk("ylb", [128, 32], F32)
        yfix = mk("yfix", [128, 64], BF16)
        hl = mk("hl", [128, 8], F32)
        hr = mk("hr", [128, 8], F32)
        t8 = [mk(f"t8_{i}", [128, 8], F32) for i in range(2)]
        wbuf = [mk(f"wbuf{i}", [128, 16 * 256], BF16) for i in range(3)]
        w2buf = [mk(f"w2buf{i}", [128, 32 * 128], BF16) for i in range(NW2)]
        pT = [mk(f"pT{i}", [128, 512], BF16) for i in range(3)]
        sqb = [mk(f"sqb{i}", [128, 512], BF16) for i in range(2)]
        st16 = [mk(f"st16_{i}", [128, 512], BF16) for i in range(3)]
        vst = [mk(f"vst{i}", [128, 256], BF16) for i in range(2)]
        rstd = [mk(f"rstd{i}", [128, 512], F32) for i in range(3)]
        tmpm = [mk(f"tmpm{i}", [128, 512], F32) for i in range(2)]
        qn = [mk(f"qn{i}", [128, 512], F32) for i in range(2)]
        t1 = [mk(f"t1_{i}", [128, 512], F32) for i in range(2)]
        t2 = [mk(f"t2_{i}", [128, 512], F32) for i in range(2)]
        cs = [mk(f"cs{i}", [128, 512], F32) for i in range(2)]
        bs = [mk(f"bs{i}", [128, 512], F32) for i in range(2)]
        zb = [mk(f"zb{i}", [128, 516], F32) for i in range(2)]
        cv = [mk(f"cv{i}", [128, 512], F32) for i in range(1)]
        yv = [mk(f"yv{i}", [128, 512], F32) for i in range(1)]
        rec, ta, tb, rl, xc, ost = qn, t1, t2, cs, bs, zb

        arenaX = sb("arenaX", [128, 32 * 1024], BF16)
        arenaY = sb("arenaY", [128, 12800], BF16)
        xg3 = arenaX[:, 0:16384].bitcast(F32).rearrange("p (k n) -> p k n", k=16)
        xg_t = [Tile(f"xg{k}") for k in range(16)]
        X32_t = Tile("X32")
        X40_t = Tile("X40")
        X48_t = Tile("X48")
        cos_ap = arenaX[:, 16384:20480].bitcast(F32)
        sin_ap = arenaX[:, 20480:24576].bitcast(F32)
        og3 = arenaX[:, 16384:20480].rearrange("p (h n) -> p h n", h=8)
        yg3 = arenaX[:, 20480:24576].rearrange("p (h n) -> p h n", h=8)
        hT3 = arenaX[:, 24576:32768].rearrange("p (k n) -> p k n", k=16)
        aT3 = arenaX[:, 0:32768].rearrange("p (j n) -> p j n", j=64)

        def aT_tile(j):
            if j < 32:
                return xg_t[j // 2]
            if j < 40:
                return X32_t
            if j < 48:
                return X40_t
            return X48_t

        yk_t = Tile("yk")
        kh = arenaY[:, 0:4352]
        vh3 = arenaY[:, 4352:8704].rearrange("p (t d) -> p t d", t=34)
        h23 = arenaY[:, 0:8192].rearrange("p (k n) -> p k n", k=16)
        qg3 = arenaY[:, 8704:12800].rearrange("p (h n) -> p h n", h=8)
        qg_t = Tile("qg")

        ps = []
        for i in range(8):
            t = es.enter_context(nc.psum_tensor(f"ps{i}", [128, 512], F32))
            ps.append(Buf(t[:], Tile(f"ps{i}")))

        xres_t = [[Tile(f"xres{g}_{j}") for j in range(16)] for g in range(5)]
        q_d_t = [[Tile(f"qd{g}_{h}") for h in range(8)] for g in range(5)]
        yg_d_t = [[Tile(f"ygd{g}_{i}") for i in range(8)] for g in range(5)]
        kctx_t = [Tile(f"kctx{h}") for h in range(2)]
        vctx_t = [Tile(f"vctx{t}") for t in range(2)]
        kT_in_t = [[[Tile(f"kTin{l}_{g}_{h}") for h in range(2)] for g in range(4)] for l in range(depth)]
        v_in_t = [[Tile(f"vin{l}_{t}") for t in range(16)] for l in range(depth)]
        halo_in_t = [Tile(f"haloin{l}") for l in range(depth)]
        kT_all_t = [Tile(f"kTall{l}") for l in range(depth)]
        v_all_t = [Tile(f"vall{l}") for l in range(depth)]
        halo_all_t = [Tile(f"haloall{l}") for l in range(depth)]
        out_t = [[Tile(f"out{g}_{j}") for j in range(16)] for g in range(4)]

        def ACT(out, in_, func, reads, writes, bias=None, scale=None):
            kw = {}
            if bias is not None:
                kw["bias"] = bias
            if scale is not None:
                kw["scale"] = scale
            K.op(qA, lambda e: e.activation(out=out, in_=in_, func=func, **kw), reads, writes)

        def TT(out, in0, in1, op, reads, writes):
            K.op(qV, lambda e: e.tensor_tensor(out=out, in0=in0, in1=in1, op=op), reads, writes)

        def TS(out, in0, s1, op0, reads, writes):
            K.op(qV, lambda e: e.tensor_scalar(out=out, in0=in0, scalar1=s1, scalar2=None, op0=op0), reads, writes)

        def STT(out, in0, scalar, in1, op0, op1, reads, writes):
            K.op(qV, lambda e: e.scalar_tensor_tensor(out=out, in0=in0, scalar=scalar, in1=in1, op0=op0, op1=op1),
                 reads, writes)

        def VCOPY(out, in_, reads, writes):
            K.op(qV, lambda e: e.tensor_copy(out=out, in_=in_), reads, writes)

        def RECIP(out, in_, reads, writes):
            K.op(qV, lambda e: e.reciprocal(out=out, in_=in_), reads, writes)

        def MEMSET(ap, val, writes):
            K.op(qV, lambda e: e.memset(ap, val), (), writes)

        def MM(out, lhsT, rhs, start, stop, reads, writes, mark=None):
            K.op(qP, lambda e: e.matmul(out, lhsT=lhsT, rhs=rhs, start=start, stop=stop), reads, writes,
                 mark=(stop if mark is None else mark))

        def DMA(q, out, in_, reads, writes):
            K.dma(q, lambda e: e.dma_start(out=out, in_=in_), reads, writes)

        ccount = [0]

        def ALLGATHER(in_ap, out_ap, reads, writes, in_h=None, out_h=None):
            if fake_cc:
                n0 = in_h.shape[0]
                DMA(qG, out_h[0:n0, :], in_h[:, :], reads, writes)
                return
            K.deps(qG, reads, writes)
            s = newsem(f"cc{ccount[0]}")
            ccount[0] += 1
            s.count = 1
            qG.ops.append(lambda e: e.collective_compute(
                "AllGather", ALU.bypass, replica_groups=PAIRS, ins=[in_ap], outs=[out_ap]).then_inc(s.h, 1))
            K.finish((s, 1), reads, writes)
            K.wait(qG, (s, 1))

        def vc(l, off, n):
            return vecs.ap[:, l * NVEC + off:l * NVEC + off + n]

        mod3 = mod.ap.rearrange("p (j s) -> p j s", s=2)
        G1v = G1.ap.rearrange("p (k s) -> p k s", s=2)
        G2v = G2.ap.rearrange("p (k s) -> p k s", s=2)

        rs_ctr = [0]

        def RSTD(psb, N):
            r = rstd[rs_ctr[0] % 3]
            rs_ctr[0] += 1
            ACT(r.ap[:, :N], psb.ap[:, :N], AF.Sqrt, [psb.t, eps_t.t], [r.t], bias=eps_t.ap[:, 0:1], scale=1.0)
            RECIP(r.ap[:, :N], r.ap[:, :N], [r.t], [r.t])
            return r

        wctr = [0]

        def load_w(src3):
            slot = wbuf[wctr[0] % 3]
            wctr[0] += 1
            DMA(qG, slot.ap.rearrange("p (k c) -> p k c", k=16), src3, [], [slot.t])
            return slot

        w2ctr = [0]

        def w2slot():
            s = w2buf[w2ctr[0] % NW2]
            w2ctr[0] += 1
            return s

        MEMSET(ones_d.ap, 1.0 / 2048.0, [ones_d.t])
        MEMSET(ones_h.ap, 1.0 / 128.0, [ones_h.t])
        MEMSET(ones_g.ap, 1.0 / 1024.0, [ones_g.t])
        MEMSET(ones_1.ap, 1.0, [ones_1.t])
        MEMSET(eps_t.ap, EPS, [eps_t.t])
        DMA(qS, vecs.ap, vecs_d.ap(), [], [vecs.t])
        DMA(qS, ccs.ap, cc_d.ap(), [], [ccs.t])
        DMA(qS, hm.ap, hmask_d.ap(), [], [hm.t])
        ACT(ccs.ap, ccs.ap, AF.Silu, [ccs.t], [ccs.t])
        for g, (s0, N, ctx) in enumerate(GROUPS):
            for j in range(16):
                src = cxT_v[:, j, :] if ctx else xT_v[:, j, s0:s0 + N]
                DMA(qS, xres_v[:, j, s0:s0 + N], src, [], [xres_t[g][j]])

        def adaln(l):
            pm = ps[7]
            for j in range(96):
                slot = w2slot()
                wa = slot.ap[:, 0:4096].bitcast(F32).rearrange("p (k c) -> p k c", k=16)
                DMA(qS, wa, w_ada_v[l][:, :, j * 128:(j + 1) * 128], [], [slot.t])
                for k in range(16):
                    MM(pm.ap[:, 2 * j:2 * j + 2], wa[:, k, :], ccs.ap[:, 2 * k:2 * k + 2], k == 0, k == 15,
                       [slot.t, ccs.t], [pm.t])
            pm3 = pm.ap[:, 0:192].rearrange("p (j s) -> p j s", s=2)
            for s in range(2):
                TT(mod3[:, :, s], pm3[:, :, s], vc(l, V_BADA, 96), ALU.add, [pm.t, vecs.t], [mod.t])
            for s in range(2):
                STT(G1v[:, :, s], mod3[:, 16:32, s], 1.0, vc(l, V_N1G, 16), ALU.add, ALU.mult, [mod.t, vecs.t], [G1.t])
                STT(G2v[:, :, s], mod3[:, 64:80, s], 1.0, vc(l, V_N2G, 16), ALU.add, ALU.mult, [mod.t, vecs.t], [G2.t])
            RECIP(rga.ap, vc(l, V_AG, 8), [vecs.t], [rga.t])
            RECIP(rgc.ap, vc(l, V_CG, 8), [vecs.t], [rgc.t])

        chunks = [("q", h, h * 128) for h in range(8)] + [("k", h, 1024 + h * 128) for h in range(2)]
        for i in range(8):
            chunks += [("b", i, 1536 + i * 128), ("c", i, 2560 + i * 128), ("u", i, 3584 + i * 128)]

        pairsA = [("q", (0, 1), 0), ("q", (2, 3), 256), ("q", (4, 5), 512), ("q", (6, 7), 768), ("k", (0, 1), 1024)]
        for ip in range(4):
            pairsA += [("b", (2 * ip, 2 * ip + 1), 1536 + ip * 256), ("c", (2 * ip, 2 * ip + 1), 2560 + ip * 256),
                       ("u", (2 * ip, 2 * ip + 1), 3584 + ip * 256)]
        mctr = [0]
        nctr = [0]
        sctr = [0]
        qctr = [0]

        def chunk_epilogue(l, g, kind, idx, pm, deferred, cur_b, cur_c):
            s0, N, ctx = GROUPS[g]
            if kind in ("q", "k"):
                sq = sqb[sctr[0] % 2]
                sctr[0] += 1
                ACT(sq.ap[:, :N], pm.ap[:, :N], AF.Square, [pm.t], [sq.t])

                def stage2(kind=kind, idx=idx, pm=pm, sq=sq):
                    nb = ps[6 + nctr[0] % 2]
                    nctr[0] += 1
                    MM(nb.ap[:, :N], ones_h.ap, sq.ap[:, :N], True, True, [sq.t, ones_h.t], [nb.t])
                    r = RSTD(nb, N)
                    qq = qn[qctr[0] % 2]
                    tt1 = t1[qctr[0] % 2]
                    tt2 = t2[qctr[0] % 2]
                    so = st16[qctr[0] % 3]
                    qctr[0] += 1
                    gq = vc(l, V_QG if kind == "q" else V_KG, 1)
                    STT(qq.ap[:, :N], pm.ap[:, :N], gq, r.ap[:, :N], ALU.mult, ALU.mult, [pm.t, vecs.t, r.t], [qq.t])
                    if ctx:
                        ACT(so.ap[:, :N], qq.ap[:, :N], AF.Copy, [qq.t], [so.t])
                    else:
                        TT(tt1.ap[:, :N], qq.ap[:, :N], cos_ap[:, s0:s0 + N], ALU.mult, [qq.t, X32_t], [tt1.t])
                        for (a, b) in ((0, 32), (32, 0), (64, 96), (96, 64)):
                            TT(tt2.ap[a:a + 32, :N], qq.ap[b:b + 32, :N], sin_ap[b:b + 32, s0:s0 + N], ALU.mult,
                               [qq.t, X40_t], [tt2.t])
                        TT(so.ap[:, :N], tt1.ap[:, :N], tt2.ap[:, :N], ALU.add, [tt1.t, tt2.t], [so.t])
                    if kind == "q":
                        DMA(qS, q_d[idx * 128:(idx + 1) * 128, s0:s0 + N], so.ap[:, :N], [so.t], [q_d_t[g][idx]])
                    elif ctx:
                        DMA(qS, kctx_d[idx * 128:(idx + 1) * 128, :], so.ap[:, :N], [so.t], [kctx_t[idx]])
                    else:
                        DMA(qS, kT_in[l][idx * 128:(idx + 1) * 128, s0:s0 + N], so.ap[:, :N], [so.t],
                            [kT_in_t[l][g][idx]])
                deferred.append(stage2)
            elif kind == "b":
                b_ = bs[idx % 2]
                ACT(b_.ap[:, :N], pm.ap[:, :N], AF.Copy, [pm.t], [b_.t])
                cur_b[idx] = b_
            elif kind == "c":
                c_ = cs[idx % 2]
                ACT(c_.ap[:, :N], pm.ap[:, :N], AF.Copy, [pm.t], [c_.t])
                cur_c[idx] = c_
            else:
                i = idx
                b_, c_ = cur_b[idx], cur_c[idx]
                z = zb[i % 2]
                MEMSET(z.ap[:, 0:1], 0.0, [z.t])
                MEMSET(z.ap[:, N + 1:N + 2], 0.0, [z.t])
                TT(z.ap[:, 1:N + 1], pm.ap[:, :N], c_.ap[:, :N], ALU.mult, [pm.t, c_.t], [z.t])
                c0 = cv[0]
                y0 = yv[0]
                TS(c0.ap[:, :N], z.ap[:, 0:N], vc(l, V_CW + i, 1), ALU.mult, [z.t, vecs.t], [c0.t])
                STT(c0.ap[:, :N], z.ap[:, 1:N + 1], vc(l, V_CW + 8 + i, 1), c0.ap[:, :N], ALU.mult, ALU.add,
                    [z.t, vecs.t, c0.t], [c0.t])
                STT(c0.ap[:, :N], z.ap[:, 2:N + 2], vc(l, V_CW + 16 + i, 1), c0.ap[:, :N], ALU.mult, ALU.add,
                    [z.t, vecs.t, c0.t], [c0.t])
                STT(y0.ap[:, :N], c0.ap[:, :N], vc(l, V_CB + i, 1), b_.ap[:, :N], ALU.add, ALU.mult,
                    [c0.t, vecs.t, b_.t], [y0.t])
                if not ctx:
                    c8 = g * 8 + i
                    ACT(zf.ap[:, c8:c8 + 1], z.ap[:, 1:2], AF.Copy, [z.t], [zf.t])
                    ACT(zl.ap[:, c8:c8 + 1], z.ap[:, N:N + 1], AF.Copy, [z.t], [zl.t])
                    ACT(bfb.ap[:, c8:c8 + 1], b_.ap[:, 0:1], AF.Copy, [b_.t], [bfb.t])
                    ACT(blb.ap[:, c8:c8 + 1], b_.ap[:, N - 1:N], AF.Copy, [b_.t], [blb.t])
                    ACT(yfb.ap[:, c8:c8 + 1], y0.ap[:, 0:1], AF.Copy, [y0.t], [yfb.t])
                    ACT(ylb.ap[:, c8:c8 + 1], y0.ap[:, N - 1:N], AF.Copy, [y0.t], [ylb.t])
                so = st16[qctr[0] % 3]
                qctr[0] += 1
                TS(so.ap[:, :N], y0.ap[:, :N], vc(l, V_CG + i, 1), ALU.mult, [y0.t, vecs.t], [so.t])
                DMA(qS, yg_d[i * 128:(i + 1) * 128, s0:s0 + N], so.ap[:, :N], [so.t], [yg_d_t[g][i]])

        def phaseA(l, g, wv3, wv_t):
            s0, N, ctx = GROUPS[g]
            s = 1 if ctx else 0
            for kq in range(4):
                DMA(qS, xg3[:, 4 * kq:4 * kq + 4, 0:N], xres_v[:, 4 * kq:4 * kq + 4, s0:s0 + N],
                    [xres_t[g][j] for j in range(4 * kq, 4 * kq + 4)], [xg_t[j] for j in range(4 * kq, 4 * kq + 4)])
            for k in range(16):
                sq = sqb[k % 2]
                ACT(sq.ap[:, :N], xg3[:, k, :N], AF.Square, [xg_t[k]], [sq.t])
                MM(ps[0].ap[:, :N], ones_d.ap, sq.ap[:, :N], k == 0, k == 15, [sq.t, ones_d.t], [ps[0].t], mark=True)
            rs = RSTD(ps[0], N)
            for k in range(16):
                tm = tmpm[k % 2]
                STT(tm.ap[:, :N], xg3[:, k, :N], G1v[:, k, s:s + 1], rs.ap[:, :N], ALU.mult, ALU.mult,
                    [xg_t[k], G1.t, rs.t], [tm.t])
                ACT(hT3[:, k, :N], tm.ap[:, :N], AF.Identity, [tm.t, mod.t], [X48_t], bias=mod3[:, k, s:s + 1], scale=1.0)
            for tt in range(N // 128):
                pb = ps[1 + tt % 2]
                for k in range(16):
                    MM(pb.ap[:, 0:256], hT3[:, k, tt * 128:(tt + 1) * 128], wv3[:, k, :], k == 0, k == 15,
                       [X48_t, wv_t], [pb.t])
                vs = vst[tt % 2]
                ACT(vs.ap, pb.ap[:, 0:256], AF.Copy, [pb.t], [vs.t])
                if ctx:
                    DMA(qS, vctx_d[tt * 128:(tt + 1) * 128, :], vs.ap, [vs.t], [vctx_t[tt]])
                else:
                    ti = s0 // 128 + tt
                    DMA(qS, v_in[l][ti * 128:(ti + 1) * 128, :], vs.ap, [vs.t], [v_in_t[l][ti]])

            npair = len(pairsA)
            slots = {}
            for pi in range(min(2, npair)):
                slots[pi] = load_w(w_in_v[l][:, :, pairsA[pi][2]:pairsA[pi][2] + 256])
            deferred = []
            cur_b = {}
            cur_c = {}
            for pi, (kind, idxs, colp) in enumerate(pairsA):
                slot = slots.pop(pi)
                if pi + 2 < npair:
                    c2 = pairsA[pi + 2][2]
                    slots[pi + 2] = load_w(w_in_v[l][:, :, c2:c2 + 256])
                w3p = slot.ap.rearrange("p (k c) -> p k c", k=16)
                for cc in range(2):
                    idx = idxs[cc]
                    pm = ps[3 + mctr[0] % 3]
                    mctr[0] += 1
                    for k in range(16):
                        MM(pm.ap[:, :N], w3p[:, k, cc * 128:(cc + 1) * 128], hT3[:, k, :N], k == 0, k == 15,
                           [slot.t, X48_t], [pm.t])
                    for f in deferred:
                        f()
                    deferred = []
                    chunk_epilogue(l, g, kind, idx, pm, deferred, cur_b, cur_c)
            for f in deferred:
                f()

        def exchange(l):
            DMA(qS, halo_in[l][:, 0:8], zf.ap[:, 0:8], [zf.t], [halo_in_t[l]])
            DMA(qS, halo_in[l][:, 8:16], zl.ap[:, 24:32], [zl.t], [halo_in_t[l]])
            ALLGATHER(kT_in[l].ap().opt(), kT_all[l].ap().opt(),
                      [t for gg in kT_in_t[l] for t in gg], [kT_all_t[l]], kT_in[l], kT_all[l])
            ALLGATHER(v_in[l].ap().opt(), v_all[l].ap().opt(), v_in_t[l], [v_all_t[l]], v_in[l], v_all[l])
            ALLGATHER(halo_in[l].ap().opt(), halo_all[l].ap().opt(), [halo_in_t[l]], [halo_all_t[l]], halo_in[l], halo_all[l])
            DMA(qS, hl.ap, halo_all[l][0:128, 8:16], [halo_all_t[l]], [hl.t])
            DMA(qS, hr.ap, halo_all[l][128:256, 0:8], [halo_all_t[l]], [hr.t])
            TS(hl.ap, hl.ap, hm.ap[:, 0:1], ALU.mult, [hl.t, hm.t], [hl.t])
            TS(hr.ap, hr.ap, hm.ap[:, 1:2], ALU.mult, [hr.t, hm.t], [hr.t])
            yfix4 = yfix.ap.rearrange("p (g e i) -> p g e i", g=4, e=2)
            for g in range(4):
                g8 = slice(g * 8, g * 8 + 8)
                left = hl.ap if g == 0 else zl.ap[:, (g - 1) * 8:g * 8]
                lt = hl.t if g == 0 else zl.t
                a = t8[0]
                TT(a.ap, bfb.ap[:, g8], vc(l, V_CW, 8), ALU.mult, [bfb.t, vecs.t], [a.t])
                TT(a.ap, a.ap, left, ALU.mult, [a.t, lt], [a.t])
                TT(a.ap, a.ap, yfb.ap[:, g8], ALU.add, [a.t, yfb.t], [a.t])
                TT(yfix4[:, g, 0, :], a.ap, vc(l, V_CG, 8), ALU.mult, [a.t, vecs.t], [yfix.t])
                right = hr.ap if g == 3 else zf.ap[:, (g + 1) * 8:(g + 2) * 8]
                rt = hr.t if g == 3 else zf.t
                b = t8[1]
                TT(b.ap, blb.ap[:, g8], vc(l, V_CW + 16, 8), ALU.mult, [blb.t, vecs.t], [b.t])
                TT(b.ap, b.ap, right, ALU.mult, [b.t, rt], [b.t])
                TT(b.ap, b.ap, ylb.ap[:, g8], ALU.add, [b.t, ylb.t], [b.t])
                TT(yfix4[:, g, 1, :], b.ap, vc(l, V_CG, 8), ALU.mult, [b.t, vecs.t], [yfix.t])
            return yfix4

        uctr = [0]
        tctr = [0]
        pctr = [0]

        def attention(l, g):
            s0, N, ctx = GROUPS[g]
            DMA(qS, qg3[:, :, :N], q_d_v[:, :, s0:s0 + N], q_d_t[g], [qg_t])
            kts = [32, 33] if ctx else list(range(34))
            n = len(kts)
            for hd in range(2):
                if not ctx:
                    DMA(qS, kh[:, 0:2048], kT_all[l][hd * 128:(hd + 1) * 128, :], [kT_all_t[l]], [yk_t])
                    DMA(qS, kh[:, 2048:4096], kT_all[l][256 + hd * 128:256 + (hd + 1) * 128, :], [kT_all_t[l]], [yk_t])
                    DMA(qS, vh3[:, 0:32, :], v_all_v[l][:, :, hd * 128:(hd + 1) * 128], [v_all_t[l]], [yk_t])
                DMA(qS, kh[:, 4096:4352], kctx_d[hd * 128:(hd + 1) * 128, :], [kctx_t[hd]], [yk_t])
                DMA(qS, vh3[:, 32:34, :], vctx_v[:, :, hd * 128:(hd + 1) * 128], vctx_t, [yk_t])
                for r in range(4):
                    head = hd * 4 + r
                    ob = ps[3 + uctr[0] % 2]
                    db = ps[5 + uctr[0] % 2]
                    uctr[0] += 1
                    pts = {}

                    def S(i):
                        kt = kts[i]
                        tb_ = ps[tctr[0] % 3]
                        tctr[0] += 1
                        MM(tb_.ap[:, :N], kh[:, kt * 128:(kt + 1) * 128], qg3[:, head, :N], True, True,
                           [yk_t, qg_t], [tb_.t])
                        p = pT[pctr[0] % 3]
                        pctr[0] += 1
                        ACT(p.ap[:, :N], tb_.ap[:, :N], AF.Exp, [tb_.t], [p.t], scale=SCALE)
                        pts[i] = p

                    S(0)
                    if n > 1:
                        S(1)
                    for i in range(n):
                        if i + 2 < n:
                            S(i + 2)
                        p = pts.pop(i)
                        kt = kts[i]
                        MM(ob.ap[:, :N], vh3[:, kt, :], p.ap[:, :N], i == 0, i == n - 1, [yk_t, p.t], [ob.t])
                        MM(db.ap[:, :N], ones_1.ap, p.ap[:, :N], i == 0, i == n - 1, [ones_1.t, p.t], [db.t], mark=True)
                    rc = rec[head % 2]
                    RECIP(rc.ap[:, :N], db.ap[:, :N], [db.t], [rc.t])
                    STT(og3[:, head, :N], ob.ap[:, :N], vc(l, V_AG + head, 1), rc.ap[:, :N], ALU.mult, ALU.mult,
                        [ob.t, vecs.t, rc.t], [X32_t])

        def phaseB(l, g, yfix4, last):
            s0, N, ctx = GROUPS[g]
            s = 1 if ctx else 0
            attention(l, g)
            DMA(qS, yg3[:, :, :N], yg_d_v[:, :, s0:s0 + N], yg_d_t[g], [X40_t])
            if not ctx:
                VCOPY(yg3[:, :, 0], yfix4[:, g, 0, :], [yfix.t], [X40_t])
                VCOPY(yg3[:, :, N - 1], yfix4[:, g, 1, :], [yfix.t], [X40_t])
            for kq in range(4):
                DMA(qS, xg3[:, 4 * kq:4 * kq + 4, 0:N], xres_v[:, 4 * kq:4 * kq + 4, s0:s0 + N],
                    [xres_t[g][j] for j in range(4 * kq, 4 * kq + 4)], [xg_t[j] for j in range(4 * kq, 4 * kq + 4)])
            for i in range(8):
                sq = sqb[i % 2]
                ACT(sq.ap[:, :N], og3[:, i, :N], AF.Square, [X32_t, rga.t], [sq.t], scale=rga.ap[:, i:i + 1])
                MM(ps[6].ap[:, :N], ones_g.ap, sq.ap[:, :N], i == 0, i == 7, [sq.t, ones_g.t], [ps[6].t], mark=True)
            for i in range(8):
                sq = sqb[i % 2]
                ACT(sq.ap[:, :N], yg3[:, i, :N], AF.Square, [X40_t, rgc.t], [sq.t], scale=rgc.ap[:, i:i + 1])
                MM(ps[7].ap[:, :N], ones_g.ap, sq.ap[:, :N], i == 0, i == 7, [sq.t, ones_g.t], [ps[7].t], mark=True)
            rs_a = RSTD(ps[6], N)
            rs_c = RSTD(ps[7], N)
            slots = {}
            for jp in range(2):
                slots[jp] = load_w(w_out_v[l][:, :, jp * 256:(jp + 1) * 256])
            dfr = []
            for j in range(16):
                jp, cc = j // 2, j % 2
                if cc == 0:
                    slot = slots.pop(jp)
                    if jp + 2 < 8:
                        slots[jp + 2] = load_w(w_out_v[l][:, :, (jp + 2) * 256:(jp + 3) * 256])
                    w3p = slot.ap.rearrange("p (k c) -> p k c", k=16)
                w3 = w3p[:, :, cc * 128:(cc + 1) * 128]
                pa = ps[0 + j % 2]
                pc = ps[2 + j % 2]
                for k in range(8):
                    MM(pa.ap[:, :N], w3[:, k, :], og3[:, k, :N], k == 0, k == 7, [slot.t, X32_t], [pa.t])
                for k in range(8):
                    MM(pc.ap[:, :N], w3[:, 8 + k, :], yg3[:, k, :N], k == 0, k == 7, [slot.t, X40_t], [pc.t])
                for f in dfr:
                    f()
                dfr = []
                a = ta[j % 2]
                b = tb[j % 2]
                TT(a.ap[:, :N], pa.ap[:, :N], rs_a.ap[:, :N], ALU.mult, [pa.t, rs_a.t], [a.t])
                TT(b.ap[:, :N], pc.ap[:, :N], rs_c.ap[:, :N], ALU.mult, [pc.t, rs_c.t], [b.t])
                TT(a.ap[:, :N], a.ap[:, :N], b.ap[:, :N], ALU.add, [a.t, b.t], [a.t])
                STT(xg3[:, j, :N], a.ap[:, :N], mod3[:, 32 + j, s:s + 1], xg3[:, j, :N], ALU.mult, ALU.add,
                    [a.t, mod.t, xg_t[j]], [xg_t[j]])
                sq = sqb[j % 2]
                ACT(sq.ap[:, :N], xg3[:, j, :N], AF.Square, [xg_t[j]], [sq.t])
                def st2(j=j, sq=sq):
                    MM(ps[4].ap[:, :N], ones_d.ap, sq.ap[:, :N], j == 0, j == 15, [sq.t, ones_d.t], [ps[4].t], mark=True)
                dfr.append(st2)
                DMA(qS, xres_v[:, j, s0:s0 + N], xg3[:, j, :N], [xg_t[j]], [xres_t[g][j]])
            for f in dfr:
                f()
            rs2 = RSTD(ps[4], N)
            for k in range(16):
                tm = tmpm[k % 2]
                STT(tm.ap[:, :N], xg3[:, k, :N], G2v[:, k, s:s + 1], rs2.ap[:, :N], ALU.mult, ALU.mult,
                    [xg_t[k], G2.t, rs2.t], [tm.t])
                ACT(h23[:, k, :N], tm.ap[:, :N], AF.Identity, [tm.t, mod.t], [yk_t], bias=mod3[:, 48 + k, s:s + 1],
                    scale=1.0)
            slots = {}
            for jp in range(2):
                slots[jp] = load_w(w1_v[l][:, :, jp * 256:(jp + 1) * 256])
            for j in range(64):
                jp, cc = j // 2, j % 2
                if cc == 0:
                    slot = slots.pop(jp)
                    if jp + 2 < 32:
                        slots[jp + 2] = load_w(w1_v[l][:, :, (jp + 2) * 256:(jp + 3) * 256])
                    w3p = slot.ap.rearrange("p (k c) -> p k c", k=16)
                w3 = w3p[:, :, cc * 128:(cc + 1) * 128]
                pb = ps[5 + j % 3]
                for k in range(16):
                    MM(pb.ap[:, :N], w3[:, k, :], h23[:, k, :N], k == 0, k == 15, [slot.t, yk_t], [pb.t])
                r_ = rl[j % 2]
                ACT(r_.ap[:, :N], pb.ap[:, :N], AF.Relu, [pb.t], [r_.t])
                TT(aT3[:, j, :N], r_.ap[:, :N], r_.ap[:, :N], ALU.mult, [r_.t], [aT_tile(j)])
            def load_w2(jp, t):
                sl = w2slot()
                DMA(qG, sl.ap.rearrange("p (k c) -> p k c", k=16),
                    w2_v[l][:, t * 16:(t + 1) * 16, jp * 256:(jp + 1) * 256], [], [sl.t])
                return sl
            tiles = [(jp, t) for jp in range(8) for t in range(4)]
            pend = {}
            for ti in range(min(3, len(tiles))):
                pend[ti] = load_w2(*tiles[ti])
            for ti, (jp, t) in enumerate(tiles):
                sl = pend.pop(ti)
                if ti + 3 < len(tiles):
                    pend[ti + 3] = load_w2(*tiles[ti + 3])
                w3p = sl.ap.rearrange("p (k c) -> p k c", k=16)
                pbs = [ps[(jp % 2) * 2 + cc] for cc in range(2)]
                for kk in range(16):
                    kg = t * 16 + kk
                    for cc in range(2):
                        MM(pbs[cc].ap[:, :N], w3p[:, kk, cc * 128:(cc + 1) * 128], aT3[:, kg, :N],
                           kg == 0, kg == 63, [sl.t, aT_tile(kg)], [pbs[cc].t])
                if t == 3:
                    for cc in range(2):
                        j = 2 * jp + cc
                        pb = pbs[cc]
                        x_ = xc[j % 2]
                        DMA(qS, x_.ap[:, :N], xres_v[:, j, s0:s0 + N], [xres_t[g][j]], [x_.t])
                        o_ = ost[j % 2]
                        STT(o_.ap[:, :N], pb.ap[:, :N], mod3[:, 80 + j, s:s + 1], x_.ap[:, :N], ALU.mult, ALU.add,
                            [pb.t, mod.t, x_.t], [o_.t])
                        if last:
                            DMA(qS, outT_v[:, j, s0:s0 + N], o_.ap[:, :N], [o_.t], [out_t[g][j]])
                        else:
                            DMA(qS, xres_v[:, j, s0:s0 + N], o_.ap[:, :N], [o_.t], [xres_t[g][j]])

        def program():
            for l in range(depth):
                last = (l == depth - 1)
                if stage == 0:
                    return
                adaln(l)
                if stage == 1:
                    return
                DMA(qS, cos_ap, ropec_d.ap(), [], [X32_t])
                DMA(qS, sin_ap, ropes_d.ap(), [], [X40_t])
                wvs = w2slot()
                wv3 = wvs.ap.rearrange("p (k c) -> p k c", k=16)
                DMA(qG, wv3, w_in_v[l][:, :, 1280:1536], [], [wvs.t])
                for g in range(5):
                    phaseA(l, g, wv3, wvs.t)
                    if stage == 2:
                        return
                if stage == 3:
                    return
                yfix4 = exchange(l)
                if stage == 4:
                    return
                for g in range(5):
                    if last and GROUPS[g][2]:
                        continue
                    phaseB(l, g, yfix4, last)
                    if stage == 5:
                        return
        program()
        for g in range(4):
            for j in range(16):
                if out_t[g][j].w is not None:
                    K.wait(qS, out_t[g][j].w)
        for q in (qP, qA, qV, qG):
            if q.sem.count > 0:
                K.wait(qS, (q.sem, q.sem.count))
        for s_ in qS.dsems + qG.dsems:
            if s_.count > 0:
                K.wait(qS, (s_, s_.count))

        block = es.enter_context(nc.Block())

        @block.tensor
        def _(e):
            for f in qP.ops:
                f(e)

        @block.scalar
        def _(e):
            for f in qA.ops:
                f(e)

        @block.vector
        def _(e):
            for f in qV.ops:
                f(e)

        @block.gpsimd
        def _(e):
            for f in qG.ops:
                f(e)

        @block.sync
        def _(e):
            for f in qS.ops:
                f(e)
    counts = {q.name: len(q.ops) for q in (qP, qA, qV, qG, qS)}
    print("instruction counts", counts, flush=True)
    return nc


def rope_tables(h):
    t = np.arange(h * NLAT, (h + 1) * NLAT)
    row = (t // 64).astype(np.float32)
    col = (t % 64).astype(np.float32)
    inv = np.float32(10000.0) ** (-(np.arange(0, 64, 2, dtype=np.float32)) / np.float32(64))
    ang_r = row[:, None] * inv[None, :]
    ang_c = col[:, None] * inv[None, :]
    ang_r = np.concatenate([ang_r, ang_r], -1)
    ang_c = np.concatenate([ang_c, ang_c], -1)
    ang = np.concatenate([ang_r, ang_c], -1).astype(np.float32)
    cos = np.cos(ang).astype(np.float32).T
    sin = np.sin(ang).astype(np.float32).T
    sgn = np.ones((128, 1), np.float32)
    sgn[0:32] = -1.0
    sgn[64:96] = -1.0
    ss = sin * sgn
    sh = np.empty_like(ss)
    sh[32:64] = ss[0:32]
    sh[0:32] = ss[32:64]
    sh[96:128] = ss[64:96]
    sh[64:96] = ss[96:128]
    return np.ascontiguousarray(cos), np.ascontiguousarray(sh)


def prep(inputs, depth):
    f = lambda a: np.asarray(a, dtype=np.float32)
    x, c, ctx, c_ctx = f(inputs["x"]), f(inputs["c"]), f(inputs["ctx"]), f(inputs["c_ctx"])
    vec_layers = []
    for l in range(depth):
        cols = [f(inputs["b_ada"])[l].reshape(96, 128).T,
                f(inputs["norm1_g"])[l].reshape(16, 128).T,
                f(inputs["norm2_g"])[l].reshape(16, 128).T,
                f(inputs["q_norm_g"])[l].reshape(1, 128).T,
                f(inputs["k_norm_g"])[l].reshape(1, 128).T]
        cw = f(inputs["conv_w"])[l]
        for tap in range(3):
            cols.append(cw[tap].reshape(8, 128).T)
        cols.append(f(inputs["conv_b"])[l].reshape(8, 128).T)
        cols.append(f(inputs["attn_out_g"])[l].reshape(8, 128).T)
        cols.append(f(inputs["conv_out_g"])[l].reshape(8, 128).T)
        v = np.concatenate(cols, axis=1)
        assert v.shape == (128, NVEC)
        vec_layers.append(v)
    vecs = np.ascontiguousarray(np.concatenate(vec_layers, axis=1))
    shared = {
        "vecs": vecs,
        "w_ada": np.ascontiguousarray(f(inputs["w_ada"])[:depth]),
        "w_in": np.ascontiguousarray(f(inputs["w_in"])[:depth]),
        "w_out": np.ascontiguousarray(f(inputs["w_out"])[:depth]),
        "w1": np.ascontiguousarray(f(inputs["w_mlp_in"])[:depth]),
        "w2": np.ascontiguousarray(f(inputs["w_mlp_out"])[:depth]),
    }
    tabs = [rope_tables(0), rope_tables(1)]
    in_maps = []
    for r in range(8):
        b, h = r // 2, r % 2
        cc = np.stack([c[b], c_ctx], -1).reshape(16, 128, 2).transpose(1, 0, 2).reshape(128, 32)
        hmask = np.zeros((128, 2), np.float32)
        hmask[:, 0] = 1.0 if h == 1 else 0.0
        hmask[:, 1] = 1.0 if h == 0 else 0.0
        m = dict(shared)
        m["xT"] = np.ascontiguousarray(x[b, h * NLAT:(h + 1) * NLAT, :].T)
        m["cxT"] = np.ascontiguousarray(ctx[b].T)
        m["cc"] = np.ascontiguousarray(cc)
        m["ropec"] = tabs[h][0]
        m["ropes"] = tabs[h][1]
        m["hmask"] = hmask
        in_maps.append(m)
    return in_maps


_NC_CACHE = {}


def run(inputs, depth=DEPTH, stage=99):
    if depth not in _NC_CACHE:
        _NC_CACHE[depth] = build(depth, stage)
    nc = _NC_CACHE[depth]
    in_maps = prep(inputs, depth)
    res = run_bass_kernel_spmd(nc, in_maps, core_ids=list(range(8)))
    out = np.empty((4, 4096, D), np.float32)
    for r in range(8):
        b, h = r // 2, r % 2
        out[b, h * NLAT:(h + 1) * NLAT, :] = np.asarray(res.results[r]["outT"]).T
    return out


def kernel(**inputs):
    return run(inputs, DEPTH)
```

```python
import numpy as np
from contextlib import ExitStack
import concourse.bass as bass
import concourse.mybir as mybir
from concourse.bass_utils import run_bass_kernel_spmd

F32 = mybir.dt.float32
BF16 = mybir.dt.bfloat16
ALU = mybir.AluOpType
AF = mybir.ActivationFunctionType

D = 2048
KC = 16
NLAT = 2048
NCTX = 256
NTOK = NLAT + NCTX
DEPTH = 4
EPS = 1e-6
SCALE = 128.0 ** -0.5
NVEC = 178
V_BADA, V_N1G, V_N2G, V_QG, V_KG, V_CW, V_CB, V_AG, V_CG = 0, 96, 112, 128, 129, 130, 154, 162, 170
GROUPS = [(0, 512, False), (512, 512, False), (1024, 512, False), (1536, 512, False), (2048, 256, True)]
PAIRS = [[0, 1], [2, 3], [4, 5], [6, 7]]
NW2 = 4


class Tile:
    __slots__ = ("name", "w", "r")

    def __init__(self, name):
        self.name = name
        self.w = None
        self.r = {}


class Sem:
    __slots__ = ("h", "count")

    def __init__(self, h):
        self.h = h
        self.count = 0


class Queue:
    def __init__(self, name, kind):
        self.name = name
        self.kind = kind
        self.ops = []
        self.seen = {}
        self.sem = None
        self.dsems = []
        self.di = 0


class Tracker:
    def wait(self, q, tok):
        s, v = tok
        if q.seen.get(s, 0) >= v:
            return
        q.seen[s] = v
        q.ops.append(lambda e, s=s, v=v: e.wait_ge(s.h, v))

    def deps(self, q, reads, writes):
        toks = []
        for t in reads:
            if t.w is not None:
                toks.append(t.w)
        for t in writes:
            if t.w is not None:
                toks.append(t.w)
            toks.extend(t.r.items())
        for tok in toks:
            if q.kind == "pe" and tok[0] is q.sem:
                continue
            self.wait(q, tok)

    def finish(self, tok, reads, writes):
        s, v = tok
        for t in reads:
            if t.r.get(s, 0) < v:
                t.r[s] = v
        for t in writes:
            t.w = tok
            t.r = {}

    def op(self, q, f, reads=(), writes=(), mark=True):
        self.deps(q, reads, writes)
        s = q.sem
        if mark:
            s.count += 1
            v = s.count
            q.ops.append(lambda e, f=f, s=s: f(e).then_inc(s.h, 1))
        else:
            v = s.count + 1
            q.ops.append(lambda e, f=f: f(e))
        self.finish((s, v), reads, writes)

    def dma(self, q, f, reads=(), writes=()):
        self.deps(q, reads, writes)
        s = q.dsems[q.di]
        q.di = (q.di + 1) % len(q.dsems)
        if s.count > 0:
            self.wait(q, (s, s.count))
        s.count += 16
        v = s.count
        q.ops.append(lambda e, f=f, s=s: f(e).then_inc(s.h, 16))
        self.finish((s, v), reads, writes)


def build(depth=DEPTH, stage=99, fake_cc=False):
    nc = bass.Bass("TRN2", target_bir_lowering=False)
    K = Tracker()

    def din(name, shape, dt=F32):
        return nc.dram_tensor(name, shape, dt, kind="ExternalInput")

    xT = din("xT", [D, NLAT])
    cxT = din("cxT", [D, NCTX])
    cc_d = din("cc", [128, 32])
    ropec_d = din("ropec", [128, NLAT])
    ropes_d = din("ropes", [128, NLAT])
    hmask_d = din("hmask", [128, 2])
    vecs_d = din("vecs", [128, depth * NVEC])
    w_ada = din("w_ada", [depth, D, 6144])
    w_in = din("w_in", [depth, D, 4608])
    w_out = din("w_out", [depth, D, D])
    w1 = din("w1", [depth, D, 4 * D])
    w2 = din("w2", [depth, 4 * D, D])
    outT = nc.dram_tensor("outT", [D, NLAT], F32, kind="ExternalOutput")

    xres = nc.dram_tensor("xres", [D, NTOK], F32)
    q_d = nc.dram_tensor("q_d", [1024, NTOK], BF16)
    yg_d = nc.dram_tensor("yg_d", [1024, NTOK], BF16)
    kctx_d = nc.dram_tensor("kctx_d", [256, NCTX], BF16)
    vctx_d = nc.dram_tensor("vctx_d", [NCTX, 256], BF16)
    kT_in = [nc.dram_tensor(f"kT_in{l}", [256, NLAT], BF16) for l in range(depth)]
    v_in = [nc.dram_tensor(f"v_in{l}", [NLAT, 256], BF16) for l in range(depth)]
    halo_in = [nc.dram_tensor(f"halo_in{l}", [128, 16], F32) for l in range(depth)]
    kT_all = [nc.dram_tensor(f"kT_all{l}", [512, NLAT], BF16) for l in range(depth)]
    v_all = [nc.dram_tensor(f"v_all{l}", [2 * NLAT, 256], BF16) for l in range(depth)]
    halo_all = [nc.dram_tensor(f"halo_all{l}", [256, 16], F32) for l in range(depth)]
    ada_in = nc.dram_tensor("ada_in", [128, depth * 96], F32)
    ada_all = nc.dram_tensor("ada_all", [256, depth * 96], F32)

    xres_v = xres.ap().rearrange("(k p) t -> p k t", p=128)
    outT_v = outT.ap().rearrange("(k p) t -> p k t", p=128)
    xT_v = xT.ap().rearrange("(k p) t -> p k t", p=128)
    cxT_v = cxT.ap().rearrange("(k p) t -> p k t", p=128)
    q_d_v = q_d.ap().rearrange("(h p) t -> p h t", p=128)
    yg_d_v = yg_d.ap().rearrange("(h p) t -> p h t", p=128)
    vctx_v = vctx_d.ap().rearrange("(t p) c -> p t c", p=128)
    w_in_v = [w_in[l].rearrange("(k p) c -> p k c", p=128) for l in range(depth)]
    w_out_v = [w_out[l].rearrange("(k p) c -> p k c", p=128) for l in range(depth)]
    w1_v = [w1[l].rearrange("(k p) c -> p k c", p=128) for l in range(depth)]
    w2_v = [w2[l].rearrange("(k p) c -> p k c", p=128) for l in range(depth)]
    w_ada_v = [w_ada[l].rearrange("(k p) c -> p k c", p=128) for l in range(depth)]
    v_all_v = [v_all[l].ap().rearrange("(t p) c -> p t c", p=128) for l in range(depth)]

    with ExitStack() as es:
        def sb(name, shape, dt):
            return es.enter_context(nc.sbuf_tensor(name, shape, dt))

        def newsem(name):
            return Sem(es.enter_context(nc.semaphore(name)))

        qP = Queue("pe", "pe")
        qA = Queue("act", "act")
        qV = Queue("dve", "dve")
        qG = Queue("pool", "pool")
        qS = Queue("sp", "sp")
        for q in (qP, qA, qV, qG, qS):
            q.sem = newsem("s_" + q.name)
        qS.dsems = [newsem(f"ds{i}") for i in range(16)]
        qG.dsems = [newsem(f"dg{i}") for i in range(8)]

        class Buf:
            def __init__(self, ap, t):
                self.ap = ap
                self.t = t

        def mk(name, shape, dt):
            t = sb(name, shape, dt)
            return Buf(t[:], Tile(name))

        ones_d = mk("ones_d", [128, 128], BF16)
        ones_h = mk("ones_h", [128, 128], BF16)
        ones_g = mk("ones_g", [128, 128], BF16)
        ones_1 = mk("ones_1", [128, 128], BF16)
        eps_t = mk("eps_t", [128, 1], F32)
        vecs = mk("vecs_sb", [128, depth * NVEC], F32)
        ccs = mk("ccs", [128, 32], F32)
        adap = mk("adap", [128, depth * 96], F32)
        modall = mk("modall", [128, 2 * depth * 96], F32)
        mod = mk("mod", [128, 192], F32)
        G1 = mk("G1", [128, 32], F32)
        G2 = mk("G2", [128, 32], F32)
        rga = mk("rga", [128, 8], F32)
        rgc = mk("rgc", [128, 8], F32)
        hm = mk("hm", [128, 2], F32)
        zf = mk("zf", [128, 32], F32)
        zl = mk("zl", [128, 32], F32)
        bfb = mk("bfb", [128, 32], F32)
        blb = mk("blb", [128, 32], F32)
        yfb = mk("yfb", [128, 32], F32)
        ylb = mk("ylb", [128, 32], F32)
        yfix = mk("yfix", [128, 64], BF16)
        hl = mk("hl", [128, 8], F32)
        hr = mk("hr", [128, 8], F32)
        t8 = [mk(f"t8_{i}", [128, 8], F32) for i in range(2)]
        wbuf = [mk(f"wbuf{i}", [128, 16 * 256], BF16) for i in range(3)]
        w2buf = [mk(f"w2buf{i}", [128, 32 * 128], BF16) for i in range(NW2)]
        pT = [mk(f"pT{i}", [128, 512], BF16) for i in range(3)]
        sqb = [mk(f"sqb{i}", [128, 512], BF16) for i in range(2)]
        st16 = [mk(f"st16_{i}", [128, 512], BF16) for i in range(3)]
        vst = [mk(f"vst{i}", [128, 256], BF16) for i in range(2)]
        rstd = [mk(f"rstd{i}", [128, 512], F32) for i in range(3)]
        tmpm = [mk(f"tmpm{i}", [128, 512], F32) for i in range(2)]
        qn = [mk(f"qn{i}", [128, 512], F32) for i in range(2)]
        t1 = [mk(f"t1_{i}", [128, 512], F32) for i in range(2)]
        t2 = [mk(f"t2_{i}", [128, 512], F32) for i in range(2)]
        cs = [mk(f"cs{i}", [128, 512], F32) for i in range(2)]
        bs = [mk(f"bs{i}", [128, 512], F32) for i in range(2)]
        zb = [mk(f"zb{i}", [128, 516], F32) for i in range(2)]
        cv = [mk(f"cv{i}", [128, 512], F32) for i in range(1)]
        yv = [mk(f"yv{i}", [128, 512], F32) for i in range(1)]
        rec, ta, tb, rl, xc, ost = qn, t1, t2, cs, bs, zb

        arenaX = sb("arenaX", [128, 32 * 1024], BF16)
        arenaY = sb("arenaY", [128, 12800], BF16)
        xg3 = arenaX[:, 0:16384].bitcast(F32).rearrange("p (k n) -> p k n", k=16)
        xg_t = [Tile(f"xg{k}") for k in range(16)]
        X32_t = Tile("X32")
        X40_t = Tile("X40")
        X48_t = Tile("X48")
        cos_ap = arenaX[:, 16384:20480].bitcast(F32)
        sin_ap = arenaX[:, 20480:24576].bitcast(F32)
        og3 = arenaX[:, 16384:20480].rearrange("p (h n) -> p h n", h=8)
        yg3 = arenaX[:, 20480:24576].rearrange("p (h n) -> p h n", h=8)
        hT3 = arenaX[:, 24576:32768].rearrange("p (k n) -> p k n", k=16)
        aT3 = arenaX[:, 0:32768].rearrange("p (j n) -> p j n", j=64)

        def aT_tile(j):
            if j < 32:
                return xg_t[j // 2]
            if j < 40:
                return X32_t
            if j < 48:
                return X40_t
            return X48_t

        yk_t = Tile("yk")
        kh = arenaY[:, 0:4352]
        vh3 = arenaY[:, 4352:8704].rearrange("p (t d) -> p t d", t=34)
        h23 = arenaY[:, 0:8192].rearrange("p (k n) -> p k n", k=16)
        qg3 = arenaY[:, 8704:12800].rearrange("p (h n) -> p h n", h=8)
        qg_t = Tile("qg")

        ps = []
        for i in range(8):
            t = es.enter_context(nc.psum_tensor(f"ps{i}", [128, 512], F32))
            ps.append(Buf(t[:], Tile(f"ps{i}")))

        xres_t = [[Tile(f"xres{g}_{j}") for j in range(16)] for g in range(5)]
        q_d_t = [[Tile(f"qd{g}_{h}") for h in range(8)] for g in range(5)]
        yg_d_t = [[Tile(f"ygd{g}_{i}") for i in range(8)] for g in range(5)]
        kctx_t = [Tile(f"kctx{h}") for h in range(2)]
        vctx_t = [Tile(f"vctx{t}") for t in range(2)]
        kT_in_t = [[[Tile(f"kTin{l}_{g}_{h}") for h in range(2)] for g in range(4)] for l in range(depth)]
        v_in_t = [[Tile(f"vin{l}_{t}") for t in range(16)] for l in range(depth)]
        halo_in_t = [Tile(f"haloin{l}") for l in range(depth)]
        kT_all_t = [Tile(f"kTall{l}") for l in range(depth)]
        v_all_t = [Tile(f"vall{l}") for l in range(depth)]
        halo_all_t = [Tile(f"haloall{l}") for l in range(depth)]
        ada_in_t = Tile("ada_in")
        ada_all_t = Tile("ada_all")
        out_t = [[Tile(f"out{g}_{j}") for j in range(16)] for g in range(4)]

        def ACT(out, in_, func, reads, writes, bias=None, scale=None):
            kw = {}
            if bias is not None:
                kw["bias"] = bias
            if scale is not None:
                kw["scale"] = scale
            K.op(qA, lambda e: e.activation(out=out, in_=in_, func=func, **kw), reads, writes)

        def TT(out, in0, in1, op, reads, writes):
            K.op(qV, lambda e: e.tensor_tensor(out=out, in0=in0, in1=in1, op=op), reads, writes)

        def TS(out, in0, s1, op0, reads, writes):
            K.op(qV, lambda e: e.tensor_scalar(out=out, in0=in0, scalar1=s1, scalar2=None, op0=op0), reads, writes)

        def STT(out, in0, scalar, in1, op0, op1, reads, writes):
            K.op(qV, lambda e: e.scalar_tensor_tensor(out=out, in0=in0, scalar=scalar, in1=in1, op0=op0, op1=op1),
                 reads, writes)

        def VCOPY(out, in_, reads, writes):
            K.op(qV, lambda e: e.tensor_copy(out=out, in_=in_), reads, writes)

        def RECIP(out, in_, reads, writes):
            K.op(qV, lambda e: e.reciprocal(out=out, in_=in_), reads, writes)

        def MEMSET(ap, val, writes):
            K.op(qV, lambda e: e.memset(ap, val), (), writes)

        def MM(out, lhsT, rhs, start, stop, reads, writes, mark=None):
            K.op(qP, lambda e: e.matmul(out, lhsT=lhsT, rhs=rhs, start=start, stop=stop), reads, writes,
                 mark=(stop if mark is None else mark))

        def DMA(q, out, in_, reads, writes):
            K.dma(q, lambda e: e.dma_start(out=out, in_=in_), reads, writes)

        ccount = [0]

        def ALLGATHER(in_ap, out_ap, reads, writes, in_h=None, out_h=None):
            if fake_cc:
                n0 = in_h.shape[0]
                DMA(qG, out_h[0:n0, :], in_h[:, :], reads, writes)
                return
            K.deps(qG, reads, writes)
            s = newsem(f"cc{ccount[0]}")
            ccount[0] += 1
            s.count = 1
            qG.ops.append(lambda e: e.collective_compute(
                "AllGather", ALU.bypass, replica_groups=PAIRS, ins=[in_ap], outs=[out_ap]).then_inc(s.h, 1))
            K.finish((s, 1), reads, writes)
            K.wait(qG, (s, 1))

        def vc(l, off, n):
            return vecs.ap[:, l * NVEC + off:l * NVEC + off + n]

        mod3 = mod.ap.rearrange("p (j s) -> p j s", s=2)
        G1v = G1.ap.rearrange("p (k s) -> p k s", s=2)
        G2v = G2.ap.rearrange("p (k s) -> p k s", s=2)

        rs_ctr = [0]

        def RSTD(psb, N):
            r = rstd[rs_ctr[0] % 3]
            rs_ctr[0] += 1
            ACT(r.ap[:, :N], psb.ap[:, :N], AF.Sqrt, [psb.t, eps_t.t], [r.t], bias=eps_t.ap[:, 0:1], scale=1.0)
            RECIP(r.ap[:, :N], r.ap[:, :N], [r.t], [r.t])
            return r

        wctr = [0]

        def load_w(src3):
            slot = wbuf[wctr[0] % 3]
            wctr[0] += 1
            DMA(qG, slot.ap.rearrange("p (k c) -> p k c", k=16), src3, [], [slot.t])
            return slot

        w2ctr = [0]

        def w2slot():
            s = w2buf[w2ctr[0] % NW2]
            w2ctr[0] += 1
            return s

        MEMSET(ones_d.ap, 1.0 / 2048.0, [ones_d.t])
        MEMSET(ones_h.ap, 1.0 / 128.0, [ones_h.t])
        MEMSET(ones_g.ap, 1.0 / 1024.0, [ones_g.t])
        MEMSET(ones_1.ap, 1.0, [ones_1.t])
        MEMSET(eps_t.ap, EPS, [eps_t.t])
        DMA(qS, vecs.ap, vecs_d.ap(), [], [vecs.t])
        DMA(qS, ccs.ap, cc_d.ap(), [], [ccs.t])
        DMA(qS, hm.ap, hmask_d.ap(), [], [hm.t])
        ACT(ccs.ap, ccs.ap, AF.Silu, [ccs.t], [ccs.t])
        for g, (s0, N, ctx) in enumerate(GROUPS):
            for j in range(16):
                src = cxT_v[:, j, :] if ctx else xT_v[:, j, s0:s0 + N]
                DMA(qS, xres_v[:, j, s0:s0 + N], src, [], [xres_t[g][j]])

        def adaln_all():
            pm = ps[7]
            for l in range(depth):
                for jj in range(48):
                    slot = w2slot()
                    wa = slot.ap[:, 0:4096].bitcast(F32).rearrange("p (k c) -> p k c", k=16)
                    DMA(qS, wa, w_ada_v[l][:, :, jj * 128:(jj + 1) * 128], [], [slot.t])
                    c0 = (l * 48 + jj) * 2
                    for k in range(16):
                        MM(pm.ap[:, c0:c0 + 2], wa[:, k, :], ccs.ap[:, 2 * k:2 * k + 2], k == 0, k == 15,
                           [slot.t, ccs.t], [pm.t])
            VCOPY(adap.ap, pm.ap[:, 0:depth * 96], [pm.t], [adap.t])
            DMA(qS, ada_in[:, :], adap.ap, [adap.t], [ada_in_t])
            ALLGATHER(ada_in.ap().opt(), ada_all.ap().opt(), [ada_in_t], [ada_all_t], ada_in, ada_all)
            DMA(qS, modall.ap.rearrange("p (r c) -> p r c", r=2), ada_all.ap().rearrange("(r p) c -> p r c", p=128),
                [ada_all_t], [modall.t])

        def adaln(l):
            m4 = modall.ap.rearrange("p (r c) -> p r c", r=2)[:, :, l * 96:(l + 1) * 96].rearrange(
                "p r (jj s) -> p r jj s", s=2)
            o4 = mod.ap.rearrange("p (r jj s) -> p r jj s", r=2, jj=48)
            bada = vc(l, V_BADA, 96).rearrange("p (r jj) -> p r jj", r=2)
            for s in range(2):
                TT(o4[:, :, :, s], m4[:, :, :, s], bada, ALU.add, [modall.t, vecs.t], [mod.t])
            for s in range(2):
                STT(G1v[:, :, s], mod3[:, 16:32, s], 1.0, vc(l, V_N1G, 16), ALU.add, ALU.mult, [mod.t, vecs.t], [G1.t])
                STT(G2v[:, :, s], mod3[:, 64:80, s], 1.0, vc(l, V_N2G, 16), ALU.add, ALU.mult, [mod.t, vecs.t], [G2.t])
            RECIP(rga.ap, vc(l, V_AG, 8), [vecs.t], [rga.t])
            RECIP(rgc.ap, vc(l, V_CG, 8), [vecs.t], [rgc.t])

        chunks = [("q", h, h * 128) for h in range(8)] + [("k", h, 1024 + h * 128) for h in range(2)]
        for i in range(8):
            chunks += [("b", i, 1536 + i * 128), ("c", i, 2560 + i * 128), ("u", i, 3584 + i * 128)]

        pairsA = [("q", (0, 1), 0), ("q", (2, 3), 256), ("q", (4, 5), 512), ("q", (6, 7), 768), ("k", (0, 1), 1024)]
        for ip in range(4):
            pairsA += [("b", (2 * ip, 2 * ip + 1), 1536 + ip * 256), ("c", (2 * ip, 2 * ip + 1), 2560 + ip * 256),
                       ("u", (2 * ip, 2 * ip + 1), 3584 + ip * 256)]
        mctr = [0]
        nctr = [0]
        sctr = [0]
        qctr = [0]

        def chunk_epilogue(l, g, kind, idx, pm, deferred, cur_b, cur_c):
            s0, N, ctx = GROUPS[g]
            if kind in ("q", "k"):
                sq = sqb[sctr[0] % 2]
                sctr[0] += 1
                ACT(sq.ap[:, :N], pm.ap[:, :N], AF.Square, [pm.t], [sq.t])

                def stage2(kind=kind, idx=idx, pm=pm, sq=sq):
                    nb = ps[6 + nctr[0] % 2]
                    nctr[0] += 1
                    MM(nb.ap[:, :N], ones_h.ap, sq.ap[:, :N], True, True, [sq.t, ones_h.t], [nb.t])
                    r = RSTD(nb, N)
                    qq = qn[qctr[0] % 2]
                    tt1 = t1[qctr[0] % 2]
                    tt2 = t2[qctr[0] % 2]
                    so = st16[qctr[0] % 3]
                    qctr[0] += 1
                    gq = vc(l, V_QG if kind == "q" else V_KG, 1)
                    STT(qq.ap[:, :N], pm.ap[:, :N], gq, r.ap[:, :N], ALU.mult, ALU.mult, [pm.t, vecs.t, r.t], [qq.t])
                    if ctx:
                        ACT(so.ap[:, :N], qq.ap[:, :N], AF.Copy, [qq.t], [so.t])
                    else:
                        TT(tt1.ap[:, :N], qq.ap[:, :N], cos_ap[:, s0:s0 + N], ALU.mult, [qq.t, X32_t], [tt1.t])
                        for (a, b) in ((0, 32), (32, 0), (64, 96), (96, 64)):
                            TT(tt2.ap[a:a + 32, :N], qq.ap[b:b + 32, :N], sin_ap[b:b + 32, s0:s0 + N], ALU.mult,
                               [qq.t, X40_t], [tt2.t])
                        TT(so.ap[:, :N], tt1.ap[:, :N], tt2.ap[:, :N], ALU.add, [tt1.t, tt2.t], [so.t])
                    if kind == "q":
                        DMA(qS, q_d[idx * 128:(idx + 1) * 128, s0:s0 + N], so.ap[:, :N], [so.t], [q_d_t[g][idx]])
                    elif ctx:
                        DMA(qS, kctx_d[idx * 128:(idx + 1) * 128, :], so.ap[:, :N], [so.t], [kctx_t[idx]])
                    else:
                        DMA(qS, kT_in[l][idx * 128:(idx + 1) * 128, s0:s0 + N], so.ap[:, :N], [so.t],
                            [kT_in_t[l][g][idx]])
                deferred.append(stage2)
            elif kind == "b":
                b_ = bs[idx % 2]
                ACT(b_.ap[:, :N], pm.ap[:, :N], AF.Copy, [pm.t], [b_.t])
                cur_b[idx] = b_
            elif kind == "c":
                c_ = cs[idx % 2]
                ACT(c_.ap[:, :N], pm.ap[:, :N], AF.Copy, [pm.t], [c_.t])
                cur_c[idx] = c_
            else:
                i = idx
                b_, c_ = cur_b[idx], cur_c[idx]
                z = zb[i % 2]
                MEMSET(z.ap[:, 0:1], 0.0, [z.t])
                MEMSET(z.ap[:, N + 1:N + 2], 0.0, [z.t])
                TT(z.ap[:, 1:N + 1], pm.ap[:, :N], c_.ap[:, :N], ALU.mult, [pm.t, c_.t], [z.t])
                c0 = cv[0]
                y0 = yv[0]
                TS(c0.ap[:, :N], z.ap[:, 0:N], vc(l, V_CW + i, 1), ALU.mult, [z.t, vecs.t], [c0.t])
                STT(c0.ap[:, :N], z.ap[:, 1:N + 1], vc(l, V_CW + 8 + i, 1), c0.ap[:, :N], ALU.mult, ALU.add,
                    [z.t, vecs.t, c0.t], [c0.t])
                STT(c0.ap[:, :N], z.ap[:, 2:N + 2], vc(l, V_CW + 16 + i, 1), c0.ap[:, :N], ALU.mult, ALU.add,
                    [z.t, vecs.t, c0.t], [c0.t])
                STT(y0.ap[:, :N], c0.ap[:, :N], vc(l, V_CB + i, 1), b_.ap[:, :N], ALU.add, ALU.mult,
                    [c0.t, vecs.t, b_.t], [y0.t])
                if not ctx:
                    c8 = g * 8 + i
                    ACT(zf.ap[:, c8:c8 + 1], z.ap[:, 1:2], AF.Copy, [z.t], [zf.t])
                    ACT(zl.ap[:, c8:c8 + 1], z.ap[:, N:N + 1], AF.Copy, [z.t], [zl.t])
                    ACT(bfb.ap[:, c8:c8 + 1], b_.ap[:, 0:1], AF.Copy, [b_.t], [bfb.t])
                    ACT(blb.ap[:, c8:c8 + 1], b_.ap[:, N - 1:N], AF.Copy, [b_.t], [blb.t])
                    ACT(yfb.ap[:, c8:c8 + 1], y0.ap[:, 0:1], AF.Copy, [y0.t], [yfb.t])
                    ACT(ylb.ap[:, c8:c8 + 1], y0.ap[:, N - 1:N], AF.Copy, [y0.t], [ylb.t])
                so = st16[qctr[0] % 3]
                qctr[0] += 1
                TS(so.ap[:, :N], y0.ap[:, :N], vc(l, V_CG + i, 1), ALU.mult, [y0.t, vecs.t], [so.t])
                DMA(qS, yg_d[i * 128:(i + 1) * 128, s0:s0 + N], so.ap[:, :N], [so.t], [yg_d_t[g][i]])

        def phaseA(l, g, wv3, wv_t):
            s0, N, ctx = GROUPS[g]
            s = 1 if ctx else 0
            for kq in range(4):
                DMA(qS, xg3[:, 4 * kq:4 * kq + 4, 0:N], xres_v[:, 4 * kq:4 * kq + 4, s0:s0 + N],
                    [xres_t[g][j] for j in range(4 * kq, 4 * kq + 4)], [xg_t[j] for j in range(4 * kq, 4 * kq + 4)])
            for k in range(16):
                sq = sqb[k % 2]
                ACT(sq.ap[:, :N], xg3[:, k, :N], AF.Square, [xg_t[k]], [sq.t])
                MM(ps[0].ap[:, :N], ones_d.ap, sq.ap[:, :N], k == 0, k == 15, [sq.t, ones_d.t], [ps[0].t], mark=True)
            rs = RSTD(ps[0], N)
            for k in range(16):
                tm = tmpm[k % 2]
                STT(tm.ap[:, :N], xg3[:, k, :N], G1v[:, k, s:s + 1], rs.ap[:, :N], ALU.mult, ALU.mult,
                    [xg_t[k], G1.t, rs.t], [tm.t])
                ACT(hT3[:, k, :N], tm.ap[:, :N], AF.Identity, [tm.t, mod.t], [X48_t], bias=mod3[:, k, s:s + 1], scale=1.0)
            for tt in range(N // 128):
                pb = ps[1 + tt % 2]
                for k in range(16):
                    MM(pb.ap[:, 0:256], hT3[:, k, tt * 128:(tt + 1) * 128], wv3[:, k, :], k == 0, k == 15,
                       [X48_t, wv_t], [pb.t])
                vs = vst[tt % 2]
                ACT(vs.ap, pb.ap[:, 0:256], AF.Copy, [pb.t], [vs.t])
                if ctx:
                    DMA(qS, vctx_d[tt * 128:(tt + 1) * 128, :], vs.ap, [vs.t], [vctx_t[tt]])
                else:
                    ti = s0 // 128 + tt
                    DMA(qS, v_in[l][ti * 128:(ti + 1) * 128, :], vs.ap, [vs.t], [v_in_t[l][ti]])

            npair = len(pairsA)
            slots = {}
            for pi in range(min(2, npair)):
                slots[pi] = load_w(w_in_v[l][:, :, pairsA[pi][2]:pairsA[pi][2] + 256])
            deferred = []
            cur_b = {}
            cur_c = {}
            for pi, (kind, idxs, colp) in enumerate(pairsA):
                slot = slots.pop(pi)
                if pi + 2 < npair:
                    c2 = pairsA[pi + 2][2]
                    slots[pi + 2] = load_w(w_in_v[l][:, :, c2:c2 + 256])
                w3p = slot.ap.rearrange("p (k c) -> p k c", k=16)
                for cc in range(2):
                    idx = idxs[cc]
                    pm = ps[3 + mctr[0] % 3]
                    mctr[0] += 1
                    for k in range(16):
                        MM(pm.ap[:, :N], w3p[:, k, cc * 128:(cc + 1) * 128], hT3[:, k, :N], k == 0, k == 15,
                           [slot.t, X48_t], [pm.t])
                    for f in deferred:
                        f()
                    deferred = []
                    chunk_epilogue(l, g, kind, idx, pm, deferred, cur_b, cur_c)
            for f in deferred:
                f()

        def exchange(l):
            DMA(qS, halo_in[l][:, 0:8], zf.ap[:, 0:8], [zf.t], [halo_in_t[l]])
            DMA(qS, halo_in[l][:, 8:16], zl.ap[:, 24:32], [zl.t], [halo_in_t[l]])
            ALLGATHER(kT_in[l].ap().opt(), kT_all[l].ap().opt(),
                      [t for gg in kT_in_t[l] for t in gg], [kT_all_t[l]], kT_in[l], kT_all[l])
            ALLGATHER(v_in[l].ap().opt(), v_all[l].ap().opt(), v_in_t[l], [v_all_t[l]], v_in[l], v_all[l])
            ALLGATHER(halo_in[l].ap().opt(), halo_all[l].ap().opt(), [halo_in_t[l]], [halo_all_t[l]], halo_in[l], halo_all[l])
            DMA(qS, hl.ap, halo_all[l][0:128, 8:16], [halo_all_t[l]], [hl.t])
            DMA(qS, hr.ap, halo_all[l][128:256, 0:8], [halo_all_t[l]], [hr.t])
            TS(hl.ap, hl.ap, hm.ap[:, 0:1], ALU.mult, [hl.t, hm.t], [hl.t])
            TS(hr.ap, hr.ap, hm.ap[:, 1:2], ALU.mult, [hr.t, hm.t], [hr.t])
            yfix4 = yfix.ap.rearrange("p (g e i) -> p g e i", g=4, e=2)
            for g in range(4):
                g8 = slice(g * 8, g * 8 + 8)
                left = hl.ap if g == 0 else zl.ap[:, (g - 1) * 8:g * 8]
                lt = hl.t if g == 0 else zl.t
                a = t8[0]
                TT(a.ap, bfb.ap[:, g8], vc(l, V_CW, 8), ALU.mult, [bfb.t, vecs.t], [a.t])
                TT(a.ap, a.ap, left, ALU.mult, [a.t, lt], [a.t])
                TT(a.ap, a.ap, yfb.ap[:, g8], ALU.add, [a.t, yfb.t], [a.t])
                TT(yfix4[:, g, 0, :], a.ap, vc(l, V_CG, 8), ALU.mult, [a.t, vecs.t], [yfix.t])
                right = hr.ap if g == 3 else zf.ap[:, (g + 1) * 8:(g + 2) * 8]
                rt = hr.t if g == 3 else zf.t
                b = t8[1]
                TT(b.ap, blb.ap[:, g8], vc(l, V_CW + 16, 8), ALU.mult, [blb.t, vecs.t], [b.t])
                TT(b.ap, b.ap, right, ALU.mult, [b.t, rt], [b.t])
                TT(b.ap, b.ap, ylb.ap[:, g8], ALU.add, [b.t, ylb.t], [b.t])
                TT(yfix4[:, g, 1, :], b.ap, vc(l, V_CG, 8), ALU.mult, [b.t, vecs.t], [yfix.t])
            return yfix4

        uctr = [0]
        tctr = [0]
        pctr = [0]

        def attention(l, g):
            s0, N, ctx = GROUPS[g]
            DMA(qS, qg3[:, :, :N], q_d_v[:, :, s0:s0 + N], q_d_t[g], [qg_t])
            kts = [32, 33] if ctx else list(range(34))
            n = len(kts)
            for hd in range(2):
                if not ctx:
                    DMA(qS, kh[:, 0:2048], kT_all[l][hd * 128:(hd + 1) * 128, :], [kT_all_t[l]], [yk_t])
                    DMA(qS, kh[:, 2048:4096], kT_all[l][256 + hd * 128:256 + (hd + 1) * 128, :], [kT_all_t[l]], [yk_t])
                    DMA(qS, vh3[:, 0:32, :], v_all_v[l][:, :, hd * 128:(hd + 1) * 128], [v_all_t[l]], [yk_t])
                DMA(qS, kh[:, 4096:4352], kctx_d[hd * 128:(hd + 1) * 128, :], [kctx_t[hd]], [yk_t])
                DMA(qS, vh3[:, 32:34, :], vctx_v[:, :, hd * 128:(hd + 1) * 128], vctx_t, [yk_t])
                for r in range(4):
                    head = hd * 4 + r
                    ob = ps[3 + uctr[0] % 2]
                    db = ps[5 + uctr[0] % 2]
                    uctr[0] += 1
                    pts = {}

                    def S(i):
                        kt = kts[i]
                        tb_ = ps[tctr[0] % 3]
                        tctr[0] += 1
                        MM(tb_.ap[:, :N], kh[:, kt * 128:(kt + 1) * 128], qg3[:, head, :N], True, True,
                           [yk_t, qg_t], [tb_.t])
                        p = pT[pctr[0] % 3]
                        pctr[0] += 1
                        ACT(p.ap[:, :N], tb_.ap[:, :N], AF.Exp, [tb_.t], [p.t], scale=SCALE)
                        pts[i] = p

                    S(0)
                    if n > 1:
                        S(1)
                    for i in range(n):
                        if i + 2 < n:
                            S(i + 2)
                        p = pts.pop(i)
                        kt = kts[i]
                        MM(ob.ap[:, :N], vh3[:, kt, :], p.ap[:, :N], i == 0, i == n - 1, [yk_t, p.t], [ob.t])
                        MM(db.ap[:, :N], ones_1.ap, p.ap[:, :N], i == 0, i == n - 1, [ones_1.t, p.t], [db.t], mark=True)
                    rc = rec[head % 2]
                    RECIP(rc.ap[:, :N], db.ap[:, :N], [db.t], [rc.t])
                    STT(og3[:, head, :N], ob.ap[:, :N], vc(l, V_AG + head, 1), rc.ap[:, :N], ALU.mult, ALU.mult,
                        [ob.t, vecs.t, rc.t], [X32_t])

        def phaseB(l, g, yfix4, last):
            s0, N, ctx = GROUPS[g]
            s = 1 if ctx else 0
            attention(l, g)
            DMA(qS, yg3[:, :, :N], yg_d_v[:, :, s0:s0 + N], yg_d_t[g], [X40_t])
            if not ctx:
                VCOPY(yg3[:, :, 0], yfix4[:, g, 0, :], [yfix.t], [X40_t])
                VCOPY(yg3[:, :, N - 1], yfix4[:, g, 1, :], [yfix.t], [X40_t])
            for kq in range(4):
                DMA(qS, xg3[:, 4 * kq:4 * kq + 4, 0:N], xres_v[:, 4 * kq:4 * kq + 4, s0:s0 + N],
                    [xres_t[g][j] for j in range(4 * kq, 4 * kq + 4)], [xg_t[j] for j in range(4 * kq, 4 * kq + 4)])
            for i in range(8):
                sq = sqb[i % 2]
                ACT(sq.ap[:, :N], og3[:, i, :N], AF.Square, [X32_t, rga.t], [sq.t], scale=rga.ap[:, i:i + 1])
                MM(ps[6].ap[:, :N], ones_g.ap, sq.ap[:, :N], i == 0, i == 7, [sq.t, ones_g.t], [ps[6].t], mark=True)
            for i in range(8):
                sq = sqb[i % 2]
                ACT(sq.ap[:, :N], yg3[:, i, :N], AF.Square, [X40_t, rgc.t], [sq.t], scale=rgc.ap[:, i:i + 1])
                MM(ps[7].ap[:, :N], ones_g.ap, sq.ap[:, :N], i == 0, i == 7, [sq.t, ones_g.t], [ps[7].t], mark=True)
            rs_a = RSTD(ps[6], N)
            rs_c = RSTD(ps[7], N)
            slots = {}
            for jp in range(2):
                slots[jp] = load_w(w_out_v[l][:, :, jp * 256:(jp + 1) * 256])
            dfr = []
            for j in range(16):
                jp, cc = j // 2, j % 2
                if cc == 0:
                    slot = slots.pop(jp)
                    if jp + 2 < 8:
                        slots[jp + 2] = load_w(w_out_v[l][:, :, (jp + 2) * 256:(jp + 3) * 256])
                    w3p = slot.ap.rearrange("p (k c) -> p k c", k=16)
                w3 = w3p[:, :, cc * 128:(cc + 1) * 128]
                pa = ps[0 + j % 2]
                pc = ps[2 + j % 2]
                for k in range(8):
                    MM(pa.ap[:, :N], w3[:, k, :], og3[:, k, :N], k == 0, k == 7, [slot.t, X32_t], [pa.t])
                for k in range(8):
                    MM(pc.ap[:, :N], w3[:, 8 + k, :], yg3[:, k, :N], k == 0, k == 7, [slot.t, X40_t], [pc.t])
                for f in dfr:
                    f()
                dfr = []
                a = ta[j % 2]
                b = tb[j % 2]
                TT(a.ap[:, :N], pa.ap[:, :N], rs_a.ap[:, :N], ALU.mult, [pa.t, rs_a.t], [a.t])
                TT(b.ap[:, :N], pc.ap[:, :N], rs_c.ap[:, :N], ALU.mult, [pc.t, rs_c.t], [b.t])
                TT(a.ap[:, :N], a.ap[:, :N], b.ap[:, :N], ALU.add, [a.t, b.t], [a.t])
                STT(xg3[:, j, :N], a.ap[:, :N], mod3[:, 32 + j, s:s + 1], xg3[:, j, :N], ALU.mult, ALU.add,
                    [a.t, mod.t, xg_t[j]], [xg_t[j]])
                sq = sqb[j % 2]
                ACT(sq.ap[:, :N], xg3[:, j, :N], AF.Square, [xg_t[j]], [sq.t])
                def st2(j=j, sq=sq):
                    MM(ps[4].ap[:, :N], ones_d.ap, sq.ap[:, :N], j == 0, j == 15, [sq.t, ones_d.t], [ps[4].t], mark=True)
                dfr.append(st2)
                DMA(qS, xres_v[:, j, s0:s0 + N], xg3[:, j, :N], [xg_t[j]], [xres_t[g][j]])
            for f in dfr:
                f()
            rs2 = RSTD(ps[4], N)
            for k in range(16):
                tm = tmpm[k % 2]
                STT(tm.ap[:, :N], xg3[:, k, :N], G2v[:, k, s:s + 1], rs2.ap[:, :N], ALU.mult, ALU.mult,
                    [xg_t[k], G2.t, rs2.t], [tm.t])
                ACT(h23[:, k, :N], tm.ap[:, :N], AF.Identity, [tm.t, mod.t], [yk_t], bias=mod3[:, 48 + k, s:s + 1],
                    scale=1.0)
            slots = {}
            for jp in range(2):
                slots[jp] = load_w(w1_v[l][:, :, jp * 256:(jp + 1) * 256])
            for j in range(64):
                jp, cc = j // 2, j % 2
                if cc == 0:
                    slot = slots.pop(jp)
                    if jp + 2 < 32:
                        slots[jp + 2] = load_w(w1_v[l][:, :, (jp + 2) * 256:(jp + 3) * 256])
                    w3p = slot.ap.rearrange("p (k c) -> p k c", k=16)
                w3 = w3p[:, :, cc * 128:(cc + 1) * 128]
                pb = ps[5 + j % 3]
                for k in range(16):
                    MM(pb.ap[:, :N], w3[:, k, :], h23[:, k, :N], k == 0, k == 15, [slot.t, yk_t], [pb.t])
                r_ = rl[j % 2]
                ACT(r_.ap[:, :N], pb.ap[:, :N], AF.Relu, [pb.t], [r_.t])
                TT(aT3[:, j, :N], r_.ap[:, :N], r_.ap[:, :N], ALU.mult, [r_.t], [aT_tile(j)])
            def load_w2(jp, t):
                sl = w2slot()
                DMA(qG, sl.ap.rearrange("p (k c) -> p k c", k=16),
                    w2_v[l][:, t * 16:(t + 1) * 16, jp * 256:(jp + 1) * 256], [], [sl.t])
                return sl
            tiles = [(jp, t) for jp in range(8) for t in range(4)]
            pend = {}
            for ti in range(min(3, len(tiles))):
                pend[ti] = load_w2(*tiles[ti])
            for ti, (jp, t) in enumerate(tiles):
                sl = pend.pop(ti)
                if ti + 3 < len(tiles):
                    pend[ti + 3] = load_w2(*tiles[ti + 3])
                w3p = sl.ap.rearrange("p (k c) -> p k c", k=16)
                pbs = [ps[(jp % 2) * 2 + cc] for cc in range(2)]
                for kk in range(16):
                    kg = t * 16 + kk
                    for cc in range(2):
                        MM(pbs[cc].ap[:, :N], w3p[:, kk, cc * 128:(cc + 1) * 128], aT3[:, kg, :N],
                           kg == 0, kg == 63, [sl.t, aT_tile(kg)], [pbs[cc].t])
                if t == 3:
                    for cc in range(2):
                        j = 2 * jp + cc
                        pb = pbs[cc]
                        x_ = xc[j % 2]
                        DMA(qS, x_.ap[:, :N], xres_v[:, j, s0:s0 + N], [xres_t[g][j]], [x_.t])
                        o_ = ost[j % 2]
                        STT(o_.ap[:, :N], pb.ap[:, :N], mod3[:, 80 + j, s:s + 1], x_.ap[:, :N], ALU.mult, ALU.add,
                            [pb.t, mod.t, x_.t], [o_.t])
                        if last:
                            DMA(qS, outT_v[:, j, s0:s0 + N], o_.ap[:, :N], [o_.t], [out_t[g][j]])
                        else:
                            DMA(qS, xres_v[:, j, s0:s0 + N], o_.ap[:, :N], [o_.t], [xres_t[g][j]])

        def program():
            if stage > 0:
                adaln_all()
            for l in range(depth):
                last = (l == depth - 1)
                if stage == 0:
                    return
                adaln(l)
                if stage == 1:
                    return
                DMA(qS, cos_ap, ropec_d.ap(), [], [X32_t])
                DMA(qS, sin_ap, ropes_d.ap(), [], [X40_t])
                wvs = w2slot()
                wv3 = wvs.ap.rearrange("p (k c) -> p k c", k=16)
                DMA(qG, wv3, w_in_v[l][:, :, 1280:1536], [], [wvs.t])
                for g in range(5):
                    phaseA(l, g, wv3, wvs.t)
                    if stage == 2:
                        return
                if stage == 3:
                    return
                yfix4 = exchange(l)
                if stage == 4:
                    return
                for g in range(5):
                    if last and GROUPS[g][2]:
                        continue
                    phaseB(l, g, yfix4, last)
                    if stage == 5:
                        return
        program()
        for g in range(4):
            for j in range(16):
                if out_t[g][j].w is not None:
                    K.wait(qS, out_t[g][j].w)
        for q in (qP, qA, qV, qG):
            if q.sem.count > 0:
                K.wait(qS, (q.sem, q.sem.count))
        for s_ in qS.dsems + qG.dsems:
            if s_.count > 0:
                K.wait(qS, (s_, s_.count))

        block = es.enter_context(nc.Block())

        @block.tensor
        def _(e):
            for f in qP.ops:
                f(e)

        @block.scalar
        def _(e):
            for f in qA.ops:
                f(e)

        @block.vector
        def _(e):
            for f in qV.ops:
                f(e)

        @block.gpsimd
        def _(e):
            for f in qG.ops:
                f(e)

        @block.sync
        def _(e):
            for f in qS.ops:
                f(e)
    counts = {q.name: len(q.ops) for q in (qP, qA, qV, qG, qS)}
    print("instruction counts", counts, flush=True)
    return nc


def rope_tables(h):
    t = np.arange(h * NLAT, (h + 1) * NLAT)
    row = (t // 64).astype(np.float32)
    col = (t % 64).astype(np.float32)
    inv = np.float32(10000.0) ** (-(np.arange(0, 64, 2, dtype=np.float32)) / np.float32(64))
    ang_r = row[:, None] * inv[None, :]
    ang_c = col[:, None] * inv[None, :]
    ang_r = np.concatenate([ang_r, ang_r], -1)
    ang_c = np.concatenate([ang_c, ang_c], -1)
    ang = np.concatenate([ang_r, ang_c], -1).astype(np.float32)
    cos = np.cos(ang).astype(np.float32).T
    sin = np.sin(ang).astype(np.float32).T
    sgn = np.ones((128, 1), np.float32)
    sgn[0:32] = -1.0
    sgn[64:96] = -1.0
    ss = sin * sgn
    sh = np.empty_like(ss)
    sh[32:64] = ss[0:32]
    sh[0:32] = ss[32:64]
    sh[96:128] = ss[64:96]
    sh[64:96] = ss[96:128]
    return np.ascontiguousarray(cos), np.ascontiguousarray(sh)


def prep(inputs, depth):
    f = lambda a: np.asarray(a, dtype=np.float32)
    x, c, ctx, c_ctx = f(inputs["x"]), f(inputs["c"]), f(inputs["ctx"]), f(inputs["c_ctx"])
    vec_layers = []
    for l in range(depth):
        cols = [f(inputs["b_ada"])[l].reshape(96, 128).T,
                f(inputs["norm1_g"])[l].reshape(16, 128).T,
                f(inputs["norm2_g"])[l].reshape(16, 128).T,
                f(inputs["q_norm_g"])[l].reshape(1, 128).T,
                f(inputs["k_norm_g"])[l].reshape(1, 128).T]
        cw = f(inputs["conv_w"])[l]
        for tap in range(3):
            cols.append(cw[tap].reshape(8, 128).T)
        cols.append(f(inputs["conv_b"])[l].reshape(8, 128).T)
        cols.append(f(inputs["attn_out_g"])[l].reshape(8, 128).T)
        cols.append(f(inputs["conv_out_g"])[l].reshape(8, 128).T)
        v = np.concatenate(cols, axis=1)
        assert v.shape == (128, NVEC)
        vec_layers.append(v)
    vecs = np.ascontiguousarray(np.concatenate(vec_layers, axis=1))
    shared = {
        "vecs": vecs,
        "w_in": np.ascontiguousarray(f(inputs["w_in"])[:depth]),
        "w_out": np.ascontiguousarray(f(inputs["w_out"])[:depth]),
        "w1": np.ascontiguousarray(f(inputs["w_mlp_in"])[:depth]),
        "w2": np.ascontiguousarray(f(inputs["w_mlp_out"])[:depth]),
    }
    tabs = [rope_tables(0), rope_tables(1)]
    w_ada_full = f(inputs["w_ada"])
    in_maps = []
    for r in range(8):
        b, h = r // 2, r % 2
        cc = np.stack([c[b], c_ctx], -1).reshape(16, 128, 2).transpose(1, 0, 2).reshape(128, 32)
        hmask = np.zeros((128, 2), np.float32)
        hmask[:, 0] = 1.0 if h == 1 else 0.0
        hmask[:, 1] = 1.0 if h == 0 else 0.0
        m = dict(shared)
        m["xT"] = np.ascontiguousarray(x[b, h * NLAT:(h + 1) * NLAT, :].T)
        m["cxT"] = np.ascontiguousarray(ctx[b].T)
        m["cc"] = np.ascontiguousarray(cc)
        m["w_ada"] = np.ascontiguousarray(w_ada_full[:depth, :, h * 6144:(h + 1) * 6144])
        m["ropec"] = tabs[h][0]
        m["ropes"] = tabs[h][1]
        m["hmask"] = hmask
        in_maps.append(m)
    return in_maps


_NC_CACHE = {}


def run(inputs, depth=DEPTH, stage=99):
    if depth not in _NC_CACHE:
        _NC_CACHE[depth] = build(depth, stage)
    nc = _NC_CACHE[depth]
    in_maps = prep(inputs, depth)
    res = run_bass_kernel_spmd(nc, in_maps, core_ids=list(range(8)))
    out = np.empty((4, 4096, D), np.float32)
    for r in range(8):
        b, h = r // 2, r % 2
        out[b, h * NLAT:(h + 1) * NLAT, :] = np.asarray(res.results[r]["outT"]).T
    return out


def kernel(**inputs):
    return run(inputs, DEPTH)
```

```python
import numpy as np
from contextlib import ExitStack
import concourse.bass as bass
import concourse.mybir as mybir
from concourse.bass_utils import run_bass_kernel_spmd

F32 = mybir.dt.float32
BF16 = mybir.dt.bfloat16
ALU = mybir.AluOpType
AF = mybir.ActivationFunctionType

D = 2048
KC = 16
NLAT = 2048
NCTX = 256
NTOK = NLAT + NCTX
DEPTH = 4
EPS = 1e-6
SCALE = 128.0 ** -0.5
NVEC = 178
V_BADA, V_N1G, V_N2G, V_QG, V_KG, V_CW, V_CB, V_AG, V_CG = 0, 96, 112, 128, 129, 130, 154, 162, 170
GROUPS = [(0, 512, False), (512, 512, False), (1024, 512, False), (1536, 512, False), (2048, 256, True)]
PAIRS = [[0, 1], [2, 3], [4, 5], [6, 7]]
NW2 = 4


class Tile:
    __slots__ = ("name", "w", "r")

    def __init__(self, name):
        self.name = name
        self.w = None
        self.r = {}


class Sem:
    __slots__ = ("h", "count")

    def __init__(self, h):
        self.h = h
        self.count = 0


class Queue:
    def __init__(self, name, kind):
        self.name = name
        self.kind = kind
        self.ops = []
        self.seen = {}
        self.sem = None
        self.dsems = []
        self.di = 0


class Tracker:
    def wait(self, q, tok):
        s, v = tok
        if q.seen.get(s, 0) >= v:
            return
        q.seen[s] = v
        q.ops.append(lambda e, s=s, v=v: e.wait_ge(s.h, v))

    def deps(self, q, reads, writes):
        toks = []
        for t in reads:
            if t.w is not None:
                toks.append(t.w)
        for t in writes:
            if t.w is not None:
                toks.append(t.w)
            toks.extend(t.r.items())
        for tok in toks:
            if q.kind == "pe" and tok[0] is q.sem:
                continue
            self.wait(q, tok)

    def finish(self, tok, reads, writes):
        s, v = tok
        for t in reads:
            if t.r.get(s, 0) < v:
                t.r[s] = v
        for t in writes:
            t.w = tok
            t.r = {}

    def op(self, q, f, reads=(), writes=(), mark=True):
        self.deps(q, reads, writes)
        s = q.sem
        if mark:
            s.count += 1
            v = s.count
            q.ops.append(lambda e, f=f, s=s: f(e).then_inc(s.h, 1))
        else:
            v = s.count + 1
            q.ops.append(lambda e, f=f: f(e))
        self.finish((s, v), reads, writes)

    def dma(self, q, f, reads=(), writes=()):
        self.deps(q, reads, writes)
        s = q.dsems[q.di]
        q.di = (q.di + 1) % len(q.dsems)
        if s.count > 0:
            self.wait(q, (s, s.count))
        s.count += 16
        v = s.count
        q.ops.append(lambda e, f=f, s=s: f(e).then_inc(s.h, 16))
        self.finish((s, v), reads, writes)


def build(depth=DEPTH, stage=99, fake_cc=False):
    nc = bass.Bass("TRN2", target_bir_lowering=False)
    K = Tracker()

    def din(name, shape, dt=F32):
        return nc.dram_tensor(name, shape, dt, kind="ExternalInput")

    xT = din("xT", [D, NLAT])
    cxT = din("cxT", [D, NCTX])
    cc_d = din("cc", [128, 32])
    ropec_d = din("ropec", [128, NLAT])
    ropes_d = din("ropes", [128, NLAT])
    hmask_d = din("hmask", [128, 2])
    vecs_d = din("vecs", [128, depth * NVEC])
    w_ada = din("w_ada", [depth, D, 6144])
    w_in = din("w_in", [depth, D, 4608])
    w_out = din("w_out", [depth, D, D])
    w1 = din("w1", [depth, D, 4 * D])
    w2 = din("w2", [depth, 4 * D, D])
    outT = nc.dram_tensor("outT", [D, NLAT], F32, kind="ExternalOutput")

    xres = nc.dram_tensor("xres", [D, NTOK], F32)
    q_d = nc.dram_tensor("q_d", [1024, NTOK], BF16)
    yg_d = nc.dram_tensor("yg_d", [1024, NTOK], BF16)
    kctx_d = nc.dram_tensor("kctx_d", [256, NCTX], BF16)
    vctx_d = nc.dram_tensor("vctx_d", [NCTX, 256], BF16)
    kT_in = [nc.dram_tensor(f"kT_in{l}", [256, NLAT], BF16) for l in range(depth)]
    v_in = [nc.dram_tensor(f"v_in{l}", [NLAT, 256], BF16) for l in range(depth)]
    halo_in = [nc.dram_tensor(f"halo_in{l}", [128, 16], F32) for l in range(depth)]
    kT_all = [nc.dram_tensor(f"kT_all{l}", [512, NLAT], BF16) for l in range(depth)]
    v_all = [nc.dram_tensor(f"v_all{l}", [2 * NLAT, 256], BF16) for l in range(depth)]
    halo_all = [nc.dram_tensor(f"halo_all{l}", [256, 16], F32) for l in range(depth)]
    ada_in = nc.dram_tensor("ada_in", [128, depth * 96], F32)
    ada_all = nc.dram_tensor("ada_all", [256, depth * 96], F32)

    xres_v = xres.ap().rearrange("(k p) t -> p k t", p=128)
    outT_v = outT.ap().rearrange("(k p) t -> p k t", p=128)
    xT_v = xT.ap().rearrange("(k p) t -> p k t", p=128)
    cxT_v = cxT.ap().rearrange("(k p) t -> p k t", p=128)
    q_d_v = q_d.ap().rearrange("(h p) t -> p h t", p=128)
    yg_d_v = yg_d.ap().rearrange("(h p) t -> p h t", p=128)
    vctx_v = vctx_d.ap().rearrange("(t p) c -> p t c", p=128)
    w_in_v = [w_in[l].rearrange("(k p) c -> p k c", p=128) for l in range(depth)]
    w_out_v = [w_out[l].rearrange("(k p) c -> p k c", p=128) for l in range(depth)]
    w1_v = [w1[l].rearrange("(k p) c -> p k c", p=128) for l in range(depth)]
    w2_v = [w2[l].rearrange("(k p) c -> p k c", p=128) for l in range(depth)]
    w_ada_v = [w_ada[l].rearrange("(k p) c -> p k c", p=128) for l in range(depth)]
    v_all_v = [v_all[l].ap().rearrange("(t p) c -> p t c", p=128) for l in range(depth)]

    with ExitStack() as es:
        def sb(name, shape, dt):
            return es.enter_context(nc.sbuf_tensor(name, shape, dt))

        def newsem(name):
            return Sem(es.enter_context(nc.semaphore(name)))

        qP = Queue("pe", "pe")
        qA = Queue("act", "act")
        qV = Queue("dve", "dve")
        qG = Queue("pool", "pool")
        qS = Queue("sp", "sp")
        for q in (qP, qA, qV, qG, qS):
            q.sem = newsem("s_" + q.name)
        qS.dsems = [newsem(f"ds{i}") for i in range(16)]
        qG.dsems = [newsem(f"dg{i}") for i in range(8)]

        class Buf:
            def __init__(self, ap, t):
                self.ap = ap
                self.t = t

        def mk(name, shape, dt):
            t = sb(name, shape, dt)
            return Buf(t[:], Tile(name))

        ones_d = mk("ones_d", [128, 128], BF16)
        ones_h = mk("ones_h", [128, 128], BF16)
        ones_g = mk("ones_g", [128, 128], BF16)
        ones_1 = mk("ones_1", [128, 128], BF16)
        eps_t = mk("eps_t", [128, 1], F32)
        vecs = mk("vecs_sb", [128, depth * NVEC], F32)
        ccs = mk("ccs", [128, 32], F32)
        adap = mk("adap", [128, depth * 96], F32)
        modall = mk("modall", [128, 2 * depth * 96], F32)
        mod = mk("mod", [128, 192], F32)
        G1 = mk("G1", [128, 32], F32)
        G2 = mk("G2", [128, 32], F32)
        rga = mk("rga", [128, 8], F32)
        rgc = mk("rgc", [128, 8], F32)
        hm = mk("hm", [128, 2], F32)
        zf = mk("zf", [128, 32], F32)
        zl = mk("zl", [128, 32], F32)
        bfb = mk("bfb", [128, 32], F32)
        blb = mk("blb", [128, 32], F32)
        yfb = mk("yfb", [128, 32], F32)
        ylb = mk("ylb", [128, 32], F32)
        yfix = mk("yfix", [128, 64], BF16)
        hl = mk("hl", [128, 8], F32)
        hr = mk("hr", [128, 8], F32)
        t8 = [mk(f"t8_{i}", [128, 8], F32) for i in range(2)]
        wbuf = [mk(f"wbuf{i}", [128, 16 * 256], BF16) for i in range(3)]
        w2buf = [mk(f"w2buf{i}", [128, 32 * 128], BF16) for i in range(NW2)]
        pT = [mk(f"pT{i}", [128, 512], BF16) for i in range(3)]
        sqb = [mk(f"sqb{i}", [128, 512], BF16) for i in range(2)]
        st16 = [mk(f"st16_{i}", [128, 512], BF16) for i in range(3)]
        vst = [mk(f"vst{i}", [128, 256], BF16) for i in range(2)]
        rstd = [mk(f"rstd{i}", [128, 512], F32) for i in range(3)]
        tmpm = [mk(f"tmpm{i}", [128, 512], F32) for i in range(2)]
        qn = [mk(f"qn{i}", [128, 512], F32) for i in range(2)]
        t1 = [mk(f"t1_{i}", [128, 512], F32) for i in range(2)]
        t2 = [mk(f"t2_{i}", [128, 512], F32) for i in range(2)]
        cs = [mk(f"cs{i}", [128, 512], F32) for i in range(2)]
        bs = [mk(f"bs{i}", [128, 512], F32) for i in range(2)]
        zb = [mk(f"zb{i}", [128, 516], F32) for i in range(2)]
        cv = [mk(f"cv{i}", [128, 512], F32) for i in range(1)]
        yv = [mk(f"yv{i}", [128, 512], F32) for i in range(1)]
        rec, ta, tb, rl, xc, ost = qn, t1, t2, cs, bs, zb

        arenaX = sb("arenaX", [128, 32 * 1024], BF16)
        arenaY = sb("arenaY", [128, 12800], BF16)
        xg3 = arenaX[:, 0:16384].bitcast(F32).rearrange("p (k n) -> p k n", k=16)
        xg_t = [Tile(f"xg{k}") for k in range(16)]
        X32_t = Tile("X32")
        X40_t = Tile("X40")
        X48_t = Tile("X48")
        cos_ap = arenaX[:, 16384:20480].bitcast(F32)
        sin_ap = arenaX[:, 20480:24576].bitcast(F32)
        og3 = arenaX[:, 16384:20480].rearrange("p (h n) -> p h n", h=8)
        yg3 = arenaX[:, 20480:24576].rearrange("p (h n) -> p h n", h=8)
        hT3 = arenaX[:, 24576:32768].rearrange("p (k n) -> p k n", k=16)
        aT3 = arenaX[:, 0:32768].rearrange("p (j n) -> p j n", j=64)

        def aT_tile(j):
            if j < 32:
                return xg_t[j // 2]
            if j < 40:
                return X32_t
            if j < 48:
                return X40_t
            return X48_t

        yk_t = Tile("yk")
        kh = arenaY[:, 0:4352]
        vh3 = arenaY[:, 4352:8704].rearrange("p (t d) -> p t d", t=34)
        h23 = arenaY[:, 0:8192].rearrange("p (k n) -> p k n", k=16)
        qg3 = arenaY[:, 8704:12800].rearrange("p (h n) -> p h n", h=8)
        qg_t = Tile("qg")

        ps = []
        for i in range(8):
            t = es.enter_context(nc.psum_tensor(f"ps{i}", [128, 512], F32))
            ps.append(Buf(t[:], Tile(f"ps{i}")))

        xres_t = [[Tile(f"xres{g}_{j}") for j in range(16)] for g in range(5)]
        q_d_t = [[Tile(f"qd{g}_{h}") for h in range(8)] for g in range(5)]
        yg_d_t = [[Tile(f"ygd{g}_{i}") for i in range(8)] for g in range(5)]
        kctx_t = [Tile(f"kctx{h}") for h in range(2)]
        vctx_t = [Tile(f"vctx{t}") for t in range(2)]
        kT_in_t = [[[Tile(f"kTin{l}_{g}_{h}") for h in range(2)] for g in range(4)] for l in range(depth)]
        v_in_t = [[Tile(f"vin{l}_{t}") for t in range(16)] for l in range(depth)]
        halo_in_t = [Tile(f"haloin{l}") for l in range(depth)]
        kT_all_t = [Tile(f"kTall{l}") for l in range(depth)]
        v_all_t = [Tile(f"vall{l}") for l in range(depth)]
        halo_all_t = [Tile(f"haloall{l}") for l in range(depth)]
        ada_in_t = Tile("ada_in")
        ada_all_t = Tile("ada_all")
        out_t = [[Tile(f"out{g}_{j}") for j in range(16)] for g in range(4)]

        def ACT(out, in_, func, reads, writes, bias=None, scale=None):
            kw = {}
            if bias is not None:
                kw["bias"] = bias
            if scale is not None:
                kw["scale"] = scale
            K.op(qA, lambda e: e.activation(out=out, in_=in_, func=func, **kw), reads, writes)

        def TT(out, in0, in1, op, reads, writes):
            K.op(qV, lambda e: e.tensor_tensor(out=out, in0=in0, in1=in1, op=op), reads, writes)

        def TS(out, in0, s1, op0, reads, writes):
            K.op(qV, lambda e: e.tensor_scalar(out=out, in0=in0, scalar1=s1, scalar2=None, op0=op0), reads, writes)

        def STT(out, in0, scalar, in1, op0, op1, reads, writes):
            K.op(qV, lambda e: e.scalar_tensor_tensor(out=out, in0=in0, scalar=scalar, in1=in1, op0=op0, op1=op1),
                 reads, writes)

        def VCOPY(out, in_, reads, writes):
            K.op(qV, lambda e: e.tensor_copy(out=out, in_=in_), reads, writes)

        def RECIP(out, in_, reads, writes):
            K.op(qV, lambda e: e.reciprocal(out=out, in_=in_), reads, writes)

        def MEMSET(ap, val, writes):
            K.op(qV, lambda e: e.memset(ap, val), (), writes)

        def MM(out, lhsT, rhs, start, stop, reads, writes, mark=None):
            K.op(qP, lambda e: e.matmul(out, lhsT=lhsT, rhs=rhs, start=start, stop=stop), reads, writes,
                 mark=(stop if mark is None else mark))

        def DMA(q, out, in_, reads, writes):
            K.dma(q, lambda e: e.dma_start(out=out, in_=in_), reads, writes)

        ccount = [0]

        def ALLGATHER(in_ap, out_ap, reads, writes, in_h=None, out_h=None):
            if fake_cc:
                n0 = in_h.shape[0]
                DMA(qG, out_h[0:n0, :], in_h[:, :], reads, writes)
                return
            K.deps(qG, reads, writes)
            s = newsem(f"cc{ccount[0]}")
            ccount[0] += 1
            s.count = 1
            qG.ops.append(lambda e: e.collective_compute(
                "AllGather", ALU.bypass, replica_groups=PAIRS, ins=[in_ap], outs=[out_ap]).then_inc(s.h, 1))
            K.finish((s, 1), reads, writes)
            K.wait(qG, (s, 1))

        def vc(l, off, n):
            return vecs.ap[:, l * NVEC + off:l * NVEC + off + n]

        mod3 = mod.ap.rearrange("p (j s) -> p j s", s=2)
        G1v = G1.ap.rearrange("p (k s) -> p k s", s=2)
        G2v = G2.ap.rearrange("p (k s) -> p k s", s=2)

        rs_ctr = [0]

        def RSTD(psb, N):
            r = rstd[rs_ctr[0] % 3]
            rs_ctr[0] += 1
            ACT(r.ap[:, :N], psb.ap[:, :N], AF.Sqrt, [psb.t, eps_t.t], [r.t], bias=eps_t.ap[:, 0:1], scale=1.0)
            RECIP(r.ap[:, :N], r.ap[:, :N], [r.t], [r.t])
            return r

        wctr = [0]

        def load_w(src3):
            slot = wbuf[wctr[0] % 3]
            wctr[0] += 1
            DMA(qG, slot.ap.rearrange("p (k c) -> p k c", k=16), src3, [], [slot.t])
            return slot

        w2ctr = [0]

        def w2slot():
            s = w2buf[w2ctr[0] % NW2]
            w2ctr[0] += 1
            return s

        MEMSET(ones_d.ap, 1.0 / 2048.0, [ones_d.t])
        MEMSET(ones_h.ap, 1.0 / 128.0, [ones_h.t])
        MEMSET(ones_g.ap, 1.0 / 1024.0, [ones_g.t])
        MEMSET(ones_1.ap, 1.0, [ones_1.t])
        MEMSET(eps_t.ap, EPS, [eps_t.t])
        DMA(qS, vecs.ap, vecs_d.ap(), [], [vecs.t])
        DMA(qS, ccs.ap, cc_d.ap(), [], [ccs.t])
        DMA(qS, hm.ap, hmask_d.ap(), [], [hm.t])
        ACT(ccs.ap, ccs.ap, AF.Silu, [ccs.t], [ccs.t])
        for g, (s0, N, ctx) in enumerate(GROUPS):
            for j in range(16):
                src = cxT_v[:, j, :] if ctx else xT_v[:, j, s0:s0 + N]
                DMA(qS, xres_v[:, j, s0:s0 + N], src, [], [xres_t[g][j]])

        def adaln_all():
            pm = ps[7]
            for l in range(depth):
                for jj in range(48):
                    slot = w2slot()
                    wa = slot.ap[:, 0:4096].bitcast(F32).rearrange("p (k c) -> p k c", k=16)
                    DMA(qS, wa, w_ada_v[l][:, :, jj * 128:(jj + 1) * 128], [], [slot.t])
                    c0 = (l * 48 + jj) * 2
                    for k in range(16):
                        MM(pm.ap[:, c0:c0 + 2], wa[:, k, :], ccs.ap[:, 2 * k:2 * k + 2], k == 0, k == 15,
                           [slot.t, ccs.t], [pm.t])
            VCOPY(adap.ap, pm.ap[:, 0:depth * 96], [pm.t], [adap.t])
            DMA(qS, ada_in[:, :], adap.ap, [adap.t], [ada_in_t])
            ALLGATHER(ada_in.ap().opt(), ada_all.ap().opt(), [ada_in_t], [ada_all_t], ada_in, ada_all)
            DMA(qS, modall.ap.rearrange("p (r c) -> p r c", r=2), ada_all.ap().rearrange("(r p) c -> p r c", p=128),
                [ada_all_t], [modall.t])

        def adaln(l):
            m4 = modall.ap.rearrange("p (r c) -> p r c", r=2)[:, :, l * 96:(l + 1) * 96].rearrange(
                "p r (jj s) -> p r jj s", s=2)
            o4 = mod.ap.rearrange("p (r jj s) -> p r jj s", r=2, jj=48)
            bada = vc(l, V_BADA, 96).rearrange("p (r jj) -> p r jj", r=2)
            for s in range(2):
                TT(o4[:, :, :, s], m4[:, :, :, s], bada, ALU.add, [modall.t, vecs.t], [mod.t])
            for s in range(2):
                STT(G1v[:, :, s], mod3[:, 16:32, s], 1.0, vc(l, V_N1G, 16), ALU.add, ALU.mult, [mod.t, vecs.t], [G1.t])
                STT(G2v[:, :, s], mod3[:, 64:80, s], 1.0, vc(l, V_N2G, 16), ALU.add, ALU.mult, [mod.t, vecs.t], [G2.t])
            RECIP(rga.ap, vc(l, V_AG, 8), [vecs.t], [rga.t])
            RECIP(rgc.ap, vc(l, V_CG, 8), [vecs.t], [rgc.t])

        chunks = [("q", h, h * 128) for h in range(8)] + [("k", h, 1024 + h * 128) for h in range(2)]
        for i in range(8):
            chunks += [("b", i, 1536 + i * 128), ("c", i, 2560 + i * 128), ("u", i, 3584 + i * 128)]

        pairsA = [("q", (0, 1), 0), ("q", (2, 3), 256), ("q", (4, 5), 512), ("q", (6, 7), 768), ("k", (0, 1), 1024)]
        for ip in range(4):
            pairsA += [("b", (2 * ip, 2 * ip + 1), 1536 + ip * 256), ("c", (2 * ip, 2 * ip + 1), 2560 + ip * 256),
                       ("u", (2 * ip, 2 * ip + 1), 3584 + ip * 256)]
        mctr = [0]
        nctr = [0]
        sctr = [0]
        qctr = [0]

        def chunk_epilogue(l, g, kind, idx, pm, deferred, cur_b, cur_c):
            s0, N, ctx = GROUPS[g]
            if kind in ("q", "k"):
                sq = sqb[sctr[0] % 2]
                sctr[0] += 1
                ACT(sq.ap[:, :N], pm.ap[:, :N], AF.Square, [pm.t], [sq.t])

                def stage2(kind=kind, idx=idx, pm=pm, sq=sq):
                    nb = ps[6 + nctr[0] % 2]
                    nctr[0] += 1
                    MM(nb.ap[:, :N], ones_h.ap, sq.ap[:, :N], True, True, [sq.t, ones_h.t], [nb.t])
                    r = RSTD(nb, N)
                    qq = qn[qctr[0] % 2]
                    tt1 = t1[qctr[0] % 2]
                    tt2 = t2[qctr[0] % 2]
                    so = st16[qctr[0] % 3]
                    qctr[0] += 1
                    gq = vc(l, V_QG if kind == "q" else V_KG, 1)
                    STT(qq.ap[:, :N], pm.ap[:, :N], gq, r.ap[:, :N], ALU.mult, ALU.mult, [pm.t, vecs.t, r.t], [qq.t])
                    if ctx:
                        ACT(so.ap[:, :N], qq.ap[:, :N], AF.Copy, [qq.t], [so.t])
                    else:
                        TT(tt1.ap[:, :N], qq.ap[:, :N], cos_ap[:, s0:s0 + N], ALU.mult, [qq.t, X32_t], [tt1.t])
                        for (a, b) in ((0, 32), (32, 0), (64, 96), (96, 64)):
                            TT(tt2.ap[a:a + 32, :N], qq.ap[b:b + 32, :N], sin_ap[b:b + 32, s0:s0 + N], ALU.mult,
                               [qq.t, X40_t], [tt2.t])
                        TT(so.ap[:, :N], tt1.ap[:, :N], tt2.ap[:, :N], ALU.add, [tt1.t, tt2.t], [so.t])
                    if kind == "q":
                        DMA(qS, q_d[idx * 128:(idx + 1) * 128, s0:s0 + N], so.ap[:, :N], [so.t], [q_d_t[g][idx]])
                    elif ctx:
                        DMA(qS, kctx_d[idx * 128:(idx + 1) * 128, :], so.ap[:, :N], [so.t], [kctx_t[idx]])
                    else:
                        DMA(qS, kT_in[l][idx * 128:(idx + 1) * 128, s0:s0 + N], so.ap[:, :N], [so.t],
                            [kT_in_t[l][g][idx]])
                deferred.append(stage2)
            elif kind == "b":
                b_ = bs[idx % 2]
                ACT(b_.ap[:, :N], pm.ap[:, :N], AF.Copy, [pm.t], [b_.t])
                cur_b[idx] = b_
            elif kind == "c":
                c_ = cs[idx % 2]
                ACT(c_.ap[:, :N], pm.ap[:, :N], AF.Copy, [pm.t], [c_.t])
                cur_c[idx] = c_
            else:
                i = idx
                b_, c_ = cur_b[idx], cur_c[idx]
                z = zb[i % 2]
                MEMSET(z.ap[:, 0:1], 0.0, [z.t])
                MEMSET(z.ap[:, N + 1:N + 2], 0.0, [z.t])
                TT(z.ap[:, 1:N + 1], pm.ap[:, :N], c_.ap[:, :N], ALU.mult, [pm.t, c_.t], [z.t])
                c0 = cv[0]
                y0 = yv[0]
                TS(c0.ap[:, :N], z.ap[:, 0:N], vc(l, V_CW + i, 1), ALU.mult, [z.t, vecs.t], [c0.t])
                STT(c0.ap[:, :N], z.ap[:, 1:N + 1], vc(l, V_CW + 8 + i, 1), c0.ap[:, :N], ALU.mult, ALU.add,
                    [z.t, vecs.t, c0.t], [c0.t])
                STT(c0.ap[:, :N], z.ap[:, 2:N + 2], vc(l, V_CW + 16 + i, 1), c0.ap[:, :N], ALU.mult, ALU.add,
                    [z.t, vecs.t, c0.t], [c0.t])
                STT(y0.ap[:, :N], c0.ap[:, :N], vc(l, V_CB + i, 1), b_.ap[:, :N], ALU.add, ALU.mult,
                    [c0.t, vecs.t, b_.t], [y0.t])
                if not ctx:
                    c8 = g * 8 + i
                    ACT(zf.ap[:, c8:c8 + 1], z.ap[:, 1:2], AF.Copy, [z.t], [zf.t])
                    ACT(zl.ap[:, c8:c8 + 1], z.ap[:, N:N + 1], AF.Copy, [z.t], [zl.t])
                    ACT(bfb.ap[:, c8:c8 + 1], b_.ap[:, 0:1], AF.Copy, [b_.t], [bfb.t])
                    ACT(blb.ap[:, c8:c8 + 1], b_.ap[:, N - 1:N], AF.Copy, [b_.t], [blb.t])
                    ACT(yfb.ap[:, c8:c8 + 1], y0.ap[:, 0:1], AF.Copy, [y0.t], [yfb.t])
                    ACT(ylb.ap[:, c8:c8 + 1], y0.ap[:, N - 1:N], AF.Copy, [y0.t], [ylb.t])
                so = st16[qctr[0] % 3]
                qctr[0] += 1
                TS(so.ap[:, :N], y0.ap[:, :N], vc(l, V_CG + i, 1), ALU.mult, [y0.t, vecs.t], [so.t])
                DMA(qS, yg_d[i * 128:(i + 1) * 128, s0:s0 + N], so.ap[:, :N], [so.t], [yg_d_t[g][i]])

        def phaseA(l, g, wv3, wv_t):
            s0, N, ctx = GROUPS[g]
            s = 1 if ctx else 0
            for kq in range(4):
                DMA(qS, xg3[:, 4 * kq:4 * kq + 4, 0:N], xres_v[:, 4 * kq:4 * kq + 4, s0:s0 + N],
                    [xres_t[g][j] for j in range(4 * kq, 4 * kq + 4)], [xg_t[j] for j in range(4 * kq, 4 * kq + 4)])
            for k in range(16):
                sq = sqb[k % 2]
                ACT(sq.ap[:, :N], xg3[:, k, :N], AF.Square, [xg_t[k]], [sq.t])
                MM(ps[0].ap[:, :N], ones_d.ap, sq.ap[:, :N], k == 0, k == 15, [sq.t, ones_d.t], [ps[0].t], mark=True)
            rs = RSTD(ps[0], N)
            for k in range(16):
                tm = tmpm[k % 2]
                STT(tm.ap[:, :N], xg3[:, k, :N], G1v[:, k, s:s + 1], rs.ap[:, :N], ALU.mult, ALU.mult,
                    [xg_t[k], G1.t, rs.t], [tm.t])
                ACT(hT3[:, k, :N], tm.ap[:, :N], AF.Identity, [tm.t, mod.t], [X48_t], bias=mod3[:, k, s:s + 1], scale=1.0)
            for tt in range(N // 128):
                pb = ps[1 + tt % 2]
                for k in range(16):
                    MM(pb.ap[:, 0:256], hT3[:, k, tt * 128:(tt + 1) * 128], wv3[:, k, :], k == 0, k == 15,
                       [X48_t, wv_t], [pb.t])
                vs = vst[tt % 2]
                ACT(vs.ap, pb.ap[:, 0:256], AF.Copy, [pb.t], [vs.t])
                if ctx:
                    DMA(qS, vctx_d[tt * 128:(tt + 1) * 128, :], vs.ap, [vs.t], [vctx_t[tt]])
                else:
                    ti = s0 // 128 + tt
                    DMA(qS, v_in[l][ti * 128:(ti + 1) * 128, :], vs.ap, [vs.t], [v_in_t[l][ti]])

            npair = len(pairsA)
            slots = {}
            for pi in range(min(2, npair)):
                slots[pi] = load_w(w_in_v[l][:, :, pairsA[pi][2]:pairsA[pi][2] + 256])
            deferred = []
            cur_b = {}
            cur_c = {}
            for pi, (kind, idxs, colp) in enumerate(pairsA):
                slot = slots.pop(pi)
                if pi + 2 < npair:
                    c2 = pairsA[pi + 2][2]
                    slots[pi + 2] = load_w(w_in_v[l][:, :, c2:c2 + 256])
                w3p = slot.ap.rearrange("p (k c) -> p k c", k=16)
                for cc in range(2):
                    idx = idxs[cc]
                    pm = ps[3 + mctr[0] % 3]
                    mctr[0] += 1
                    for k in range(16):
                        MM(pm.ap[:, :N], w3p[:, k, cc * 128:(cc + 1) * 128], hT3[:, k, :N], k == 0, k == 15,
                           [slot.t, X48_t], [pm.t])
                    for f in deferred:
                        f()
                    deferred = []
                    chunk_epilogue(l, g, kind, idx, pm, deferred, cur_b, cur_c)
            for f in deferred:
                f()

        def exchange(l):
            DMA(qS, halo_in[l][:, 0:8], zf.ap[:, 0:8], [zf.t], [halo_in_t[l]])
            DMA(qS, halo_in[l][:, 8:16], zl.ap[:, 24:32], [zl.t], [halo_in_t[l]])
            ALLGATHER(kT_in[l].ap().opt(), kT_all[l].ap().opt(),
                      [t for gg in kT_in_t[l] for t in gg], [kT_all_t[l]], kT_in[l], kT_all[l])
            ALLGATHER(v_in[l].ap().opt(), v_all[l].ap().opt(), v_in_t[l], [v_all_t[l]], v_in[l], v_all[l])
            ALLGATHER(halo_in[l].ap().opt(), halo_all[l].ap().opt(), [halo_in_t[l]], [halo_all_t[l]], halo_in[l], halo_all[l])
            DMA(qS, hl.ap, halo_all[l][0:128, 8:16], [halo_all_t[l]], [hl.t])
            DMA(qS, hr.ap, halo_all[l][128:256, 0:8], [halo_all_t[l]], [hr.t])
            TS(hl.ap, hl.ap, hm.ap[:, 0:1], ALU.mult, [hl.t, hm.t], [hl.t])
            TS(hr.ap, hr.ap, hm.ap[:, 1:2], ALU.mult, [hr.t, hm.t], [hr.t])
            yfix4 = yfix.ap.rearrange("p (g e i) -> p g e i", g=4, e=2)
            for g in range(4):
                g8 = slice(g * 8, g * 8 + 8)
                left = hl.ap if g == 0 else zl.ap[:, (g - 1) * 8:g * 8]
                lt = hl.t if g == 0 else zl.t
                a = t8[0]
                TT(a.ap, bfb.ap[:, g8], vc(l, V_CW, 8), ALU.mult, [bfb.t, vecs.t], [a.t])
                TT(a.ap, a.ap, left, ALU.mult, [a.t, lt], [a.t])
                TT(a.ap, a.ap, yfb.ap[:, g8], ALU.add, [a.t, yfb.t], [a.t])
                TT(yfix4[:, g, 0, :], a.ap, vc(l, V_CG, 8), ALU.mult, [a.t, vecs.t], [yfix.t])
                right = hr.ap if g == 3 else zf.ap[:, (g + 1) * 8:(g + 2) * 8]
                rt = hr.t if g == 3 else zf.t
                b = t8[1]
                TT(b.ap, blb.ap[:, g8], vc(l, V_CW + 16, 8), ALU.mult, [blb.t, vecs.t], [b.t])
                TT(b.ap, b.ap, right, ALU.mult, [b.t, rt], [b.t])
                TT(b.ap, b.ap, ylb.ap[:, g8], ALU.add, [b.t, ylb.t], [b.t])
                TT(yfix4[:, g, 1, :], b.ap, vc(l, V_CG, 8), ALU.mult, [b.t, vecs.t], [yfix.t])
            return yfix4

        uctr = [0]
        tctr = [0]
        pctr = [0]

        def attention(l, g):
            s0, N, ctx = GROUPS[g]
            DMA(qS, qg3[:, :, :N], q_d_v[:, :, s0:s0 + N], q_d_t[g], [qg_t])
            kts = [32, 33] if ctx else list(range(34))
            n = len(kts)
            for hd in range(2):
                if not ctx:
                    DMA(qS, kh[:, 0:2048], kT_all[l][hd * 128:(hd + 1) * 128, :], [kT_all_t[l]], [yk_t])
                    DMA(qS, kh[:, 2048:4096], kT_all[l][256 + hd * 128:256 + (hd + 1) * 128, :], [kT_all_t[l]], [yk_t])
                    DMA(qS, vh3[:, 0:32, :], v_all_v[l][:, :, hd * 128:(hd + 1) * 128], [v_all_t[l]], [yk_t])
                DMA(qS, kh[:, 4096:4352], kctx_d[hd * 128:(hd + 1) * 128, :], [kctx_t[hd]], [yk_t])
                DMA(qS, vh3[:, 32:34, :], vctx_v[:, :, hd * 128:(hd + 1) * 128], vctx_t, [yk_t])
                units = []
                for r in range(4):
                    units.append((hd * 4 + r, ps[3 + uctr[0] % 2], ps[5 + uctr[0] % 2]))
                    uctr[0] += 1
                tl = [(u, i) for u in range(4) for i in range(n)]
                pts = {}

                def S(t):
                    u, i = tl[t]
                    hh = units[u][0]
                    kt = kts[i]
                    tb_ = ps[tctr[0] % 3]
                    tctr[0] += 1
                    MM(tb_.ap[:, :N], kh[:, kt * 128:(kt + 1) * 128], qg3[:, hh, :N], True, True,
                       [yk_t, qg_t], [tb_.t])
                    p = pT[pctr[0] % 3]
                    pctr[0] += 1
                    ACT(p.ap[:, :N], tb_.ap[:, :N], AF.Exp, [tb_.t], [p.t], scale=SCALE)
                    pts[t] = p

                S(0)
                if len(tl) > 1:
                    S(1)
                for t, (u, i) in enumerate(tl):
                    if t + 2 < len(tl):
                        S(t + 2)
                    head, ob, db = units[u]
                    p = pts.pop(t)
                    kt = kts[i]
                    MM(ob.ap[:, :N], vh3[:, kt, :], p.ap[:, :N], i == 0, i == n - 1, [yk_t, p.t], [ob.t])
                    MM(db.ap[:, :N], ones_1.ap, p.ap[:, :N], i == 0, i == n - 1, [ones_1.t, p.t], [db.t], mark=True)
                    if i == n - 1:
                        rc = rec[head % 2]
                        RECIP(rc.ap[:, :N], db.ap[:, :N], [db.t], [rc.t])
                        STT(og3[:, head, :N], ob.ap[:, :N], vc(l, V_AG + head, 1), rc.ap[:, :N], ALU.mult, ALU.mult,
                            [ob.t, vecs.t, rc.t], [X32_t])

        def phaseB(l, g, yfix4, last):
            s0, N, ctx = GROUPS[g]
            s = 1 if ctx else 0
            attention(l, g)
            DMA(qS, yg3[:, :, :N], yg_d_v[:, :, s0:s0 + N], yg_d_t[g], [X40_t])
            if not ctx:
                VCOPY(yg3[:, :, 0], yfix4[:, g, 0, :], [yfix.t], [X40_t])
                VCOPY(yg3[:, :, N - 1], yfix4[:, g, 1, :], [yfix.t], [X40_t])
            for kq in range(4):
                DMA(qS, xg3[:, 4 * kq:4 * kq + 4, 0:N], xres_v[:, 4 * kq:4 * kq + 4, s0:s0 + N],
                    [xres_t[g][j] for j in range(4 * kq, 4 * kq + 4)], [xg_t[j] for j in range(4 * kq, 4 * kq + 4)])
            for i in range(8):
                sq = sqb[i % 2]
                ACT(sq.ap[:, :N], og3[:, i, :N], AF.Square, [X32_t, rga.t], [sq.t], scale=rga.ap[:, i:i + 1])
                MM(ps[6].ap[:, :N], ones_g.ap, sq.ap[:, :N], i == 0, i == 7, [sq.t, ones_g.t], [ps[6].t], mark=True)
            for i in range(8):
                sq = sqb[i % 2]
                ACT(sq.ap[:, :N], yg3[:, i, :N], AF.Square, [X40_t, rgc.t], [sq.t], scale=rgc.ap[:, i:i + 1])
                MM(ps[7].ap[:, :N], ones_g.ap, sq.ap[:, :N], i == 0, i == 7, [sq.t, ones_g.t], [ps[7].t], mark=True)
            rs_a = RSTD(ps[6], N)
            rs_c = RSTD(ps[7], N)
            slots = {}
            for jp in range(2):
                slots[jp] = load_w(w_out_v[l][:, :, jp * 256:(jp + 1) * 256])
            dfr = []
            for j in range(16):
                jp, cc = j // 2, j % 2
                if cc == 0:
                    slot = slots.pop(jp)
                    if jp + 2 < 8:
                        slots[jp + 2] = load_w(w_out_v[l][:, :, (jp + 2) * 256:(jp + 3) * 256])
                    w3p = slot.ap.rearrange("p (k c) -> p k c", k=16)
                w3 = w3p[:, :, cc * 128:(cc + 1) * 128]
                pa = ps[0 + j % 2]
                pc = ps[2 + j % 2]
                for k in range(8):
                    MM(pa.ap[:, :N], w3[:, k, :], og3[:, k, :N], k == 0, k == 7, [slot.t, X32_t], [pa.t])
                for k in range(8):
                    MM(pc.ap[:, :N], w3[:, 8 + k, :], yg3[:, k, :N], k == 0, k == 7, [slot.t, X40_t], [pc.t])
                for f in dfr:
                    f()
                dfr = []
                a = ta[j % 2]
                b = tb[j % 2]
                TT(a.ap[:, :N], pa.ap[:, :N], rs_a.ap[:, :N], ALU.mult, [pa.t, rs_a.t], [a.t])
                TT(b.ap[:, :N], pc.ap[:, :N], rs_c.ap[:, :N], ALU.mult, [pc.t, rs_c.t], [b.t])
                TT(a.ap[:, :N], a.ap[:, :N], b.ap[:, :N], ALU.add, [a.t, b.t], [a.t])
                STT(xg3[:, j, :N], a.ap[:, :N], mod3[:, 32 + j, s:s + 1], xg3[:, j, :N], ALU.mult, ALU.add,
                    [a.t, mod.t, xg_t[j]], [xg_t[j]])
                sq = sqb[j % 2]
                ACT(sq.ap[:, :N], xg3[:, j, :N], AF.Square, [xg_t[j]], [sq.t])
                def st2(j=j, sq=sq):
                    MM(ps[4].ap[:, :N], ones_d.ap, sq.ap[:, :N], j == 0, j == 15, [sq.t, ones_d.t], [ps[4].t], mark=True)
                dfr.append(st2)
                DMA(qS, xres_v[:, j, s0:s0 + N], xg3[:, j, :N], [xg_t[j]], [xres_t[g][j]])
            for f in dfr:
                f()
            rs2 = RSTD(ps[4], N)
            for k in range(16):
                tm = tmpm[k % 2]
                STT(tm.ap[:, :N], xg3[:, k, :N], G2v[:, k, s:s + 1], rs2.ap[:, :N], ALU.mult, ALU.mult,
                    [xg_t[k], G2.t, rs2.t], [tm.t])
                ACT(h23[:, k, :N], tm.ap[:, :N], AF.Identity, [tm.t, mod.t], [yk_t], bias=mod3[:, 48 + k, s:s + 1],
                    scale=1.0)
            slots = {}
            for jp in range(2):
                slots[jp] = load_w(w1_v[l][:, :, jp * 256:(jp + 1) * 256])
            for j in range(64):
                jp, cc = j // 2, j % 2
                if cc == 0:
                    slot = slots.pop(jp)
                    if jp + 2 < 32:
                        slots[jp + 2] = load_w(w1_v[l][:, :, (jp + 2) * 256:(jp + 3) * 256])
                    w3p = slot.ap.rearrange("p (k c) -> p k c", k=16)
                w3 = w3p[:, :, cc * 128:(cc + 1) * 128]
                pb = ps[5 + j % 3]
                for k in range(16):
                    MM(pb.ap[:, :N], w3[:, k, :], h23[:, k, :N], k == 0, k == 15, [slot.t, yk_t], [pb.t])
                r_ = rl[j % 2]
                ACT(r_.ap[:, :N], pb.ap[:, :N], AF.Relu, [pb.t], [r_.t])
                TT(aT3[:, j, :N], r_.ap[:, :N], r_.ap[:, :N], ALU.mult, [r_.t], [aT_tile(j)])
            def load_w2(jp, t):
                sl = w2slot()
                DMA(qG, sl.ap.rearrange("p (k c) -> p k c", k=16),
                    w2_v[l][:, t * 16:(t + 1) * 16, jp * 256:(jp + 1) * 256], [], [sl.t])
                return sl
            tiles = [(jp, t) for jp in range(8) for t in range(4)]
            pend = {}
            for ti in range(min(3, len(tiles))):
                pend[ti] = load_w2(*tiles[ti])
            for ti, (jp, t) in enumerate(tiles):
                sl = pend.pop(ti)
                if ti + 3 < len(tiles):
                    pend[ti + 3] = load_w2(*tiles[ti + 3])
                w3p = sl.ap.rearrange("p (k c) -> p k c", k=16)
                pbs = [ps[(jp % 2) * 2 + cc] for cc in range(2)]
                for kk in range(16):
                    kg = t * 16 + kk
                    for cc in range(2):
                        MM(pbs[cc].ap[:, :N], w3p[:, kk, cc * 128:(cc + 1) * 128], aT3[:, kg, :N],
                           kg == 0, kg == 63, [sl.t, aT_tile(kg)], [pbs[cc].t])
                if t == 3:
                    for cc in range(2):
                        j = 2 * jp + cc
                        pb = pbs[cc]
                        x_ = xc[j % 2]
                        DMA(qS, x_.ap[:, :N], xres_v[:, j, s0:s0 + N], [xres_t[g][j]], [x_.t])
                        o_ = ost[j % 2]
                        STT(o_.ap[:, :N], pb.ap[:, :N], mod3[:, 80 + j, s:s + 1], x_.ap[:, :N], ALU.mult, ALU.add,
                            [pb.t, mod.t, x_.t], [o_.t])
                        if last:
                            DMA(qS, outT_v[:, j, s0:s0 + N], o_.ap[:, :N], [o_.t], [out_t[g][j]])
                        else:
                            DMA(qS, xres_v[:, j, s0:s0 + N], o_.ap[:, :N], [o_.t], [xres_t[g][j]])

        def program():
            if stage > 0:
                adaln_all()
            for l in range(depth):
                last = (l == depth - 1)
                if stage == 0:
                    return
                adaln(l)
                if stage == 1:
                    return
                DMA(qS, cos_ap, ropec_d.ap(), [], [X32_t])
                DMA(qS, sin_ap, ropes_d.ap(), [], [X40_t])
                wvs = w2slot()
                wv3 = wvs.ap.rearrange("p (k c) -> p k c", k=16)
                DMA(qG, wv3, w_in_v[l][:, :, 1280:1536], [], [wvs.t])
                for g in range(5):
                    phaseA(l, g, wv3, wvs.t)
                    if stage == 2:
                        return
                if stage == 3:
                    return
                yfix4 = exchange(l)
                if stage == 4:
                    return
                for g in range(5):
                    if last and GROUPS[g][2]:
                        continue
                    phaseB(l, g, yfix4, last)
                    if stage == 5:
                        return
        program()
        for g in range(4):
            for j in range(16):
                if out_t[g][j].w is not None:
                    K.wait(qS, out_t[g][j].w)
        for q in (qP, qA, qV, qG):
            if q.sem.count > 0:
                K.wait(qS, (q.sem, q.sem.count))
        for s_ in qS.dsems + qG.dsems:
            if s_.count > 0:
                K.wait(qS, (s_, s_.count))

        block = es.enter_context(nc.Block())

        @block.tensor
        def _(e):
            for f in qP.ops:
                f(e)

        @block.scalar
        def _(e):
            for f in qA.ops:
                f(e)

        @block.vector
        def _(e):
            for f in qV.ops:
                f(e)

        @block.gpsimd
        def _(e):
            for f in qG.ops:
                f(e)

        @block.sync
        def _(e):
            for f in qS.ops:
                f(e)
    counts = {q.name: len(q.ops) for q in (qP, qA, qV, qG, qS)}
    print("instruction counts", counts, flush=True)
    return nc


def rope_tables(h):
    t = np.arange(h * NLAT, (h + 1) * NLAT)
    row = (t // 64).astype(np.float32)
    col = (t % 64).astype(np.float32)
    inv = np.float32(10000.0) ** (-(np.arange(0, 64, 2, dtype=np.float32)) / np.float32(64))
    ang_r = row[:, None] * inv[None, :]
    ang_c = col[:, None] * inv[None, :]
    ang_r = np.concatenate([ang_r, ang_r], -1)
    ang_c = np.concatenate([ang_c, ang_c], -1)
    ang = np.concatenate([ang_r, ang_c], -1).astype(np.float32)
    cos = np.cos(ang).astype(np.float32).T
    sin = np.sin(ang).astype(np.float32).T
    sgn = np.ones((128, 1), np.float32)
    sgn[0:32] = -1.0
    sgn[64:96] = -1.0
    ss = sin * sgn
    sh = np.empty_like(ss)
    sh[32:64] = ss[0:32]
    sh[0:32] = ss[32:64]
    sh[96:128] = ss[64:96]
    sh[64:96] = ss[96:128]
    return np.ascontiguousarray(cos), np.ascontiguousarray(sh)


def prep(inputs, depth):
    f = lambda a: np.asarray(a, dtype=np.float32)
    x, c, ctx, c_ctx = f(inputs["x"]), f(inputs["c"]), f(inputs["ctx"]), f(inputs["c_ctx"])
    vec_layers = []
    for l in range(depth):
        cols = [f(inputs["b_ada"])[l].reshape(96, 128).T,
                f(inputs["norm1_g"])[l].reshape(16, 128).T,
                f(inputs["norm2_g"])[l].reshape(16, 128).T,
                f(inputs["q_norm_g"])[l].reshape(1, 128).T,
                f(inputs["k_norm_g"])[l].reshape(1, 128).T]
        cw = f(inputs["conv_w"])[l]
        for tap in range(3):
            cols.append(cw[tap].reshape(8, 128).T)
        cols.append(f(inputs["conv_b"])[l].reshape(8, 128).T)
        cols.append(f(inputs["attn_out_g"])[l].reshape(8, 128).T)
        cols.append(f(inputs["conv_out_g"])[l].reshape(8, 128).T)
        v = np.concatenate(cols, axis=1)
        assert v.shape == (128, NVEC)
        vec_layers.append(v)
    vecs = np.ascontiguousarray(np.concatenate(vec_layers, axis=1))
    shared = {
        "vecs": vecs,
        "w_in": np.ascontiguousarray(f(inputs["w_in"])[:depth]),
        "w_out": np.ascontiguousarray(f(inputs["w_out"])[:depth]),
        "w1": np.ascontiguousarray(f(inputs["w_mlp_in"])[:depth]),
        "w2": np.ascontiguousarray(f(inputs["w_mlp_out"])[:depth]),
    }
    tabs = [rope_tables(0), rope_tables(1)]
    w_ada_full = f(inputs["w_ada"])
    in_maps = []
    for r in range(8):
        b, h = r // 2, r % 2
        cc = np.stack([c[b], c_ctx], -1).reshape(16, 128, 2).transpose(1, 0, 2).reshape(128, 32)
        hmask = np.zeros((128, 2), np.float32)
        hmask[:, 0] = 1.0 if h == 1 else 0.0
        hmask[:, 1] = 1.0 if h == 0 else 0.0
        m = dict(shared)
        m["xT"] = np.ascontiguousarray(x[b, h * NLAT:(h + 1) * NLAT, :].T)
        m["cxT"] = np.ascontiguousarray(ctx[b].T)
        m["cc"] = np.ascontiguousarray(cc)
        m["w_ada"] = np.ascontiguousarray(w_ada_full[:depth, :, h * 6144:(h + 1) * 6144])
        m["ropec"] = tabs[h][0]
        m["ropes"] = tabs[h][1]
        m["hmask"] = hmask
        in_maps.append(m)
    return in_maps


_NC_CACHE = {}


def run(inputs, depth=DEPTH, stage=99):
    if depth not in _NC_CACHE:
        _NC_CACHE[depth] = build(depth, stage)
    nc = _NC_CACHE[depth]
    in_maps = prep(inputs, depth)
    res = run_bass_kernel_spmd(nc, in_maps, core_ids=list(range(8)))
    out = np.empty((4, 4096, D), np.float32)
    for r in range(8):
        b, h = r // 2, r % 2
        out[b, h * NLAT:(h + 1) * NLAT, :] = np.asarray(res.results[r]["outT"]).T
    return out


def kernel(**inputs):
    return run(inputs, DEPTH)
```
